# Optimizing a Trainium2 kernel written in Bass

```python
import math
import jax, jax.numpy as jnp
from jax import lax
import numpy as np

D_MODEL = 1024
BATCH = 16
SEQ = 4096
DEPTH = 2
DEC_BATCH = 2
DEC_SEQ = 8192
PAST_LEN = 128

A_HEADS = 8
A_DK = 128
A_DV = D_MODEL // A_HEADS
A_WK = A_HEADS * A_DK
A_WV = A_HEADS * A_DV
R_HEADS = 8
R_DK = 64
R_DV = 2 * R_DK
R_WK = R_HEADS * R_DK
R_WV = R_HEADS * R_DV
CHUNK = 64
ROPE_BASE = 10000.0
LN_EPS = 1e-5
LOG_FLOOR = 1e-30
NEG_BIG = -1e30
DEEPNORM_ALPHA = (2.0 * DEPTH) ** 0.25
DEEPNORM_BETA = (8.0 * DEPTH) ** -0.25
IN_SEGMENTS = [A_WK, A_WK, A_WK, A_WV, A_WV, R_WK, R_WK, R_WV, R_WV]
VALUE_SEGMENTS = (3, 7)
IN_WIDTH = sum(IN_SEGMENTS)
IN_OFFSETS = [int(v) for v in np.cumsum(IN_SEGMENTS)[:-1]]

kernel_name = 'hybrid_hgrn2_retention_encoder'

F32 = jnp.float32


def _flip(a):
    return jnp.flip(a, axis=1)


def _layer_norm(h):
    h = h.astype(F32)
    mu = jnp.mean(h, axis=-1, keepdims=True)
    var = jnp.mean(jnp.square(h - mu), axis=-1, keepdims=True)
    return (h - mu) * lax.rsqrt(var + LN_EPS)


def _rms_norm(o):
    return o * lax.rsqrt(jnp.mean(jnp.square(o), axis=-1, keepdims=True) + LN_EPS)


def _rope(T):
    pos = jnp.arange(T, dtype=F32)
    inv = ROPE_BASE ** (-jnp.arange(0, R_DK, 2, dtype=F32) / R_DK)
    ang = pos[:, None] * inv[None, :]
    return jnp.cos(ang)[:, None, :], jnp.sin(ang)[:, None, :]


def _apply_rope(x, cos, sin):
    x1, x2 = jnp.split(x, 2, axis=-1)
    return jnp.concatenate([x1 * cos - x2 * sin, x1 * sin + x2 * cos], axis=-1)


def _hgrn2_dir(q, k, logf, v):
    B_, T, H, dk = q.shape
    dv = v.shape[-1]
    N = T // CHUNK

    def chunks(a):
        return a.reshape(B_, N, CHUNK, H, a.shape[-1]).transpose(1, 0, 3, 2, 4)

    causal = jnp.tril(jnp.ones((CHUNK, CHUNK), dtype=bool))[:, :, None]

    def step(S, inp):
        qc, kc, fc, vc = inp
        b = jnp.cumsum(fc, axis=2)
        b_last = b[:, :, -1:, :]
        inter = jnp.einsum('bhtk,bhkv->bhtv', qc * jnp.exp(b), S)
        decay = jnp.exp(jnp.where(causal, b[:, :, :, None, :] - b[:, :, None, :, :], NEG_BIG))
        attn = jnp.einsum('bhtk,bhtsk,bhsk->bhts', qc, decay, kc)
        intra = jnp.einsum('bhts,bhsv->bhtv', attn, vc)
        S_new = jnp.exp(b_last[:, :, 0, :, None]) * S + jnp.einsum('bhsk,bhsv->bhkv', kc * jnp.exp(b_last - b), vc)
        return S_new, inter + intra

    S0 = jnp.zeros((B_, H, dk, dv), F32)
    _, o = lax.scan(step, S0, (chunks(q), chunks(k), chunks(logf), chunks(v)))
    return o.transpose(1, 0, 3, 2, 4).reshape(B_, T, H, dv)


def _retention_dir(q, k, v, lg):
    B_, T, H, dk = q.shape
    dv = v.shape[-1]
    N = T // CHUNK
    q = q.reshape(B_, N, CHUNK, H, dk)
    k = k.reshape(B_, N, CHUNK, H, dk)
    v = v.reshape(B_, N, CHUNK, H, dv)
    pos = jnp.arange(CHUNK, dtype=F32)
    dist = pos[:, None] - pos[None, :]
    decay = jnp.exp(jnp.where(dist[None] >= 0, dist[None] * lg[:, None, None], NEG_BIG))
    scores = jnp.einsum('bnthd,bnshd->bnhts', q, k) * decay
    intra = jnp.einsum('bnhts,bnshe->bnthe', scores, v)
    k_dec = k * jnp.exp((CHUNK - 1 - pos)[:, None] * lg[None, :])[:, :, None]
    kv = jnp.einsum('bnshd,bnshe->nbhde', k_dec, v)
    chunk_decay = jnp.exp(CHUNK * lg)[None, :, None, None]

    def step(S, kv_n):
        return S * chunk_decay + kv_n, S

    _, S_prev = lax.scan(step, jnp.zeros((B_, H, dk, dv), F32), kv)
    q_dec = q * jnp.exp((pos + 1)[:, None] * lg[None, :])[:, :, None]
    inter = jnp.einsum('bnthd,nbhde->bnthe', q_dec, S_prev)
    return (intra + inter).reshape(B_, T, H, dv)


def _layer(x, c, cos, sin, lb, w_ada, b_ada, w_in, a_norm_w, ret_decay, w_pa, w_pb, w_mg, b_mg, w_out, ln_g, ln_b):
    dt = x.dtype
    B_, T, _ = x.shape
    ada = jax.nn.silu(c) @ w_ada + b_ada
    shift, scale, gate = jnp.split(ada, 3, axis=-1)
    u = x * (1 + scale[:, None, :]) + shift[:, None, :]
    proj = u @ w_in
    aq, af_f, af_b, ai, ag, rq, rk, rv, rg = jnp.split(proj, IN_OFFSETS, axis=-1)

    q_a = (jax.nn.silu(aq.astype(F32)) * A_DK ** -0.5).reshape(B_, T, A_HEADS, A_DK)
    v_a = ai.astype(F32).reshape(B_, T, A_HEADS, A_DV)
    lbh = lb.reshape(A_HEADS, A_DK)
    log_lb = jnp.log(jnp.maximum(lbh, LOG_FLOOR))
    log_1m_lb = jnp.log1p(-lbh)

    def forget(fl):
        fl = fl.astype(F32).reshape(B_, T, A_HEADS, A_DK)
        logf = jnp.logaddexp(log_lb, log_1m_lb + jax.nn.log_sigmoid(fl))
        kk = (1.0 - lbh) * jax.nn.sigmoid(-fl)
        return logf, kk

    logf_f, k_f = forget(af_f)
    logf_b, k_b = forget(af_b)
    o_a = _hgrn2_dir(q_a, k_f, logf_f, v_a) + _flip(_hgrn2_dir(_flip(q_a), _flip(k_b), _flip(logf_b), _flip(v_a)))
    y_a = (_rms_norm(o_a) * a_norm_w).reshape(B_, T, A_WV) * jax.nn.silu(ag.astype(F32))
    y_a = y_a.astype(dt)

    q_r = _apply_rope(rq.astype(F32).reshape(B_, T, R_HEADS, R_DK), cos, sin) * R_DK ** -0.5
    k_r = _apply_rope(rk.astype(F32).reshape(B_, T, R_HEADS, R_DK), cos, sin)
    v_r = rv.astype(F32).reshape(B_, T, R_HEADS, R_DV)
    lg = jax.nn.log_sigmoid(ret_decay.astype(F32))
    o_r = _retention_dir(q_r, k_r, v_r, lg[0]) + _flip(_retention_dir(_flip(q_r), _flip(k_r), _flip(v_r), lg[1]))
    o_r = _layer_norm(o_r)
    y_r = (o_r.reshape(B_, T, R_WV) * jax.nn.silu(rg.astype(F32))).astype(dt)

    p_a = y_a @ w_pa
    p_r = y_r @ w_pb
    g_a, g_r = jnp.split(jax.nn.sigmoid(u @ w_mg + b_mg), 2, axis=-1)
    s = (g_a * p_a + g_r * p_r) @ w_out
    h = DEEPNORM_ALPHA * x + (1 + gate[:, None, :]) * s
    return (_layer_norm(h) * ln_g + ln_b).astype(dt)


def _trunk(x, c, lbs, w_ada, b_ada, w_in, a_norm_w, ret_decay, w_pa, w_pb, w_mg, b_mg, w_out, ln_g, ln_b):
    cos, sin = _rope(x.shape[1])
    for l in range(DEPTH):
        x = _layer(x, c, cos, sin, lbs[l], w_ada[l], b_ada[l], w_in[l], a_norm_w[l], ret_decay[l],
                   w_pa[l], w_pb[l], w_mg[l], b_mg[l], w_out[l], ln_g[l], ln_b[l])
    return x


def setup_inputs(seed: int = 0) -> dict:
    key = jax.random.key(seed)
    ks = jax.random.split(key, 24)

    def nrm(k, shape, s):
        return jax.random.normal(k, shape, F32) * s

    D = D_MODEL
    seg_keys = jax.random.split(ks[6], len(IN_SEGMENTS))
    w_in = jnp.concatenate([
        nrm(seg_keys[j], (DEPTH, D, w), D ** -0.5 * (DEEPNORM_BETA if j in VALUE_SEGMENTS else 1.0))
        for j, w in enumerate(IN_SEGMENTS)], axis=-1)
    eps = 2.0 ** -(5.0 + jnp.arange(R_HEADS, dtype=F32))
    decay_logit = jnp.log1p(-eps) - jnp.log(eps)
    return {
        'x_prompt': nrm(ks[0], (BATCH, SEQ, D), 1.0),
        'x_sample': nrm(ks[1], (DEC_BATCH, DEC_SEQ, D), 1.0),
        'c_prompt': nrm(ks[2], (BATCH, D), 1.0),
        'c_sample': nrm(ks[3], (DEC_BATCH, D), 1.0),
        'w_ada': nrm(ks[4], (DEPTH, D, 3 * D), 0.2 * D ** -0.5),
        'b_ada': nrm(ks[5], (DEPTH, 3 * D), 0.01),
        'w_in': w_in,
        'hgrn_lb': nrm(ks[7], (DEPTH, A_WK), 0.5),
        'a_norm_w': 1.0 + nrm(ks[8], (DEPTH, A_DV), 0.02),
        'ret_decay': decay_logit[None, None, :] + nrm(ks[9], (DEPTH, 2, R_HEADS), 0.01),
        'w_pa': nrm(ks[10], (DEPTH, A_WV, D), DEEPNORM_BETA * A_WV ** -0.5),
        'w_pb': nrm(ks[11], (DEPTH, R_WV, D), DEEPNORM_BETA * R_WV ** -0.5),
        'w_mg': nrm(ks[12], (DEPTH, D, 2 * D), D ** -0.5),
        'b_mg': nrm(ks[13], (DEPTH, 2 * D), 0.01),
        'w_out': nrm(ks[14], (DEPTH, D, D), DEEPNORM_BETA * D ** -0.5),
        'ln_g': 1.0 + nrm(ks[15], (DEPTH, D), 0.02),
        'ln_b': nrm(ks[16], (DEPTH, D), 0.01),
    }


def reference(x_prompt, x_sample, c_prompt, c_sample, w_ada, b_ada, w_in, hgrn_lb, a_norm_w, ret_decay,
              w_pa, w_pb, w_mg, b_mg, w_out, ln_g, ln_b):
    p = jax.nn.softmax(hgrn_lb.astype(F32), axis=0)
    lbs = jnp.cumsum(p, axis=0) - p[0:1]
    y_prompt = _trunk(x_prompt, c_prompt, lbs, w_ada, b_ada, w_in, a_norm_w, ret_decay,
                      w_pa, w_pb, w_mg, b_mg, w_out, ln_g, ln_b)
    y_sample = _trunk(x_sample, c_sample, lbs, w_ada, b_ada, w_in, a_norm_w, ret_decay,
                      w_pa, w_pb, w_mg, b_mg, w_out, ln_g, ln_b)
    return (y_prompt, y_sample)
```

```python
import contextlib
import numpy as np
import concourse.bass as bass
import concourse.mybir as mybir
from concourse.bass_utils import run_bass_kernel_spmd

F32 = mybir.dt.float32
BF16 = mybir.dt.bfloat16
ALU = mybir.AluOpType
AF = mybir.ActivationFunctionType

D = 1024
TS = 512
CH = 64
NCH = TS // CH
WHC = 1152
O_FB, O_AI, O_RK, O_RKS, O_RV, O_AQ, O_FF, O_AG, O_RQ, O_RQS, O_RG = 0, 128, 256, 320, 384, 512, 640, 768, 896, 960, 1024
S1C = 512
IN_OFF = {"aq": 0, "ff": 1024, "fb": 2048, "ai": 3072, "ag": 4096, "rq": 5120, "rk": 5632, "rv": 6144, "rg": 7168}
EPS_A = 1e-5 * 128.0
EPS_R = 1e-5 * 64.0
ALPHA = 4.0 ** 0.25
BIG = 1e30


class Buf:
    __slots__ = ("name", "lw", "rd", "tw", "tr")

    def __init__(self, name):
        self.name = name
        self.lw = None
        self.rd = []
        self.tw = 0.0
        self.tr = 0.0


class Eng:
    def __init__(self, fw, name, handle, is_pe=False):
        self.name = name
        self.h = handle
        self.is_pe = is_pe
        self.sem = fw.new_sem("s_" + name)
        self.count = 0
        self.waited = {}


class DmaQ:
    def __init__(self, fw, name, handle, nslots):
        self.name = name
        self.h = handle
        self.sems = [fw.new_sem("d_%s%d" % (name, i)) for i in range(nslots)]
        self.uses = [0] * nslots
        self.idx = 0
        self.waited = {}
        self.is_pe = False


class FW:
    def __init__(self, nc, stack):
        self.nc = nc
        self.stack = stack
        self.bufs = {}
        self.pe = Eng(self, "pe", nc.tensor, is_pe=True)
        self.act = Eng(self, "act", nc.scalar)
        self.dve = Eng(self, "dve", nc.vector)
        self.pool = Eng(self, "pool", nc.gpsimd)
        self.q_sync = DmaQ(self, "sy", nc.sync, 32)
        self.q_pool = DmaQ(self, "gp", nc.gpsimd, 8)
        self.q_pool.waited = self.pool.waited
        self.n_inst = 0
        self.n_wait = 0
        self.alias = {}
        self.t_eng = {}
        self.last_finish = 0.0

    def _vt(self, eng, reads, writes, dur, occupy=True):
        st = self.t_eng.get(id(eng), 0.0)
        for b in reads:
            if b.tw > st:
                st = b.tw
        for b in writes:
            if b.tw > st:
                st = b.tw
            if b.tr > st:
                st = b.tr
        fin = st + dur
        if occupy:
            self.t_eng[id(eng)] = fin
        else:
            self.t_eng[id(eng)] = st + 0.1
        for b in reads:
            if fin > b.tr:
                b.tr = fin
        for b in writes:
            b.tw = fin + 1.2
        self.last_finish = fin

    def new_sem(self, name):
        return self.stack.enter_context(self.nc.semaphore(name))

    def _bl(self, xs):
        out = []
        for x in xs:
            for nm in self.alias.get(x, (x,)):
                b = self.bufs.get(nm)
                if b is None:
                    b = Buf(nm)
                    self.bufs[nm] = b
                out.append(b)
        return out

    def _wait(self, eng, sem, val):
        key = id(sem)
        if eng.waited.get(key, 0) >= val:
            return
        eng.h.wait_ge(sem, val)
        eng.waited[key] = val
        self.n_wait += 1

    def _deps(self, reads, writes):
        deps = {}
        for b in reads:
            if b.lw is not None:
                deps[(id(b.lw[0]), b.lw[1])] = b.lw
        for b in writes:
            if b.lw is not None:
                deps[(id(b.lw[0]), b.lw[1])] = b.lw
            for r in b.rd:
                deps[(id(r[0]), r[1])] = r
        return list(deps.values())

    def _mark(self, tok, reads, writes):
        for b in reads:
            b.rd.append(tok)
            if len(b.rd) > 12:
                last = {}
                for t in b.rd:
                    k = id(t[0])
                    if k not in last or last[k][1] < t[1]:
                        last[k] = t
                b.rd = list(last.values())
        for b in writes:
            b.lw = tok
            b.rd = []

    def op(self, eng, fn, r=(), w=(), cost=0.6):
        reads = self._bl(r)
        writes = self._bl(w)
        self._vt(eng, reads, writes, cost)
        for (sem, val, src) in self._deps(reads, writes):
            if src is eng and eng.is_pe:
                continue
            self._wait(eng, sem, val)
        inst = fn()
        eng.count += 1
        inst.then_inc(eng.sem, 1)
        self._mark((eng.sem, eng.count, eng), reads, writes)
        self.n_inst += 1
        return inst

    def dma(self, q, out, in_, r=(), w=(), **kw):
        reads = self._bl(r)
        writes = self._bl(w)
        self._vt(q, reads, writes, 4.0, occupy=False)
        for (sem, val, src) in self._deps(reads, writes):
            self._wait(q, sem, val)
        j = q.idx % len(q.sems)
        q.idx += 1
        sem = q.sems[j]
        if q.uses[j] > 0:
            self._wait(q, sem, 16 * q.uses[j])
        q.uses[j] += 1
        inst = q.h.dma_start(out=out, in_=in_, **kw)
        inst.then_inc(sem, 16)
        self._mark((sem, 16 * q.uses[j], q), reads, writes)
        self.n_inst += 1
        return inst

    def barrier(self):
        engs = [self.pe, self.act, self.dve, self.pool]
        qs = [self.q_sync, self.q_pool]
        for e in engs + [self.q_sync]:
            for o in engs:
                if o is not e and o.count > 0:
                    self._wait(e, o.sem, o.count)
            for q in qs:
                for j, sem in enumerate(q.sems):
                    if q.uses[j] > 0:
                        self._wait(e, sem, 16 * q.uses[j])

    def finish(self, names):
        for b in self._bl(names):
            if b.lw is not None:
                self._wait(self.q_sync, b.lw[0], b.lw[1])
            for t in b.rd:
                self._wait(self.q_sync, t[0], t[1])


class _Stop(Exception):
    pass


def build(NS, depth=2, upto=99):
    NT = NS * TS
    nc = bass.Bass("TRN2", target_bir_lowering=False)
    dt_in = lambda name, shape, dt=F32: nc.dram_tensor(name, shape, dt, kind="ExternalInput").ap()
    x_tok = dt_in("x_tok", [NT, D])
    xT = dt_in("xT", [D, NT])
    cT = dt_in("cT", [D, NS])
    chain = dt_in("chain", [1, NS + 1])
    ropeC = dt_in("ropeC", [NS, 64, TS])
    ropeS = dt_in("ropeS", [NS, 64, TS])
    w_ada = dt_in("w_ada", [2, D, 3 * D])
    b_ada = dt_in("b_ada", [2, 3 * D])
    w_in = dt_in("w_in", [2, D, 8192])
    hgrn_lb = dt_in("hgrn_lb", [2, D])
    a_norm_w = dt_in("a_norm_w", [2, 128])
    ret_decay = dt_in("ret_decay", [1, 32])
    w_pa = dt_in("w_pa", [2, D, D])
    w_pb = dt_in("w_pb", [2, D, D])
    w_mg = dt_in("w_mg", [2, D, 2 * D])
    b_mg = dt_in("b_mg", [2, 2 * D])
    w_out = dt_in("w_out", [2, D, D])
    ln_g = dt_in("ln_g", [2, D])
    ln_b = dt_in("ln_b", [2, D])
    cst = dt_in("cst", [128, 4 * 128 + 128 + 2 + 128])
    y_out = nc.dram_tensor("y", [NT, D], F32, kind="ExternalOutput").ap()

    uTd = [nc.dram_tensor("uTd%d" % l, [D, NT], BF16).ap() for l in range(2)]
    x1d = nc.dram_tensor("x1d", [NT, D], F32).ap()
    WHd = nc.dram_tensor("WHd", [2, 8, 128, 8, WHC], BF16).ap()
    TWd = nc.dram_tensor("TWd", [2, 8, 128, 8, 4, 128], BF16).ap()
    WOd = nc.dram_tensor("WOd", [2, 2, 128, 8, 512], BF16).ap()
    g1pd = nc.dram_tensor("g1pd", [2, NS, D], F32).ap()
    SBId = nc.dram_tensor("SBId", [NS, 8, 2, 128, 128], F32).ap()

    with contextlib.ExitStack() as st, contextlib.suppress(_Stop):
        fw = FW(nc, st)
        pe, act, dve, pool, qs, qp = fw.pe, fw.act, fw.dve, fw.pool, fw.q_sync, fw.q_pool
        sb = lambda name, shape, dt=F32: st.enter_context(nc.sbuf_tensor(name, shape, dt))
        psum = lambda name, dt=F32: st.enter_context(nc.psum_tensor(name, [128, 512], dt))
        PS = [psum("ps%d" % i) for i in range(7)]
        PST = psum("pst", BF16)

        def A(out, in_, func, r, w, bias=None, scale=1.0, accum_out=None):
            kw = {}
            if bias is not None:
                kw["bias"] = bias
            if accum_out is not None:
                kw["accum_out"] = accum_out
            return fw.op(act, lambda: nc.scalar.activation(out=out, in_=in_, func=func, scale=scale, **kw), r, w,
                         cost=0.22 + in_.free_size() / 1200.0)

        def TT(eng, out, in0, in1, op, r, w):
            h = nc.vector if eng is dve else nc.gpsimd
            return fw.op(eng, lambda: h.tensor_tensor(out=out, in0=in0, in1=in1, op=op), r, w,
                         cost=(0.1 + in0.free_size() / 960.0) if eng is dve else (0.2 + in0.free_size() / 450.0))

        def TSC(eng, out, in0, s1, s2, op0, op1, r, w):
            h = nc.vector if eng is dve else nc.gpsimd
            if s2 is None:
                return fw.op(eng, lambda: h.tensor_scalar(out=out, in0=in0, scalar1=s1, scalar2=None, op0=op0), r, w,
                             cost=0.1 + in0.free_size() / 960.0)
            return fw.op(eng, lambda: h.tensor_scalar(out=out, in0=in0, scalar1=s1, scalar2=s2, op0=op0, op1=op1), r, w,
                         cost=0.1 + in0.free_size() / 960.0)

        def STT(out, in0, scalar, in1, op0, op1, r, w):
            return fw.op(dve, lambda: nc.vector.scalar_tensor_tensor(out=out, in0=in0, scalar=scalar, in1=in1, op0=op0, op1=op1), r, w,
                         cost=0.1 + in0.free_size() / 960.0)

        def MM(out, lhsT, rhs, start, stop, r, w):
            return fw.op(pe, lambda: nc.tensor.matmul(out, lhsT=lhsT, rhs=rhs, start=start, stop=stop), r, w,
                         cost=(0.07 + rhs.free_size() / 1950.0) * (4.0 if rhs.dtype == F32 else 1.0))

        def TR(out, in_, ident, r, w):
            return fw.op(pe, lambda: nc.tensor.transpose(out, in_, ident), r, w, cost=0.12)

        cstt = sb("cstt", [128, 4 * 128 + 128 + 2 + 128])
        fw.dma(qs, cstt[:], cst, r=["cst"], w=["cstt"])
        MF = cstt[:, 0:128]
        MB = cstt[:, 128:256]
        DF = cstt[:, 256:384]
        DB = cstt[:, 384:512]
        TP1 = cstt[0:64, 512:576]
        TPB = cstt[0:64, 576:640]
        CF = cstt[:, 640:641]
        CB = cstt[:, 641:642]
        IDF = cstt[:, 642:770]
        idb = sb("idb", [128, 128], BF16)
        fw.op(dve, lambda: nc.vector.tensor_copy(out=idb[:], in_=IDF), ["cstt"], ["idb"])
        ones = sb("ones", [128, 128])
        fw.op(pool, lambda: nc.gpsimd.memset(ones[:], 1.0), [], ["ones"])
        smask = sb("smask", [128, TS])
        fw.op(pool, lambda: nc.gpsimd.memset(smask[:], 1.0), [], ["smask"])
        smv = smask[:].rearrange("p (c t) -> p c t", t=CH)
        fw.op(pool, lambda: nc.gpsimd.memset(smv[:, :, 0:1], 0.0), [], ["smask"])
        chn = sb("chn", [128, NS + 1])
        fw.dma(qs, chn[:], chain[0].partition_broadcast(128), r=["chain"], w=["chn"])

        with nc.allow_non_contiguous_dma(reason="weight relayout"):
            for l in range(2):
                WHv = WHd[l].rearrange("h p kc c -> h p kc c")
                segs = [("aq", O_AQ, 128), ("ff", O_FF, 128), ("fb", O_FB, 128), ("ai", O_AI, 128), ("ag", O_AG, 128),
                        ("rq", O_RQ, 64), ("rk", O_RK, 64), ("rv", O_RV, 128), ("rg", O_RG, 128)]
                for (nm, do, wd) in segs:
                    src = w_in[l][:, IN_OFF[nm]:IN_OFF[nm] + 8 * wd].rearrange("(kc p) (h w) -> h p kc w", p=128, w=wd)
                    for hh in range(8):
                        fw.dma(qp, WHd[l, hh][:, :, do:do + wd], src[hh], r=["w_in"], w=["WHd"])
                for (nm, do) in (("rq", O_RQS), ("rk", O_RKS)):
                    src = w_in[l][:, IN_OFF[nm]:IN_OFF[nm] + 512].rearrange("(kc p) (h w) -> h p kc w", p=128, w=64)
                    for hh in range(8):
                        fw.dma(qp, WHd[l, hh][:, :, do:do + 32], src[hh][:, :, 32:64], r=["w_in"], w=["WHd"])
                        fw.dma(qp, WHd[l, hh][:, :, do + 32:do + 64], src[hh][:, :, 0:32], r=["w_in"], w=["WHd"])
                for j, wsrc in enumerate((w_pa[l], w_pb[l], w_mg[l][:, 0:D], w_mg[l][:, D:2 * D])):
                    src = wsrc.rearrange("(kc p) (cb c) -> cb p kc c", p=128, c=128)
                    for cb in range(8):
                        fw.dma(qp, TWd[l, cb][:, :, j, :], src[cb], r=["w_t"], w=["TWd"])
                src = w_out[l].rearrange("(kc p) (hf c) -> hf p kc c", p=128, c=512)
                for hf in range(2):
                    fw.dma(qp, WOd[l, hf], src[hf], r=["w_t"], w=["WOd"])

        prohold = []

        def stage(n):
            if upto <= n:
                fw.barrier()
                for p_ in prohold:
                    p_.close()
                raise _Stop()
        stage(1)
        lbt = sb("lbt", [128, 2, 8])
        c1t = sb("c1t", [128, 2, 8])
        anw = sb("anw", [128, 2])
        hbm = sb("hbm", [128, 2, 16])
        with nc.allow_non_contiguous_dma(reason="tiny param relayout"):
            fw.dma(qs, lbt[:], hgrn_lb.rearrange("l (h k) -> k l h", k=128), r=["hgrn_lb"], w=["lbt"])
            fw.dma(qs, anw[:], a_norm_w.rearrange("l k -> k l"), r=["a_norm_w"], w=["anw"])
            fw.dma(qs, hbm[:], b_mg.rearrange("l (c p) -> p l c", p=128), r=["b_mg"], w=["hbm"])
        tl = sb("tl", [128, 8])
        TT(dve, tl[:], lbt[:, 0, :], lbt[:, 1, :], ALU.subtract, ["lbt"], ["tl"])
        A(tl[:], tl[:], AF.Exp, ["tl"], ["tl"])
        TSC(dve, tl[:], tl[:], 1.0, None, ALU.add, None, ["tl"], ["tl"])
        fw.op(dve, lambda: nc.vector.reciprocal(out=tl[:], in_=tl[:]), ["tl"], ["tl"])
        fw.op(pool, lambda: nc.gpsimd.memset(c1t[:, 0, :], 0.5), [], ["c1t"])
        TSC(dve, c1t[:, 1, :], tl[:], -0.5, 0.5, ALU.mult, ALU.add, ["tl", "c1t"], ["c1t"])
        TSC(dve, hbm[:], hbm[:], 0.5, None, ALU.mult, None, ["hbm"], ["hbm"])
        rdt = sb("rdt", [128, 32])
        fw.dma(qs, rdt[:], ret_decay[0].partition_broadcast(128), r=["ret_decay"], w=["rdt"])
        lg = sb("lg", [128, 32])
        A(lg[:], rdt[:], AF.Exp, ["rdt"], ["lg"], scale=-1.0)
        A(lg[:], lg[:], AF.Ln, ["lg"], ["lg"], bias=1.0)
        TSC(dve, lg[:], lg[:], -1.0, None, ALU.mult, None, ["lg"], ["lg"])
        g64 = sb("g64", [128, 32])
        A(g64[:], lg[:], AF.Exp, ["lg"], ["g64"], scale=64.0)
        DMt = sb("DMt", [128, 8, 128])
        PFt = sb("PFt", [64, 16, 64])
        pfc = sb("pfc", [128, 32])
        tm1 = sb("tm1", [128, 128])
        tm2 = sb("tm2", [128, 128])
        uT = sb("uT", [128, 8, TS], BF16)
        adaT = sb("adaT", [128, 2, 16, NS])
        pro = contextlib.ExitStack()
        prohold.append(pro)
        sbp = lambda name, shape, dt=F32: pro.enter_context(nc.sbuf_tensor(name, shape, dt))
        for l in range(2):
            for h in range(8):
                cf = l * 16 + h
                cbk = l * 16 + 8 + h
                A(pfc[:, cf:cf + 1], CF, AF.Exp, ["cstt", "lg"], ["pfc"], scale=lg[:, cf:cf + 1])
                A(pfc[:, cbk:cbk + 1], CB, AF.Exp, ["cstt", "lg"], ["pfc"], scale=lg[:, cbk:cbk + 1])

        stage(2)
        scT = sbp("scT", [128, 8, NS])
        fw.dma(qs, scT[:], cT.rearrange("(kc p) s -> p kc s", p=128), r=["cT"], w=["scT"])
        A(scT[:], scT[:], AF.Silu, ["scT"], ["scT"])
        badT = sbp("badT", [128, 2, 24])
        with nc.allow_non_contiguous_dma(reason="tiny param relayout"):
            fw.dma(qs, badT[:], b_ada.rearrange("l (c p) -> p l c", p=128), r=["b_ada"], w=["badT"])
        bgr = sbp("bgr", [NS, D])
        wad = sbp("wad", [128, 8, 512])
        grow = sbp("grow", [NS, 512])
        for l in range(2):
            fw.dma(qs, bgr[:], b_ada[l, 2 * D:3 * D].partition_broadcast(NS), r=["b_ada"], w=["bgr"])
            for blk in range(6):
                fw.dma(qs, wad[:], w_ada[l][:, blk * 512:(blk + 1) * 512].rearrange("(kc p) c -> p kc c", p=128),
                       r=["w_ada"], w=["wad"])
                if blk < 4:
                    for cc in range(4):
                        ch = blk * 4 + cc
                        for kc in range(8):
                            MM(PS[0][:, cc * NS:(cc + 1) * NS], wad[:, kc, cc * 128:(cc + 1) * 128], scT[:, kc, :],
                               kc == 0, kc == 7, ["wad", "scT"], ["ps0"])
                        if ch < 8:
                            TSC(dve, adaT[:, l, ch, :], PS[0][:, cc * NS:(cc + 1) * NS], badT[:, l, ch:ch + 1], None,
                                ALU.add, None, ["ps0", "badT"], ["adaT"])
                        else:
                            TSC(dve, adaT[:, l, ch, :], PS[0][:, cc * NS:(cc + 1) * NS], badT[:, l, ch:ch + 1], 1.0,
                                ALU.add, ALU.add, ["ps0", "badT"], ["adaT"])
                else:
                    for kc in range(8):
                        MM(PS[1][0:NS, :], scT[:, kc, :], wad[:, kc, :], kc == 0, kc == 7, ["wad", "scT"], ["ps1"])
                    c0 = (blk - 4) * 512
                    TT(dve, grow[:], PS[1][0:NS, :], bgr[:, c0:c0 + 512], ALU.add, ["ps1", "bgr"], ["grow"])
                    TSC(dve, grow[:], grow[:], 0.5, 0.5, ALU.mult, ALU.add, ["grow"], ["grow"])
                    fw.dma(qs, g1pd[l][:, c0:c0 + 512], grow[:], r=["grow"], w=["g1pd"])

        stage(3)
        xTt = sbp("xTt", [128, 8, TS])
        for s in range(NS):
            fw.dma(qs, xTt[:], xT[:, s * TS:(s + 1) * TS].rearrange("(kc p) t -> p kc t", p=128), r=["xT"], w=["xTt"])
            for kc in range(8):
                A(uT[:, kc, :], xTt[:, kc, :], AF.Identity, ["xTt", "adaT"], ["uT"],
                  scale=adaT[:, 0, 8 + kc, s:s + 1], bias=adaT[:, 0, kc, s:s + 1])
            fw.dma(qs, uTd[0][:, s * TS:(s + 1) * TS].rearrange("(kc p) t -> p kc t", p=128), uT[:], r=["uT"], w=["uTd0"])

        stage(4)
        fw.barrier()
        pro.close()
        prohold.clear()
        NREG = 32
        arena = sb("arena", [128, NREG * TS])
        WH = [sb("WH%d" % i, [128, 8, WHC], BF16) for i in range(2)]
        lfe = {d: sb("lfe" + d, [128, TS + 1]) for d in "fb"}
        for d in "fb":
            fw.op(pool, lambda: nc.gpsimd.memset(lfe[d][:, 0:1], 0.0), [], ["lfe" + d])
        regs = {}

        def view(name, r0, nreg=1, dt=F32, shape=None, half=0):
            ap = arena[:, r0 * TS:(r0 + nreg) * TS]
            if dt is BF16:
                ap = ap.bitcast(BF16)
                if nreg == 1 and shape is None:
                    ap = ap[:, half * TS:(half + 1) * TS]
            if shape is not None:
                ap = ap.rearrange(shape[0], **shape[1])
            regs[name] = ap
            fw.alias[name] = tuple("R%d" % r for r in range(r0, r0 + nreg))
            return ap

        T = {}
        for i, d in enumerate("fb"):
            b0 = i * 6
            T["aa" + d] = view("aa" + d, b0 + 0)
            T["kk" + d] = view("kk" + d, b0 + 1)
            T["cs" + d] = view("cs" + d, b0 + 2)
            T["ep" + d] = view("ep" + d, b0 + 3)
            T["em" + d] = view("em" + d, b0 + 4)
            T["kh" + d] = view("kh" + d, b0 + 5, dt=BF16)
        T["qs"] = view("qs", 12)
        T["x1"] = view("x1", 13)
        T["x2"] = view("x2", 14)
        T["o"] = view("o", 15)
        T["osq"] = view("osq", 16)
        T["r1"] = view("r1", 17)
        T["r2"] = view("r2", 18)
        T["m1"] = view("m1", 19)
        T["m2"] = view("m2", 20)
        T["Ac"] = view("Ac", 21, dt=BF16)
        s3 = ("p (c k) -> p c k", dict(k=128))
        snf = view("snf", 22, dt=BF16, shape=s3)
        snb = view("snb", 23, dt=BF16, shape=s3)
        srf = view("srf", 24, dt=BF16, shape=s3)
        srb = view("srb", 25, dt=BF16, shape=s3)
        WO = view("WO", 0, 8, dt=BF16, shape=("p (h kc c) -> p h kc c", dict(h=2, kc=8)))
        MT = view("MT", 8, 4, dt=BF16, shape=("p (kc t) -> p kc t", dict(kc=8)))
        g1r = view("g1r", 12, 2)
        hh_ = view("hh", 14, 2)
        zz = view("zz", 16, 2)
        T["t_o"] = view("t_o", 18)
        T["t_r1"] = view("t_r1", 19)
        T["t_r2"] = view("t_r2", 20)
        T["t_x1"] = view("t_x1", 21)
        T["t_x2"] = view("t_x2", 22)

        def par(name, shape, dt=BF16):
            return [sb("%s%d" % (name, i), shape, dt) for i in range(2)]
        Qf, Qb, Kf, Kb = par("Qf", [128, TS]), par("Qb", [128, TS]), par("Kf", [128, TS]), par("Kb", [128, TS])
        KHf, KHb = par("KHf", [128, 4, 128]), par("KHb", [128, 4, 128])
        vA, sgA = par("vA", [128, 4, 128]), par("sgA", [128, TS])
        qr, kr, Qrf, Qrb = par("qr", [64, TS]), par("kr", [64, TS]), par("Qrf", [64, TS]), par("Qrb", [64, TS])
        KRf, KRb = par("KRf", [128, 4, 64]), par("KRb", [128, 4, 64])
        vR, sgR = par("vR", [128, 4, 128]), par("sgR", [128, TS])
        scl = {n: par("c_" + n, [128, NCH], F32) for n in ("Df", "Ef", "Gf", "Db", "Eb", "Gb", "t1f", "t2f", "t1b", "t2b")}
        SfA = sb("SfA", [128, 8, 128])
        SbA = sb("SbA", [128, 8, 128])
        SfR = sb("SfR", [64, 8, 128])
        SbR = sb("SbR", [64, 8, 128])
        rC = sb("rC", [64, TS])
        rS = sb("rS", [64, TS])
        Y = sb("Y", [128, 16, TS], BF16)
        TWb = [sb("TW0", [128, 8, 4, 128], BF16)[:], view("TW1", 27, 4, dt=BF16, shape=("p (kc j c) -> p kc j c", dict(kc=8, j=4)))]
        lngr = view("lngr", 23, 2)
        lnbr = view("lnbr", 25, 2)
        st4 = sb("st4", [128, 8])
        u1t = sb("u1t", [128, 8, 128], BF16)

        def run_all(gens):
            gens = [g for g in gens if g is not None]
            while gens:
                for g in list(gens):
                    try:
                        next(g)
                    except StopIteration:
                        gens.remove(g)

        def sched(parts):
            sts = []
            for p_ in parts:
                if p_:
                    sts.append({"stages": [list(x) for x in p_], "cur": {}, "t0": 0.0})
            def refill(st):
                while not st["cur"] and st["stages"]:
                    for g in st["stages"].pop(0):
                        st["cur"][g] = st["t0"]
            for st in sts:
                refill(st)
            while True:
                best = None
                for st in sts:
                    for g, c in st["cur"].items():
                        if best is None or c < best[2]:
                            best = (st, g, c)
                if best is None:
                    break
                st, g, _ = best
                try:
                    next(g)
                    st["cur"][g] = fw.last_finish
                    if fw.last_finish > st["t0"]:
                        st["t0"] = fw.last_finish
                except StopIteration:
                    del st["cur"][g]
                    refill(st)

        def inter(*gens):
            gens = [g for g in gens if g is not None]
            while gens:
                for g in list(gens):
                    try:
                        next(g)
                        yield
                    except StopIteration:
                        gens.remove(g)

        def layer_consts(l):
            for h in range(8):
                cf = l * 16 + h
                cbk = l * 16 + 8 + h
                A(tm1[:], DF, AF.Exp, ["cstt", "lg"], ["tm1"], scale=lg[:, cf:cf + 1])
                TT(dve, tm1[:], tm1[:], MF, ALU.mult, ["tm1", "cstt"], ["tm1"])
                A(tm2[:], DB, AF.Exp, ["cstt", "lg"], ["tm2"], scale=lg[:, cbk:cbk + 1])
                TT(dve, tm2[:], tm2[:], MB, ALU.mult, ["tm2", "cstt"], ["tm2"])
                TT(dve, DMt[:, h, :], tm1[:], tm2[:], ALU.add, ["tm1", "tm2"], ["DMt"])
                A(PFt[:, h, :], TP1, AF.Exp, ["cstt", "lg"], ["PFt"], scale=lg[0:64, cf:cf + 1])
                A(PFt[:, 8 + h, :], TPB, AF.Exp, ["cstt", "lg"], ["PFt"], scale=lg[0:64, cbk:cbk + 1])

        def load_uT(l, s):
            fw.dma(qs, uT[:], uTd[l][:, s * TS:(s + 1) * TS].rearrange("(kc p) t -> p kc t", p=128), r=["uTd%d" % l], w=["uT"])

        def load_rope(s):
            fw.dma(qs, rC[:], ropeC[s], r=["ropeC"], w=["rC"])
            fw.dma(qs, rS[:], ropeS[s], r=["ropeS"], w=["rS"])

        def proj_fm(ps, psname, wp, col0, ncol):
            for kc in range(8):
                MM(ps[0:ncol, :], WH[wp][:, kc, col0:col0 + ncol], uT[:, kc, :], kc == 0, kc == 7, ["WH%d" % wp, "uT"], [psname])

        def proj_tm(ps, psname, wp, col0):
            for i in range(4):
                for kc in range(8):
                    MM(ps[:, i * 128:(i + 1) * 128], uT[:, kc, i * 128:(i + 1) * 128], WH[wp][:, kc, col0:col0 + 128],
                       kc == 0, kc == 7, ["WH%d" % wp, "uT"], [psname])

        pbank = [0]

        def nb():
            b = pbank[0] % 3
            pbank[0] += 1
            return PS[b], "ps%d" % b

        T["vT"] = view("vT", 26, dt=BF16)
        T["AcR"] = view("AcR", 26, dt=BF16, half=1)
        T["oR"] = view("oR", 27)
        T["osqR"] = view("osqR", 28)
        T["r1R"] = view("r1R", 29)
        T["r2R"] = view("r2R", 30)
        T["m1R"] = view("m1R", 31)
        fw.alias["m1R"] = ("R31a", "R31b", "R31c", "R31d")
        s4 = ("p (j k) -> p j k", dict(k=128))
        DS = []
        for ci, (ra, rb) in enumerate(((15, 16), (17, 18), (27, 28), (29, 30))):
            va = view("dS%da" % ci, ra, shape=s4)
            vb = view("dS%db" % ci, rb, shape=s4)
            xv = arena[:, 31 * TS + ci * 128:31 * TS + (ci + 1) * 128]
            fw.alias["X%d" % ci] = ("R31" + "abcd"[ci],)
            DS.append((va, "dS%da" % ci, vb, "dS%db" % ci, xv, "X%d" % ci))

        def proj_v(wp, col0, dst, dname):
            ps, pn = nb()
            proj_fm(ps, pn, wp, col0, 128)
            A(T["vT"], ps[:], AF.Copy, [pn], ["vT"])
            transpose_k(T["vT"], "vT", 128, [(dst, dname, None, act)])

        def transpose_k(src_bf, srcname, nrow, outs):
            for i in range(4):
                TR(PST[:, i * nrow:(i + 1) * nrow], src_bf[0:nrow, i * 128:(i + 1) * 128], idb[0:nrow, 0:nrow], [srcname, "idb"], ["pst"])
            pv = PST[:, 0:4 * nrow].rearrange("p (i k) -> p i k", k=nrow)
            prev = []
            for (dst, dname, sc, eng) in outs:
                if eng is act:
                    if sc is None:
                        A(dst[:], pv, AF.Copy, ["pst"] + prev, [dname])
                    else:
                        A(dst[:], pv, AF.Copy, ["pst", "pfc"] + prev, [dname], scale=sc)
                else:
                    fw.op(dve, lambda: nc.vector.tensor_scalar(out=dst[:], in0=pv, scalar1=sc, scalar2=None, op0=ALU.mult),
                          ["pst", "pfc"] + prev, [dname])
                prev = [dname]

        def g_gate(l, h, p, wp, d, full, ts=None, cofs=0, res=None, skip=False):
            bwd = (d == "b")
            ts = ts or d
            aa, kk, cs, ep, em, kh = T["aa" + ts], T["kk" + ts], T["cs" + ts], T["ep" + ts], T["em" + ts], T["kh" + ts]
            lf = lfe[ts]
            n = lambda x: x + ts
            if not skip:
                ps, pn = nb()
                proj_fm(ps, pn, wp, (O_FB if bwd else O_FF) + cofs, 128)
                A(aa, ps[:], AF.Tanh, [pn], [n("aa")], scale=-0.5)
                yield
            c1 = c1t[:, l, h:h + 1]
            TSC(dve, kk, aa, c1, c1, ALU.mult, ALU.add, [n("aa"), "c1t"], [n("kk")])
            yield
            A(lf[:, 1:TS + 1], kk, AF.Ln, [n("kk")], [n("lfe")], scale=-1.0, bias=1.0)
            yield
            csv = cs.rearrange("p (c t) -> p c t", t=CH)
            aav = aa.rearrange("p (c t) -> p c t", t=CH)
            Dn, En, Gn = scl["D" + d][p], scl["E" + d][p], scl["G" + d][p]
            t1, t2 = scl["t1" + d][p], scl["t2" + d][p]
            Dnm, Enm, Gnm = "D%s%d" % (d, p), "E%s%d" % (d, p), "G%s%d" % (d, p)
            if res is not None:
                Dn, Dnm = res["D"]
            if not bwd:
                fw.op(dve, lambda: nc.vector.tensor_tensor_scan(out=cs, data0=smask[:], data1=lf[:, 1:TS + 1], initial=0.0,
                                                                op0=ALU.mult, op1=ALU.add), ["smask", n("lfe")], [n("cs")])
                yield
                A(Dn[:], csv[:, :, CH - 1], AF.Exp, [n("cs")], [Dnm])
                if full:
                    A(En[:], csv[:, :, 31], AF.Exp, [n("cs")], [Enm])
                    TT(dve, t1[:], csv[:, :, CH - 1], csv[:, :, 31], ALU.subtract, [n("cs")], [n("ct1")])
                    yield
                    A(Gn[:], t1[:], AF.Exp, [n("ct1")], [Gnm])
                    TT(dve, aav, csv, csv[:, :, 31:32].to_broadcast([128, NCH, CH]), ALU.subtract, [n("cs")], [n("aa")])
            else:
                fw.op(dve, lambda: nc.vector.tensor_tensor_scan(out=cs, data0=lf[:, 0:TS], data1=smask[:], initial=0.0,
                                                                op0=ALU.add, op1=ALU.mult), ["smask", n("lfe")], [n("cs")])
                yield
                lfv = lf[:, 1:TS + 1].rearrange("p (c t) -> p c t", t=CH)
                TT(dve, t1[:], csv[:, :, CH - 1], lfv[:, :, CH - 1], ALU.add, [n("cs"), n("lfe")], [n("ct1")])
                A(Dn[:], t1[:], AF.Exp, [n("ct1")], [Dnm])
                if full:
                    A(Gn[:], csv[:, :, 32], AF.Exp, [n("cs")], [Gnm])
                    TT(dve, t2[:], t1[:], csv[:, :, 32], ALU.subtract, [n("ct1"), n("cs")], [n("ct2")])
                    yield
                    A(En[:], t2[:], AF.Exp, [n("ct2")], [Enm])
                    TT(dve, aav, csv[:, :, 32:33].to_broadcast([128, NCH, CH]), csv, ALU.subtract, [n("cs")], [n("aa")])
            yield
            KH = (KHb if bwd else KHf)[p]
            KHn = "KH%s%d" % (d, p)
            if res is not None:
                KH, KHn = res["KH"]
            if full:
                A(ep, aa, AF.Exp, [n("aa")], [n("ep")])
                yield
                A(em, aa, AF.Exp, [n("aa")], [n("em")], scale=-1.0)
                yield
                Qd, Kd = ((Qb, Kb) if bwd else (Qf, Kf))
                Qn, Kn = "Q%s%d" % (d, p), "K%s%d" % (d, p)
                TT(pool, Qd[p][:], T["qs"], ep, ALU.mult, ["qs", n("ep")], [Qn])
                TT(pool, Kd[p][:], kk, em, ALU.mult, [n("kk"), n("em")], [Kn])
                yield
                TT(dve, kh.rearrange("p (c t) -> p c t", t=CH), Kd[p][:].rearrange("p (c t) -> p c t", t=CH),
                   Gn[:].unsqueeze(2).to_broadcast([128, NCH, CH]), ALU.mult, [Kn, Gnm], [n("kh")])
            else:
                A(ep, cs, AF.Exp, [n("cs")], [n("ep")])
                yield
                TT(pool, kh, kk, ep, ALU.mult, [n("kk"), n("ep")], [n("kh")])
            yield
            transpose_k(kh, n("kh"), 128, [(KH, KHn, None, act)])
            yield

        def g_rope(dst, dname, wp, c_plain, c_swap):
            ps, pn = nb()
            proj_fm(ps, pn, wp, c_plain, 64)
            TT(dve, T["x1"][0:64, :], ps[0:64, :], rC[:], ALU.mult, [pn, "rC"], ["x1"])
            ps, pn = nb()
            proj_fm(ps, pn, wp, c_swap, 64)
            TT(dve, T["x2"][0:64, :], ps[0:64, :], rS[:], ALU.mult, [pn, "rS"], ["x2"])
            TT(pool, dst[:], T["x1"][0:64, :], T["x2"][0:64, :], ALU.add, ["x1", "x2"], [dname])
            yield

        def g_p0(l, h, p, wp, full, cofs=0, res=None, skip=False):
            if full and not skip:
                ps, pn = nb()
                proj_fm(ps, pn, wp, O_AQ, 128)
                A(T["qs"], ps[:], AF.Silu, [pn], ["qs"])
                yield
            vt, vn = res["v"] if res is not None else (vA[p], "vA%d" % p)
            proj_v(wp, O_AI + cofs, vt, vn)
            yield
            if full and not skip:
                ps, pn = nb()
                proj_fm(ps, pn, wp, O_AG, 128)
                A(sgA[p][:], ps[:], AF.Silu, [pn], ["sgA%d" % p])
                yield
            if full:
                yield from g_rope(qr[p], "qr%d" % p, wp, O_RQ, O_RQS)
            yield from g_rope(kr[p], "kr%d" % p, wp, O_RK + cofs, O_RKS + cofs)
            cf = l * 16 + h
            cbk = l * 16 + 8 + h
            if full:
                transpose_k(kr[p], "kr%d" % p, 64, [(KRf[p], "KRf%d" % p, pfc[:, cf:cf + 1], act), (KRb[p], "KRb%d" % p, pfc[:, cbk:cbk + 1], dve)])
                yield
                qv = qr[p][:].rearrange("p (c t) -> p c t", t=CH)
                TT(pool, Qrf[p][:].rearrange("p (c t) -> p c t", t=CH), qv, PFt[:, h, :].unsqueeze(1).to_broadcast([64, NCH, CH]),
                   ALU.mult, ["qr%d" % p, "PFt"], ["Qrf%d" % p])
                yield
                TT(pool, Qrb[p][:].rearrange("p (c t) -> p c t", t=CH), qv, PFt[:, 8 + h, :].unsqueeze(1).to_broadcast([64, NCH, CH]),
                   ALU.mult, ["qr%d" % p, "PFt"], ["Qrb%d" % p])
                yield
            else:
                krt, krn = res["KR"] if res is not None else (KRb[p], "KRb%d" % p)
                transpose_k(kr[p], "kr%d" % p, 64, [(krt, krn, pfc[:, cbk:cbk + 1], act)])
                yield
            vt, vn = res["vR"] if res is not None else (vR[p], "vR%d" % p)
            proj_v(wp, O_RV + cofs, vt, vn)
            yield
            if full and not skip:
                ps, pn = nb()
                proj_fm(ps, pn, wp, O_RG, 128)
                A(sgR[p][:], ps[:], AF.Silu, [pn], ["sgR%d" % p])
                yield

        def g_recur(S, Sname, order, KH, KHname, V, Vname, Dcol, Dname, snap, snapname, Ecol, Ename, nrow, flagcol, bank, pre=None, ci=0):
            if pre is not None:
                pre()
            if flagcol is not None:
                TSC(dve, S, S, flagcol, None, ALU.mult, None, [Sname, "chn"], [Sname])
            va, van, vb, vbn, X, Xn = DS[ci]
            groups = [(0, va, van, bank), (1, vb, vbn, bank + 1)]
            if order[0] % 2 == 1:
                groups = groups[::-1]
            for (par_, dv, dvn, bk) in groups:
                pn = "ps%d" % bk
                for j in range(4):
                    c = 2 * j + par_
                    hp = par_ * 64
                    MM(PS[bk][0:nrow, j * 128:(j + 1) * 128], KH[hp:hp + 64, j, :], V[hp:hp + 64, j, :], True, True,
                       [KHname, Vname], [pn])
                A(dv[0:nrow], PS[bk][0:nrow, :].rearrange("p (j k) -> p j k", k=128), AF.Copy, [pn], [dvn])
                yield
            src, srcn, dst, dstn = S, Sname, X[0:nrow, :], Xn
            for n_, c in enumerate(order):
                dv, dvn = (va, van) if c % 2 == 0 else (vb, vbn)
                if snap is not None:
                    if Ecol is not None:
                        A(snap[0:nrow, c, :], src, AF.Copy, [srcn, Ename], [snapname], scale=Ecol(c))
                    else:
                        A(snap[0:nrow, c, :], src, AF.Copy, [srcn], [snapname])
                STT(dst, src, Dcol(c), dv[0:nrow, c // 2, :], ALU.mult, ALU.add, [srcn, Dname, dvn], [dstn])
                src, srcn, dst, dstn = dst, dstn, src, srcn
                yield

        ASC = list(range(NCH))
        DESC = list(range(NCH - 1, -1, -1))

        def g_first2(l, h, p, wp):
            for (col, dst, dn, fn, sc) in ((O_FF, T["aaf"], "aaf", AF.Tanh, -0.5), (O_FB, T["aab"], "aab", AF.Tanh, -0.5),
                                           (O_AQ, T["qs"], "qs", AF.Silu, 1.0), (O_AG, sgA[p][:], "sgA%d" % p, AF.Silu, 1.0),
                                           (O_RG, sgR[p][:], "sgR%d" % p, AF.Silu, 1.0)):
                ps, pn = nb()
                proj_fm(ps, pn, wp, col, 128)
                A(dst, ps[:], fn, [pn], [dn], scale=sc)
            yield

        def g_first1(l, h, wp):
            for (cofs, ts) in ((0, "b"), (S1C, "f")):
                ps, pn = nb()
                proj_fm(ps, pn, wp, O_FB + cofs, 128)
                A(T["aa" + ts], ps[:], AF.Tanh, [pn], ["aa" + ts], scale=-0.5)
            yield

        def early2(l, s, h, p, wp):
            return [[g_first2(l, h, p, wp)],
                    [g_p0(l, h, p, wp, True, skip=True), g_gate(l, h, p, wp, "f", True, skip=True), g_gate(l, h, p, wp, "b", True, skip=True)]]

        v4 = lambda t: t[:].rearrange("p (i k) -> p i k", k=128)
        RES1 = []
        for q_ in range(4):
            j_ = q_ % 2
            if q_ < 2:
                RES1.append(dict(KH=(KHb[j_], "KHb%d" % j_), v=(vA[j_][:], "vA%d" % j_), KR=(KRb[j_], "KRb%d" % j_),
                                 vR=(vR[j_][:], "vR%d" % j_), D=(scl["Db"][j_], "Db%d" % j_)))
            else:
                RES1.append(dict(KH=(KHf[j_], "KHf%d" % j_), v=(v4(sgA[j_]), "sgA%d" % j_), KR=(KRf[j_], "KRf%d" % j_),
                                 vR=(v4(sgR[j_]), "sgR%d" % j_), D=(scl["Df"][j_], "Df%d" % j_)))

        def early1(l, s, h, wp, q0):
            return [[g_first1(l, h, wp)],
                    [g_p0(l, h, 0, wp, False, 0, RES1[q0]), g_gate(l, h, 0, wp, "b", False, "b", 0, RES1[q0], skip=True),
                     g_p0(l, h + 1, 1, wp, False, S1C, RES1[q0 + 1]), g_gate(l, h + 1, 1, wp, "b", False, "f", S1C, RES1[q0 + 1], skip=True)]]

        def late1(l, s, h, q0):
            return [late1u(l, s, h, RES1[q0], 0) + late1u(l, s, h + 1, RES1[q0 + 1], 1)]

        def late1u(l, s, h, res, cj):
            SA, SAn = SbA[:, h, :], "SbA%d" % h
            SR, SRn = SbR[:, h, :], "SbR%d" % h

            def preA():
                TSC(dve, SA, SA, chn[:, s + 1:s + 2], None, ALU.mult, None, [SAn, "chn"], [SAn])
                fw.dma(qs, SBId[s, h, 0], SA, r=[SAn], w=["SBId%d_%d" % (s, h)])

            def preR():
                TSC(dve, SR, SR, chn[0:64, s + 1:s + 2], None, ALU.mult, None, [SRn, "chn"], [SRn])
                fw.dma(qs, SBId[s, h, 1, 0:64], SR, r=[SRn], w=["SBId%d_%d" % (s, h)])
            gcol = g64[0:64, l * 16 + 8 + h:l * 16 + 8 + h + 1]
            Db, Dbn = res["D"]
            return [
                g_recur(SA, SAn, DESC, res["KH"][0], res["KH"][1], res["v"][0], res["v"][1], lambda c: Db[:, c:c + 1], Dbn,
                        None, None, None, None, 128, None, 3, pre=preA, ci=cj),
                g_recur(SR, SRn, DESC, res["KR"][0], res["KR"][1], res["vR"][0], res["vR"][1], lambda c: gcol, "g64",
                        None, None, None, None, 64, None, 5, pre=preR, ci=2 + cj)]

        def g_scA(l, h, p):
            v3 = lambda t: t.rearrange("p (i k) -> p i k", k=128)
            mfb = MF.unsqueeze(1).to_broadcast([128, 4, 128])
            mbb = MB.unsqueeze(1).to_broadcast([128, 4, 128])
            psa, pna = nb()
            for i in range(4):
                cs_ = slice(i * 128, (i + 1) * 128)
                MM(psa[:, cs_], Kf[p][:, cs_], Qf[p][:, cs_], True, True, ["Kf%d" % p, "Qf%d" % p], [pna])
            TSC(dve, T["m1"], psa[:], BIG, -BIG, ALU.min, ALU.max, [pna], ["m1"])
            TT(pool, v3(T["m1"]), v3(T["m1"]), mfb, ALU.mult, ["m1", "cstt"], ["m1"])
            yield
            psb, pnb = nb()
            for i in range(4):
                cs_ = slice(i * 128, (i + 1) * 128)
                MM(psb[:, cs_], Kb[p][:, cs_], Qb[p][:, cs_], True, True, ["Kb%d" % p, "Qb%d" % p], [pnb])
            TSC(dve, T["m2"], psb[:], BIG, -BIG, ALU.min, ALU.max, [pnb], ["m2"])
            TT(pool, v3(T["m2"]), v3(T["m2"]), mbb, ALU.mult, ["m2", "cstt"], ["m2"])
            yield
            TT(pool, T["Ac"], T["m1"], T["m2"], ALU.add, ["m1", "m2"], ["Ac"])
            yield

        def g_outA(l, h, p):
            for i in range(4):
                cs_ = slice(i * 128, (i + 1) * 128)
                MM(PS[6][:, cs_], vA[p][:, i, :], T["Ac"][:, cs_], True, False, ["vA%d" % p, "Ac"], ["ps6"])
                for cc in range(2):
                    c = 2 * i + cc
                    ct = slice(c * CH, (c + 1) * CH)
                    MM(PS[6][:, ct], snf[:, c, :], Qf[p][:, ct], False, False, ["snf", "Qf%d" % p], ["ps6"])
                    MM(PS[6][:, ct], snb[:, c, :], Qb[p][:, ct], False, cc == 1, ["snb", "Qb%d" % p], ["ps6"])
            yield
            A(T["osq"], PS[6][:], AF.Square, ["ps6"], ["osq"])
            A(T["o"], PS[6][:], AF.Copy, ["ps6", "osq"], ["o"])
            yield
            MM(PS[4][:], ones[:], T["osq"], True, True, ["ones", "osq"], ["ps4"])
            A(T["r1"], PS[4][:], AF.Ln, ["ps4"], ["r1"], scale=1.0 / 128.0, bias=EPS_A)
            yield
            A(T["r2"], T["r1"], AF.Exp, ["r1"], ["r2"], scale=-0.5)
            yield
            STT(T["o"], T["o"], anw[:, l:l + 1], T["r2"], ALU.mult, ALU.mult, ["o", "anw", "r2"], ["o"])
            yield
            TT(dve, Y[:, h, :], T["o"], sgA[p][:], ALU.mult, ["o", "sgA%d" % p], ["Y%d" % h])
            yield

        def g_scR(l, h, p):
            v3 = lambda t: t.rearrange("p (i k) -> p i k", k=128)
            psr, pnr = nb()
            for i in range(4):
                cs_ = slice(i * 128, (i + 1) * 128)
                MM(psr[:, cs_], kr[p][:, cs_], qr[p][:, cs_], True, True, ["kr%d" % p, "qr%d" % p], [pnr])
            TT(dve, v3(T["AcR"]), psr[:].rearrange("p (i k) -> p i k", k=128),
               DMt[:, h, :].unsqueeze(1).to_broadcast([128, 4, 128]), ALU.mult, [pnr, "DMt"], ["AcR"])
            yield

        def g_outR(l, h, p):
            for i in range(4):
                cs_ = slice(i * 128, (i + 1) * 128)
                MM(PS[5][:, cs_], vR[p][:, i, :], T["AcR"][:, cs_], True, False, ["vR%d" % p, "AcR"], ["ps5"])
                for cc in range(2):
                    c = 2 * i + cc
                    ct = slice(c * CH, (c + 1) * CH)
                    MM(PS[5][:, ct], srf[0:64, c, :], Qrf[p][:, ct], False, False, ["srf", "Qrf%d" % p], ["ps5"])
                    MM(PS[5][:, ct], srb[0:64, c, :], Qrb[p][:, ct], False, cc == 1, ["srb", "Qrb%d" % p], ["ps5"])
            yield
            A(T["osqR"], PS[5][:], AF.Square, ["ps5"], ["osqR"])
            A(T["oR"], PS[5][:], AF.Copy, ["ps5", "osqR"], ["oR"])
            yield
            MM(PS[3][:], ones[:], T["osqR"], True, True, ["ones", "osqR"], ["ps3"])
            psm, pnm = nb()
            MM(psm[:], ones[:], T["oR"], True, True, ["ones", "oR"], [pnm])
            A(T["m1R"], psm[:], AF.Copy, [pnm], ["m1R"], scale=1.0 / 128.0)
            TT(pool, T["r2R"], T["m1R"], T["m1R"], ALU.mult, ["m1R"], ["r2R"])
            yield
            STT(T["r1R"], PS[3][:], 1.0 / 128.0, T["r2R"], ALU.mult, ALU.subtract, ["ps3", "r2R"], ["r1R"])
            A(T["r1R"], T["r1R"], AF.Ln, ["r1R"], ["r1R"], bias=EPS_R)
            yield
            A(T["r2R"], T["r1R"], AF.Exp, ["r1R"], ["r2R"], scale=-0.5)
            TT(pool, T["oR"], T["oR"], T["m1R"], ALU.subtract, ["oR", "m1R"], ["oR"])
            yield
            TT(pool, T["oR"], T["oR"], T["r2R"], ALU.mult, ["oR", "r2R"], ["oR"])
            yield
            TT(dve, Y[:, 8 + h, :], T["oR"], sgR[p][:], ALU.mult, ["oR", "sgR%d" % p], ["Y%d" % (8 + h)])
            yield

        def late2(l, s, h, p):
            cf = l * 16 + h
            cbk = l * 16 + 8 + h
            Df, Ef, Db, Eb = scl["Df"][p], scl["Ef"][p], scl["Db"][p], scl["Eb"][p]
            gf = g64[0:64, cf:cf + 1]
            gb = g64[0:64, cbk:cbk + 1]

            def preAb():
                fw.dma(qs, SbA[:, h, :], SBId[s, h, 0], r=["SBId%d_%d" % (s, h)], w=["SbA%d" % h])

            def preRb():
                fw.dma(qs, SbR[:, h, :], SBId[s, h, 1, 0:64], r=["SBId%d_%d" % (s, h)], w=["SbR%d" % h])
            rec = [
                g_recur(SfA[:, h, :], "SfA%d" % h, ASC, KHf[p], "KHf%d" % p, vA[p], "vA%d" % p, lambda c: Df[:, c:c + 1], "Df%d" % p,
                        snf, "snf", lambda c: Ef[:, c:c + 1], "Ef%d" % p, 128, chn[:, s:s + 1], 3, ci=0),
                g_recur(SbA[:, h, :], "SbA%d" % h, DESC, KHb[p], "KHb%d" % p, vA[p], "vA%d" % p, lambda c: Db[:, c:c + 1], "Db%d" % p,
                        snb, "snb", lambda c: Eb[:, c:c + 1], "Eb%d" % p, 128, None, 3, pre=preAb, ci=1),
                g_recur(SfR[:, h, :], "SfR%d" % h, ASC, KRf[p], "KRf%d" % p, vR[p], "vR%d" % p, lambda c: gf, "g64",
                        srf, "srf", None, None, 64, chn[0:64, s:s + 1], 5, ci=2),
                g_recur(SbR[:, h, :], "SbR%d" % h, DESC, KRb[p], "KRb%d" % p, vR[p], "vR%d" % p, lambda c: gb, "g64",
                        srb, "srb", None, None, 64, None, 5, pre=preRb, ci=3)]
            return [rec + [g_scA(l, h, p), g_scR(l, h, p)], [g_outA(l, h, p), g_outR(l, h, p)]]

        def tail(l, s):
            last = (l == depth - 1)
            yn = ["Y%d" % i for i in range(16)]
            for hf in range(2):
                fw.dma(qs, WO[:, hf], WOd[l, hf], r=["WOd"], w=["WO"])
            fw.dma(qs, g1r, g1pd[l, s].partition_broadcast(128), r=["g1pd"], w=["g1r"])
            fw.dma(qs, lngr, ln_g[l].partition_broadcast(128), r=["ln_g"], w=["lngr"])
            fw.dma(qs, lnbr, ln_b[l].partition_broadcast(128), r=["ln_b"], w=["lnbr"])
            fw.dma(qs, TWb[0], TWd[l, 0], r=["TWd"], w=["TW0"])
            for cb in range(8):
                TW, twn = TWb[cb % 2], "TW%d" % (cb % 2)
                if cb < 7:
                    fw.dma(qs, TWb[(cb + 1) % 2], TWd[l, cb + 1], r=["TWd"], w=["TW%d" % ((cb + 1) % 2)])
                for (j, ps, pn, kofs) in ((0, PS[0], "ps0", 0), (1, PS[1], "ps1", 8)):
                    for kc in range(8):
                        MM(ps[:], TW[:, kc, j, :], Y[:, kofs + kc, :], kc == 0, kc == 7, [twn] + yn, [pn])
                for (j, ps, pn) in ((2, PS[4], "ps4"), (3, PS[5], "ps5")):
                    for kc in range(8):
                        MM(ps[:], TW[:, kc, j, :], uT[:, kc, :], kc == 0, kc == 7, [twn, "uT"], [pn])
                A(T["t_r1"], PS[4][:], AF.Tanh, ["ps4", "hbm"], ["t_r1"], scale=0.5, bias=hbm[:, l, cb:cb + 1])
                A(T["t_r2"], PS[5][:], AF.Tanh, ["ps5", "hbm"], ["t_r2"], scale=0.5, bias=hbm[:, l, 8 + cb:8 + cb + 1])
                STT(T["t_x1"], T["t_r1"], 1.0, PS[0][:], ALU.add, ALU.mult, ["t_r1", "ps0"], ["t_x1"])
                STT(T["t_x2"], T["t_r2"], 1.0, PS[1][:], ALU.add, ALU.mult, ["t_r2", "ps1"], ["t_x2"])
                TT(pool, MT[:, cb, :], T["t_x1"], T["t_x2"], ALU.add, ["t_x1", "t_x2"], ["MT"])
            xsrc = x_tok if l == 0 else x1d
            for i in range(4):
                t0 = s * TS + i * 128
                fw.dma(qs, hh_, xsrc[t0:t0 + 128, :], r=["x1d" if l else "x_tok"], w=["hh"])
                for hf in range(2):
                    ps, pn = (PS[6], "ps6") if hf == 0 else (PS[2], "ps2")
                    for kc in range(8):
                        MM(ps[:], MT[:, kc, i * 128:(i + 1) * 128], WO[:, hf, kc, :], kc == 0, kc == 7, ["MT", "WO"], [pn])
                    hs = slice(hf * 512, (hf + 1) * 512)
                    TT(dve, T["t_o"], ps[:], g1r[:, hs], ALU.mult, [pn, "g1r"], ["t_o"])
                    STT(hh_[:, hs], hh_[:, hs], ALPHA, T["t_o"], ALU.mult, ALU.add, ["hh", "t_o"], ["hh"])
                A(zz, hh_, AF.Identity, ["hh"], ["zz", "st4a"], accum_out=st4[:, 0:1])
                A(zz, hh_, AF.Square, ["hh"], ["zz", "st4b"], accum_out=st4[:, 1:2])
                TSC(dve, st4[:, 2:3], st4[:, 0:1], 1.0 / D, None, ALU.mult, None, ["st4a"], ["st4c"])
                TT(dve, st4[:, 3:4], st4[:, 2:3], st4[:, 2:3], ALU.mult, ["st4c"], ["st4d"])
                STT(st4[:, 4:5], st4[:, 1:2], 1.0 / D, st4[:, 3:4], ALU.mult, ALU.subtract, ["st4b", "st4d"], ["st4e"])
                A(st4[:, 4:5], st4[:, 4:5], AF.Ln, ["st4e"], ["st4e"], bias=1e-5)
                A(st4[:, 5:6], st4[:, 4:5], AF.Exp, ["st4e"], ["st4f"], scale=-0.5)
                STT(st4[:, 6:7], st4[:, 2:3], -1.0, st4[:, 5:6], ALU.mult, ALU.mult, ["st4c", "st4f"], ["st4g"])
                A(zz, hh_, AF.Identity, ["hh", "st4f", "st4g"], ["zz"], scale=st4[:, 5:6], bias=st4[:, 6:7])
                TT(dve, zz, zz, lngr, ALU.mult, ["zz", "lngr"], ["zz"])
                TT(pool, zz, zz, lnbr, ALU.add, ["zz", "lnbr"], ["zz"])
                if last:
                    fw.dma(qs, y_out[t0:t0 + 128, :], zz, r=["zz"], w=["y"])
                else:
                    fw.dma(qs, x1d[t0:t0 + 128, :], zz, r=["zz"], w=["x1d"])
                    for g in range(2):
                        ps, pn = (PS[4], "ps4") if g == 0 else (PS[5], "ps5")
                        for kk_ in range(4):
                            kc = g * 4 + kk_
                            TR(ps[:, kk_ * 128:(kk_ + 1) * 128], zz[:, kc * 128:(kc + 1) * 128], IDF, ["zz", "cstt"], [pn])
                        for kk_ in range(4):
                            kc = g * 4 + kk_
                            A(u1t[:, kc, :], ps[:, kk_ * 128:(kk_ + 1) * 128], AF.Identity, [pn, "adaT"], ["u1t"],
                              scale=adaT[:, l + 1, 8 + kc, s:s + 1], bias=adaT[:, l + 1, kc, s:s + 1])
                    fw.dma(qs, uTd[l + 1][:, t0:t0 + 128].rearrange("(kc p) t -> p kc t", p=128), u1t[:], r=["u1t"], w=["uTd%d" % (l + 1)])

        for l in range(depth):
            layer_consts(l)
            for t_, nm in ((SfA, "SfA"), (SbA, "SbA"), (SfR, "SfR"), (SbR, "SbR")):
                fw.op(pool, lambda: nc.gpsimd.memset(t_[:], 0.0), [], [nm + str(h) for h in range(8)])
            units = [(s, h) for s in range(NS - 1, -1, -1) for h in (0, 2, 4, 6)]
            prev = None

            def ld1(wp_, h_):
                for j in range(2):
                    fw.dma(qs, WH[wp_][:, :, j * S1C:(j + 1) * S1C], WHd[l, h_ + j][:, :, 0:S1C], r=["WHd"], w=["WH%d" % wp_])
            for ui, (s, h) in enumerate(units):
                wp = ui % 2
                if ui == 0:
                    ld1(wp, h)
                if h == 0:
                    load_uT(l, s)
                    load_rope(s)
                if ui + 1 < len(units):
                    ld1(1 - wp, units[ui + 1][1])
                q0 = 2 * (ui % 2)
                sched([early1(l, s, h, wp, q0), prev])
                prev = late1(l, s, h, q0)
            sched([prev])
            stage(5)
            units = [(s, h) for s in range(NS) for h in range(8)]
            prev = None
            for ui, (s, h) in enumerate(units):
                p = wp = ui % 2
                if ui == 0:
                    fw.dma(qs, WH[wp][:], WHd[l, h], r=["WHd"], w=["WH%d" % wp])
                if h == 0:
                    load_uT(l, s)
                    load_rope(s)
                if ui + 1 < len(units):
                    hn = units[ui + 1][1]
                    fw.dma(qs, WH[1 - wp][:], WHd[l, hn], r=["WHd"], w=["WH%d" % (1 - wp)])
                sched([early2(l, s, h, p, wp), prev])
                prev = late2(l, s, h, p)
                if h == 7:
                    sched([prev])
                    prev = None
                    stage(6)
                    tail(l, s)
        fw.finish(["y"])
        build.stats = (fw.n_inst, fw.n_wait)
    return nc


def _consts():
    s = np.arange(128)[:, None]
    t = np.arange(128)[None, :]
    same = (s // 64) == (t // 64)
    MF = (same & (s <= t)).astype(np.float32)
    MB = (same & (s >= t)).astype(np.float32)
    DF = np.where(same & (s <= t), t - s, 0).astype(np.float32)
    DB = np.where(same & (s >= t), s - t, 0).astype(np.float32)
    TP = np.zeros((128, 128), np.float32)
    TP[:, 0:64] = np.arange(64)[None, :] + 1
    TP[:, 64:128] = 64 - np.arange(64)[None, :]
    CF = np.zeros((128, 2), np.float32)
    CF[:, 0] = 63 - (np.arange(128) % 64)
    CF[:, 1] = np.arange(128) % 64
    return np.concatenate([MF, MB, DF, DB, TP, CF, np.eye(128, dtype=np.float32)], axis=1)


def _rope_tables(pos0):
    inv = (10000.0 ** (-np.arange(0, 64, 2, dtype=np.float32) / 64)).astype(np.float32)
    pos = (pos0 + np.arange(TS)).astype(np.float32)
    ang = (pos[None, :] * inv[:, None]).astype(np.float32)
    c = np.cos(ang).astype(np.float32)
    s_ = np.sin(ang).astype(np.float32)
    return np.concatenate([c, c], 0), np.concatenate([-s_, s_], 0)


def make_core_inputs(seqs, NS, weights):
    NT = NS * TS
    x_tok = np.zeros((NT, D), np.float32)
    cT = np.zeros((D, NS), np.float32)
    chain = np.zeros((1, NS + 1), np.float32)
    rC = np.zeros((NS, 64, TS), np.float32)
    rS = np.zeros((NS, 64, TS), np.float32)
    s = 0
    for (x, c) in seqs:
        n = x.shape[0] // TS
        x_tok[s * TS:(s + n) * TS] = x
        for j in range(n):
            cT[:, s + j] = c
            if j > 0:
                chain[0, s + j] = 1.0
            rC[s + j], rS[s + j] = _rope_tables(j * TS)
        s += n
    for j in range(s, NS):
        rC[j], rS[j] = _rope_tables(0)
    m = dict(weights)
    m.update(x_tok=x_tok, xT=np.ascontiguousarray(x_tok.T), cT=cT, chain=chain, ropeC=rC, ropeS=rS, cst=_consts())
    return m


def kernel(x_prompt, x_sample, c_prompt, c_sample, w_ada, b_ada, w_in, hgrn_lb, a_norm_w, ret_decay,
           w_pa, w_pb, w_mg, b_mg, w_out, ln_g, ln_b):
    f = lambda a: np.ascontiguousarray(np.asarray(a, dtype=np.float32))
    weights = dict(w_ada=f(w_ada), b_ada=f(b_ada), w_in=f(w_in), hgrn_lb=f(hgrn_lb), a_norm_w=f(a_norm_w),
                   ret_decay=f(ret_decay).reshape(1, 32), w_pa=f(w_pa), w_pb=f(w_pb), w_mg=f(w_mg), b_mg=f(b_mg),
                   w_out=f(w_out), ln_g=f(ln_g), ln_b=f(ln_b))
    x_prompt, x_sample, c_prompt, c_sample = f(x_prompt), f(x_sample), f(c_prompt), f(c_sample)
    NS = 24
    assign = [[("s", 0), ("p", 0)], [("s", 1), ("p", 1)], [("p", 2), ("p", 3), ("p", 4)], [("p", 5), ("p", 6), ("p", 7)],
              [("p", 8), ("p", 9)], [("p", 10), ("p", 11)], [("p", 12), ("p", 13)], [("p", 14), ("p", 15)]]
    in_maps = []
    for core in assign:
        seqs = [((x_sample[i], c_sample[i]) if k == "s" else (x_prompt[i], c_prompt[i])) for (k, i) in core]
        in_maps.append(make_core_inputs(seqs, NS, weights))
    nc = build(NS)
    res = run_bass_kernel_spmd(nc, in_maps, core_ids=list(range(8)))
    y_prompt = np.zeros_like(x_prompt)
    y_sample = np.zeros_like(x_sample)
    for ci, core in enumerate(assign):
        yc = res.results[ci]["y"]
        s = 0
        for (k, i) in core:
            if k == "s":
                n = x_sample.shape[1]
                y_sample[i] = yc[s:s + n]
            else:
                n = x_prompt.shape[1]
                y_prompt[i] = yc[s:s + n]
            s += n
    return (y_prompt, y_sample)
```

```python
import contextlib
import numpy as np
import concourse.bass as bass
import concourse.mybir as mybir
from concourse.bass_utils import run_bass_kernel_spmd

F32 = mybir.dt.float32
BF16 = mybir.dt.bfloat16
ALU = mybir.AluOpType
AF = mybir.ActivationFunctionType

D = 1024
TS = 512
CH = 64
NCH = TS // CH
WHC = 1152
O_FB, O_AI, O_RK, O_RKS, O_RV, O_AQ, O_FF, O_AG, O_RQ, O_RQS, O_RG = 0, 128, 256, 320, 384, 512, 640, 768, 896, 960, 1024
S1C = 512
IN_OFF = {"aq": 0, "ff": 1024, "fb": 2048, "ai": 3072, "ag": 4096, "rq": 5120, "rk": 5632, "rv": 6144, "rg": 7168}
EPS_A = 1e-5 * 128.0
EPS_R = 1e-5 * 64.0
ALPHA = 4.0 ** 0.25
BIG = 1e30


class Buf:
    __slots__ = ("name", "lw", "rd", "tw", "tr")

    def __init__(self, name):
        self.name = name
        self.lw = None
        self.rd = []
        self.tw = 0.0
        self.tr = 0.0


class Eng:
    def __init__(self, fw, name, handle, is_pe=False):
        self.name = name
        self.h = handle
        self.is_pe = is_pe
        self.sem = fw.new_sem("s_" + name)
        self.count = 0
        self.waited = {}


class DmaQ:
    def __init__(self, fw, name, handle, nslots):
        self.name = name
        self.h = handle
        self.sems = [fw.new_sem("d_%s%d" % (name, i)) for i in range(nslots)]
        self.uses = [0] * nslots
        self.idx = 0
        self.waited = {}
        self.is_pe = False


class FW:
    def __init__(self, nc, stack):
        self.nc = nc
        self.stack = stack
        self.bufs = {}
        self.pe = Eng(self, "pe", nc.tensor, is_pe=True)
        self.act = Eng(self, "act", nc.scalar)
        self.dve = Eng(self, "dve", nc.vector)
        self.pool = Eng(self, "pool", nc.gpsimd)
        self.q_sync = DmaQ(self, "sy", nc.sync, 32)
        self.q_pool = DmaQ(self, "gp", nc.gpsimd, 8)
        self.q_pool.waited = self.pool.waited
        self.n_inst = 0
        self.n_wait = 0
        self.alias = {}
        self.t_eng = {}
        self.last_finish = 0.0

    def _vt(self, eng, reads, writes, dur, occupy=True):
        st = self.t_eng.get(id(eng), 0.0)
        for b in reads:
            if b.tw > st:
                st = b.tw
        for b in writes:
            if b.tw > st:
                st = b.tw
            if b.tr > st:
                st = b.tr
        fin = st + dur
        if occupy:
            self.t_eng[id(eng)] = fin
        else:
            self.t_eng[id(eng)] = st + 0.1
        for b in reads:
            if fin > b.tr:
                b.tr = fin
        for b in writes:
            b.tw = fin + 1.2
        self.last_finish = fin

    def new_sem(self, name):
        return self.stack.enter_context(self.nc.semaphore(name))

    def _bl(self, xs):
        out = []
        for x in xs:
            for nm in self.alias.get(x, (x,)):
                b = self.bufs.get(nm)
                if b is None:
                    b = Buf(nm)
                    self.bufs[nm] = b
                out.append(b)
        return out

    def _wait(self, eng, sem, val):
        key = id(sem)
        if eng.waited.get(key, 0) >= val:
            return
        eng.h.wait_ge(sem, val)
        eng.waited[key] = val
        self.n_wait += 1

    def _deps(self, reads, writes):
        deps = {}
        for b in reads:
            if b.lw is not None:
                deps[(id(b.lw[0]), b.lw[1])] = b.lw
        for b in writes:
            if b.lw is not None:
                deps[(id(b.lw[0]), b.lw[1])] = b.lw
            for r in b.rd:
                deps[(id(r[0]), r[1])] = r
        return list(deps.values())

    def _mark(self, tok, reads, writes):
        for b in reads:
            b.rd.append(tok)
            if len(b.rd) > 12:
                last = {}
                for t in b.rd:
                    k = id(t[0])
                    if k not in last or last[k][1] < t[1]:
                        last[k] = t
                b.rd = list(last.values())
        for b in writes:
            b.lw = tok
            b.rd = []

    def op(self, eng, fn, r=(), w=(), cost=0.6):
        reads = self._bl(r)
        writes = self._bl(w)
        self._vt(eng, reads, writes, cost)
        for (sem, val, src) in self._deps(reads, writes):
            if src is eng and eng.is_pe:
                continue
            self._wait(eng, sem, val)
        inst = fn()
        eng.count += 1
        inst.then_inc(eng.sem, 1)
        self._mark((eng.sem, eng.count, eng), reads, writes)
        self.n_inst += 1
        return inst

    def dma(self, q, out, in_, r=(), w=(), **kw):
        reads = self._bl(r)
        writes = self._bl(w)
        self._vt(q, reads, writes, 4.0, occupy=False)
        for (sem, val, src) in self._deps(reads, writes):
            self._wait(q, sem, val)
        j = q.idx % len(q.sems)
        q.idx += 1
        sem = q.sems[j]
        if q.uses[j] > 0:
            self._wait(q, sem, 16 * q.uses[j])
        q.uses[j] += 1
        inst = q.h.dma_start(out=out, in_=in_, **kw)
        inst.then_inc(sem, 16)
        self._mark((sem, 16 * q.uses[j], q), reads, writes)
        self.n_inst += 1
        return inst

    def barrier(self):
        engs = [self.pe, self.act, self.dve, self.pool]
        qs = [self.q_sync, self.q_pool]
        for e in engs + [self.q_sync]:
            for o in engs:
                if o is not e and o.count > 0:
                    self._wait(e, o.sem, o.count)
            for q in qs:
                for j, sem in enumerate(q.sems):
                    if q.uses[j] > 0:
                        self._wait(e, sem, 16 * q.uses[j])

    def finish(self, names):
        for b in self._bl(names):
            if b.lw is not None:
                self._wait(self.q_sync, b.lw[0], b.lw[1])
            for t in b.rd:
                self._wait(self.q_sync, t[0], t[1])


class _Stop(Exception):
    pass


def build(NS, depth=2, upto=99):
    NT = NS * TS
    nc = bass.Bass("TRN2", target_bir_lowering=False)
    dt_in = lambda name, shape, dt=F32: nc.dram_tensor(name, shape, dt, kind="ExternalInput").ap()
    x_tok = dt_in("x_tok", [NT, D])
    xT = dt_in("xT", [D, NT])
    cT = dt_in("cT", [D, NS])
    chain = dt_in("chain", [1, NS + 1])
    ropeC = dt_in("ropeC", [NS, 64, TS])
    ropeS = dt_in("ropeS", [NS, 64, TS])
    w_ada = dt_in("w_ada", [2, D, 3 * D])
    b_ada = dt_in("b_ada", [2, 3 * D])
    w_in = dt_in("w_in", [2, D, 8192])
    hgrn_lb = dt_in("hgrn_lb", [2, D])
    a_norm_w = dt_in("a_norm_w", [2, 128])
    ret_decay = dt_in("ret_decay", [1, 32])
    w_pa = dt_in("w_pa", [2, D, D])
    w_pb = dt_in("w_pb", [2, D, D])
    w_mg = dt_in("w_mg", [2, D, 2 * D])
    b_mg = dt_in("b_mg", [2, 2 * D])
    w_out = dt_in("w_out", [2, D, D])
    ln_g = dt_in("ln_g", [2, D])
    ln_b = dt_in("ln_b", [2, D])
    cst = dt_in("cst", [128, 4 * 128 + 128 + 2 + 128])
    y_out = nc.dram_tensor("y", [NT, D], F32, kind="ExternalOutput").ap()

    uTd = [nc.dram_tensor("uTd%d" % l, [D, NT], BF16).ap() for l in range(2)]
    x1d = nc.dram_tensor("x1d", [NT, D], F32).ap()
    WHd = nc.dram_tensor("WHd", [2, 8, 128, 8, WHC], BF16).ap()
    TWd = nc.dram_tensor("TWd", [2, 8, 128, 8, 4, 128], BF16).ap()
    WOd = nc.dram_tensor("WOd", [2, 2, 128, 8, 512], BF16).ap()
    g1pd = nc.dram_tensor("g1pd", [2, NS, D], F32).ap()
    SBId = nc.dram_tensor("SBId", [NS, 8, 2, 128, 128], F32).ap()

    with contextlib.ExitStack() as st, contextlib.suppress(_Stop):
        fw = FW(nc, st)
        pe, act, dve, pool, qs, qp = fw.pe, fw.act, fw.dve, fw.pool, fw.q_sync, fw.q_pool
        sb = lambda name, shape, dt=F32: st.enter_context(nc.sbuf_tensor(name, shape, dt))
        psum = lambda name, dt=F32: st.enter_context(nc.psum_tensor(name, [128, 512], dt))
        PS = [psum("ps%d" % i) for i in range(7)]
        PST = psum("pst", BF16)

        def A(out, in_, func, r, w, bias=None, scale=1.0, accum_out=None):
            kw = {}
            if bias is not None:
                kw["bias"] = bias
            if accum_out is not None:
                kw["accum_out"] = accum_out
            return fw.op(act, lambda: nc.scalar.activation(out=out, in_=in_, func=func, scale=scale, **kw), r, w,
                         cost=0.22 + in_.free_size() / 1200.0)

        def TT(eng, out, in0, in1, op, r, w):
            h = nc.vector if eng is dve else nc.gpsimd
            return fw.op(eng, lambda: h.tensor_tensor(out=out, in0=in0, in1=in1, op=op), r, w,
                         cost=(0.1 + in0.free_size() / 960.0) if eng is dve else (0.2 + in0.free_size() / 450.0))

        def TSC(eng, out, in0, s1, s2, op0, op1, r, w):
            h = nc.vector if eng is dve else nc.gpsimd
            if s2 is None:
                return fw.op(eng, lambda: h.tensor_scalar(out=out, in0=in0, scalar1=s1, scalar2=None, op0=op0), r, w,
                             cost=0.1 + in0.free_size() / 960.0)
            return fw.op(eng, lambda: h.tensor_scalar(out=out, in0=in0, scalar1=s1, scalar2=s2, op0=op0, op1=op1), r, w,
                         cost=0.1 + in0.free_size() / 960.0)

        def STT(out, in0, scalar, in1, op0, op1, r, w):
            return fw.op(dve, lambda: nc.vector.scalar_tensor_tensor(out=out, in0=in0, scalar=scalar, in1=in1, op0=op0, op1=op1), r, w,
                         cost=0.1 + in0.free_size() / 960.0)

        def MM(out, lhsT, rhs, start, stop, r, w):
            return fw.op(pe, lambda: nc.tensor.matmul(out, lhsT=lhsT, rhs=rhs, start=start, stop=stop), r, w,
                         cost=(0.07 + rhs.free_size() / 1950.0) * (4.0 if rhs.dtype == F32 else 1.0))

        def TR(out, in_, ident, r, w):
            return fw.op(pe, lambda: nc.tensor.transpose(out, in_, ident), r, w, cost=0.12)

        cstt = sb("cstt", [128, 4 * 128 + 128 + 2 + 128])
        fw.dma(qs, cstt[:], cst, r=["cst"], w=["cstt"])
        MF = cstt[:, 0:128]
        MB = cstt[:, 128:256]
        DF = cstt[:, 256:384]
        DB = cstt[:, 384:512]
        TP1 = cstt[0:64, 512:576]
        TPB = cstt[0:64, 576:640]
        CF = cstt[:, 640:641]
        CB = cstt[:, 641:642]
        IDF = cstt[:, 642:770]
        idb = sb("idb", [128, 128], BF16)
        fw.op(dve, lambda: nc.vector.tensor_copy(out=idb[:], in_=IDF), ["cstt"], ["idb"])
        ones = sb("ones", [128, 128])
        fw.op(pool, lambda: nc.gpsimd.memset(ones[:], 1.0), [], ["ones"])
        smask = sb("smask", [128, TS])
        fw.op(pool, lambda: nc.gpsimd.memset(smask[:], 1.0), [], ["smask"])
        smv = smask[:].rearrange("p (c t) -> p c t", t=CH)
        fw.op(pool, lambda: nc.gpsimd.memset(smv[:, :, 0:1], 0.0), [], ["smask"])
        chn = sb("chn", [128, NS + 1])
        fw.dma(qs, chn[:], chain[0].partition_broadcast(128), r=["chain"], w=["chn"])

        with nc.allow_non_contiguous_dma(reason="weight relayout"):
            for l in range(2):
                WHv = WHd[l].rearrange("h p kc c -> h p kc c")
                segs = [("aq", O_AQ, 128), ("ff", O_FF, 128), ("fb", O_FB, 128), ("ai", O_AI, 128), ("ag", O_AG, 128),
                        ("rq", O_RQ, 64), ("rk", O_RK, 64), ("rv", O_RV, 128), ("rg", O_RG, 128)]
                for (nm, do, wd) in segs:
                    src = w_in[l][:, IN_OFF[nm]:IN_OFF[nm] + 8 * wd].rearrange("(kc p) (h w) -> h p kc w", p=128, w=wd)
                    for hh in range(8):
                        fw.dma(qp, WHd[l, hh][:, :, do:do + wd], src[hh], r=["w_in"], w=["WHd"])
                for (nm, do) in (("rq", O_RQS), ("rk", O_RKS)):
                    src = w_in[l][:, IN_OFF[nm]:IN_OFF[nm] + 512].rearrange("(kc p) (h w) -> h p kc w", p=128, w=64)
                    for hh in range(8):
                        fw.dma(qp, WHd[l, hh][:, :, do:do + 32], src[hh][:, :, 32:64], r=["w_in"], w=["WHd"])
                        fw.dma(qp, WHd[l, hh][:, :, do + 32:do + 64], src[hh][:, :, 0:32], r=["w_in"], w=["WHd"])
                for j, wsrc in enumerate((w_pa[l], w_pb[l], w_mg[l][:, 0:D], w_mg[l][:, D:2 * D])):
                    src = wsrc.rearrange("(kc p) (cb c) -> cb p kc c", p=128, c=128)
                    for cb in range(8):
                        fw.dma(qp, TWd[l, cb][:, :, j, :], src[cb], r=["w_t"], w=["TWd"])
                src = w_out[l].rearrange("(kc p) (hf c) -> hf p kc c", p=128, c=512)
                for hf in range(2):
                    fw.dma(qp, WOd[l, hf], src[hf], r=["w_t"], w=["WOd"])

        prohold = []

        def stage(n):
            if upto <= n:
                fw.barrier()
                for p_ in prohold:
                    p_.close()
                raise _Stop()
        stage(1)
        lbt = sb("lbt", [128, 2, 8])
        c1t = sb("c1t", [128, 2, 8])
        anw = sb("anw", [128, 2])
        hbm = sb("hbm", [128, 2, 16])
        with nc.allow_non_contiguous_dma(reason="tiny param relayout"):
            fw.dma(qs, lbt[:], hgrn_lb.rearrange("l (h k) -> k l h", k=128), r=["hgrn_lb"], w=["lbt"])
            fw.dma(qs, anw[:], a_norm_w.rearrange("l k -> k l"), r=["a_norm_w"], w=["anw"])
            fw.dma(qs, hbm[:], b_mg.rearrange("l (c p) -> p l c", p=128), r=["b_mg"], w=["hbm"])
        tl = sb("tl", [128, 8])
        TT(dve, tl[:], lbt[:, 0, :], lbt[:, 1, :], ALU.subtract, ["lbt"], ["tl"])
        A(tl[:], tl[:], AF.Exp, ["tl"], ["tl"])
        TSC(dve, tl[:], tl[:], 1.0, None, ALU.add, None, ["tl"], ["tl"])
        fw.op(dve, lambda: nc.vector.reciprocal(out=tl[:], in_=tl[:]), ["tl"], ["tl"])
        fw.op(pool, lambda: nc.gpsimd.memset(c1t[:, 0, :], 0.5), [], ["c1t"])
        TSC(dve, c1t[:, 1, :], tl[:], -0.5, 0.5, ALU.mult, ALU.add, ["tl", "c1t"], ["c1t"])
        TSC(dve, hbm[:], hbm[:], 0.5, None, ALU.mult, None, ["hbm"], ["hbm"])
        rdt = sb("rdt", [128, 32])
        fw.dma(qs, rdt[:], ret_decay[0].partition_broadcast(128), r=["ret_decay"], w=["rdt"])
        lg = sb("lg", [128, 32])
        A(lg[:], rdt[:], AF.Exp, ["rdt"], ["lg"], scale=-1.0)
        A(lg[:], lg[:], AF.Ln, ["lg"], ["lg"], bias=1.0)
        TSC(dve, lg[:], lg[:], -1.0, None, ALU.mult, None, ["lg"], ["lg"])
        g64 = sb("g64", [128, 32])
        A(g64[:], lg[:], AF.Exp, ["lg"], ["g64"], scale=64.0)
        DMt = sb("DMt", [128, 8, 128])
        PFt = sb("PFt", [64, 16, 64])
        pfc = sb("pfc", [128, 32])
        tm1 = sb("tm1", [128, 128])
        tm2 = sb("tm2", [128, 128])
        uT = sb("uT", [128, 8, TS], BF16)
        adaT = sb("adaT", [128, 2, 16, NS])
        pro = contextlib.ExitStack()
        prohold.append(pro)
        sbp = lambda name, shape, dt=F32: pro.enter_context(nc.sbuf_tensor(name, shape, dt))
        for l in range(2):
            for h in range(8):
                cf = l * 16 + h
                cbk = l * 16 + 8 + h
                A(pfc[:, cf:cf + 1], CF, AF.Exp, ["cstt", "lg"], ["pfc"], scale=lg[:, cf:cf + 1])
                A(pfc[:, cbk:cbk + 1], CB, AF.Exp, ["cstt", "lg"], ["pfc"], scale=lg[:, cbk:cbk + 1])

        stage(2)
        scT = sbp("scT", [128, 8, NS])
        fw.dma(qs, scT[:], cT.rearrange("(kc p) s -> p kc s", p=128), r=["cT"], w=["scT"])
        A(scT[:], scT[:], AF.Silu, ["scT"], ["scT"])
        badT = sbp("badT", [128, 2, 24])
        with nc.allow_non_contiguous_dma(reason="tiny param relayout"):
            fw.dma(qs, badT[:], b_ada.rearrange("l (c p) -> p l c", p=128), r=["b_ada"], w=["badT"])
        bgr = sbp("bgr", [NS, D])
        wad = sbp("wad", [128, 8, 512])
        grow = sbp("grow", [NS, 512])
        for l in range(2):
            fw.dma(qs, bgr[:], b_ada[l, 2 * D:3 * D].partition_broadcast(NS), r=["b_ada"], w=["bgr"])
            for blk in range(6):
                fw.dma(qs, wad[:], w_ada[l][:, blk * 512:(blk + 1) * 512].rearrange("(kc p) c -> p kc c", p=128),
                       r=["w_ada"], w=["wad"])
                if blk < 4:
                    for cc in range(4):
                        ch = blk * 4 + cc
                        for kc in range(8):
                            MM(PS[0][:, cc * NS:(cc + 1) * NS], wad[:, kc, cc * 128:(cc + 1) * 128], scT[:, kc, :],
                               kc == 0, kc == 7, ["wad", "scT"], ["ps0"])
                        if ch < 8:
                            TSC(dve, adaT[:, l, ch, :], PS[0][:, cc * NS:(cc + 1) * NS], badT[:, l, ch:ch + 1], None,
                                ALU.add, None, ["ps0", "badT"], ["adaT"])
                        else:
                            TSC(dve, adaT[:, l, ch, :], PS[0][:, cc * NS:(cc + 1) * NS], badT[:, l, ch:ch + 1], 1.0,
                                ALU.add, ALU.add, ["ps0", "badT"], ["adaT"])
                else:
                    for kc in range(8):
                        MM(PS[1][0:NS, :], scT[:, kc, :], wad[:, kc, :], kc == 0, kc == 7, ["wad", "scT"], ["ps1"])
                    c0 = (blk - 4) * 512
                    TT(dve, grow[:], PS[1][0:NS, :], bgr[:, c0:c0 + 512], ALU.add, ["ps1", "bgr"], ["grow"])
                    TSC(dve, grow[:], grow[:], 0.5, 0.5, ALU.mult, ALU.add, ["grow"], ["grow"])
                    fw.dma(qs, g1pd[l][:, c0:c0 + 512], grow[:], r=["grow"], w=["g1pd"])

        stage(3)
        xTt = sbp("xTt", [128, 8, TS])
        for s in range(NS):
            fw.dma(qs, xTt[:], xT[:, s * TS:(s + 1) * TS].rearrange("(kc p) t -> p kc t", p=128), r=["xT"], w=["xTt"])
            for kc in range(8):
                A(uT[:, kc, :], xTt[:, kc, :], AF.Identity, ["xTt", "adaT"], ["uT"],
                  scale=adaT[:, 0, 8 + kc, s:s + 1], bias=adaT[:, 0, kc, s:s + 1])
            fw.dma(qs, uTd[0][:, s * TS:(s + 1) * TS].rearrange("(kc p) t -> p kc t", p=128), uT[:], r=["uT"], w=["uTd0"])

        stage(4)
        fw.barrier()
        pro.close()
        prohold.clear()
        NREG = 32
        arena = sb("arena", [128, NREG * TS])
        WH = [sb("WH%d" % i, [128, 8, WHC], BF16) for i in range(2)]
        lfe = {d: sb("lfe" + d, [128, TS + 1]) for d in "fb"}
        for d in "fb":
            fw.op(pool, lambda: nc.gpsimd.memset(lfe[d][:, 0:1], 0.0), [], ["lfe" + d])
        regs = {}

        def view(name, r0, nreg=1, dt=F32, shape=None, half=0):
            ap = arena[:, r0 * TS:(r0 + nreg) * TS]
            if dt is BF16:
                ap = ap.bitcast(BF16)
                if nreg == 1 and shape is None:
                    ap = ap[:, half * TS:(half + 1) * TS]
            if shape is not None:
                ap = ap.rearrange(shape[0], **shape[1])
            regs[name] = ap
            fw.alias[name] = tuple("R%d" % r for r in range(r0, r0 + nreg))
            return ap

        T = {}
        for i, d in enumerate("fb"):
            b0 = i * 6
            T["aa" + d] = view("aa" + d, b0 + 0)
            T["kk" + d] = view("kk" + d, b0 + 1)
            T["cs" + d] = view("cs" + d, b0 + 2)
            T["ep" + d] = view("ep" + d, b0 + 3)
            T["em" + d] = view("em" + d, b0 + 4)
            T["kh" + d] = view("kh" + d, b0 + 5, dt=BF16)
        T["qs"] = view("qs", 12)
        T["x1"] = view("x1", 13)
        T["x2"] = view("x2", 14)
        T["o"] = view("o", 15)
        T["osq"] = view("osq", 16)
        T["r1"] = view("r1", 17)
        T["r2"] = view("r2", 18)
        T["m1"] = view("m1", 19)
        T["m2"] = view("m2", 20)
        T["Ac"] = view("Ac", 21, dt=BF16)
        s3 = ("p (c k) -> p c k", dict(k=128))
        snf = view("snf", 22, dt=BF16, shape=s3)
        snb = view("snb", 23, dt=BF16, shape=s3)
        srf = view("srf", 24, dt=BF16, shape=s3)
        srb = view("srb", 25, dt=BF16, shape=s3)
        WO = view("WO", 0, 8, dt=BF16, shape=("p (h kc c) -> p h kc c", dict(h=2, kc=8)))
        MT = view("MT", 8, 4, dt=BF16, shape=("p (kc t) -> p kc t", dict(kc=8)))
        g1r = view("g1r", 12, 2)
        hh_ = view("hh", 14, 2)
        zz = view("zz", 16, 2)
        T["t_o"] = view("t_o", 18)
        T["t_r1"] = view("t_r1", 19)
        T["t_r2"] = view("t_r2", 20)
        T["t_x1"] = view("t_x1", 21)
        T["t_x2"] = view("t_x2", 22)

        def par(name, shape, dt=BF16):
            return [sb("%s%d" % (name, i), shape, dt) for i in range(2)]
        Qf, Qb, Kf, Kb = par("Qf", [128, TS]), par("Qb", [128, TS]), par("Kf", [128, TS]), par("Kb", [128, TS])
        KHf, KHb = par("KHf", [128, 4, 128]), par("KHb", [128, 4, 128])
        vA, sgA = par("vA", [128, 4, 128]), par("sgA", [128, TS])
        qr, kr, Qrf, Qrb = par("qr", [64, TS]), par("kr", [64, TS]), par("Qrf", [64, TS]), par("Qrb", [64, TS])
        KRf, KRb = par("KRf", [128, 4, 64]), par("KRb", [128, 4, 64])
        vR, sgR = par("vR", [128, 4, 128]), par("sgR", [128, TS])
        scl = {n: par("c_" + n, [128, NCH], F32) for n in ("Df", "Ef", "Gf", "Db", "Eb", "Gb", "t1f", "t2f", "t1b", "t2b")}
        SfA = sb("SfA", [128, 8, 128])
        SbA = sb("SbA", [128, 8, 128])
        SfR = sb("SfR", [64, 8, 128])
        SbR = sb("SbR", [64, 8, 128])
        rC = sb("rC", [64, TS])
        rS = sb("rS", [64, TS])
        Y = sb("Y", [128, 16, TS], BF16)
        TWb = [sb("TW0", [128, 8, 4, 128], BF16)[:], view("TW1", 27, 4, dt=BF16, shape=("p (kc j c) -> p kc j c", dict(kc=8, j=4)))]
        lngr = view("lngr", 23, 2)
        lnbr = view("lnbr", 25, 2)
        st4 = sb("st4", [128, 8])
        u1t = sb("u1t", [128, 8, 128], BF16)

        def run_all(gens):
            gens = [g for g in gens if g is not None]
            while gens:
                for g in list(gens):
                    try:
                        next(g)
                    except StopIteration:
                        gens.remove(g)

        def sched(parts):
            sts = []
            for p_ in parts:
                if p_:
                    sts.append({"stages": [list(x) for x in p_], "cur": {}, "t0": 0.0})
            def refill(st):
                while not st["cur"] and st["stages"]:
                    for g in st["stages"].pop(0):
                        st["cur"][g] = st["t0"]
            for st in sts:
                refill(st)
            while True:
                best = None
                for st in sts:
                    for g, c in st["cur"].items():
                        if best is None or c < best[2]:
                            best = (st, g, c)
                if best is None:
                    break
                st, g, _ = best
                try:
                    next(g)
                    st["cur"][g] = fw.last_finish
                    if fw.last_finish > st["t0"]:
                        st["t0"] = fw.last_finish
                except StopIteration:
                    del st["cur"][g]
                    refill(st)

        def inter(*gens):
            gens = [g for g in gens if g is not None]
            while gens:
                for g in list(gens):
                    try:
                        next(g)
                        yield
                    except StopIteration:
                        gens.remove(g)

        def layer_consts(l):
            for h in range(8):
                cf = l * 16 + h
                cbk = l * 16 + 8 + h
                A(tm1[:], DF, AF.Exp, ["cstt", "lg"], ["tm1"], scale=lg[:, cf:cf + 1])
                TT(dve, tm1[:], tm1[:], MF, ALU.mult, ["tm1", "cstt"], ["tm1"])
                A(tm2[:], DB, AF.Exp, ["cstt", "lg"], ["tm2"], scale=lg[:, cbk:cbk + 1])
                TT(dve, tm2[:], tm2[:], MB, ALU.mult, ["tm2", "cstt"], ["tm2"])
                TT(dve, DMt[:, h, :], tm1[:], tm2[:], ALU.add, ["tm1", "tm2"], ["DMt"])
                A(PFt[:, h, :], TP1, AF.Exp, ["cstt", "lg"], ["PFt"], scale=lg[0:64, cf:cf + 1])
                A(PFt[:, 8 + h, :], TPB, AF.Exp, ["cstt", "lg"], ["PFt"], scale=lg[0:64, cbk:cbk + 1])

        def load_uT(l, s):
            fw.dma(qs, uT[:], uTd[l][:, s * TS:(s + 1) * TS].rearrange("(kc p) t -> p kc t", p=128), r=["uTd%d" % l], w=["uT"])

        def load_rope(s):
            fw.dma(qs, rC[:], ropeC[s], r=["ropeC"], w=["rC"])
            fw.dma(qs, rS[:], ropeS[s], r=["ropeS"], w=["rS"])

        def proj_fm(ps, psname, wp, col0, ncol):
            for kc in range(8):
                MM(ps[0:ncol, :], WH[wp][:, kc, col0:col0 + ncol], uT[:, kc, :], kc == 0, kc == 7, ["WH%d" % wp, "uT"], [psname])

        def proj_tm(ps, psname, wp, col0):
            for i in range(4):
                for kc in range(8):
                    MM(ps[:, i * 128:(i + 1) * 128], uT[:, kc, i * 128:(i + 1) * 128], WH[wp][:, kc, col0:col0 + 128],
                       kc == 0, kc == 7, ["WH%d" % wp, "uT"], [psname])

        pbank = [0]

        def nb():
            b = pbank[0] % 3
            pbank[0] += 1
            return PS[b], "ps%d" % b

        T["vT"] = view("vT", 26, dt=BF16)
        T["AcR"] = view("AcR", 26, dt=BF16, half=1)
        T["oR"] = view("oR", 27)
        T["osqR"] = view("osqR", 28)
        T["r1R"] = view("r1R", 29)
        T["r2R"] = view("r2R", 30)
        T["m1R"] = view("m1R", 31)
        fw.alias["m1R"] = ("R31a", "R31b", "R31c", "R31d")
        s4 = ("p (j k) -> p j k", dict(k=128))
        DS = []
        for ci, (ra, rb) in enumerate(((15, 16), (17, 18), (27, 28), (29, 30))):
            va = view("dS%da" % ci, ra, shape=s4)
            vb = view("dS%db" % ci, rb, shape=s4)
            xv = arena[:, 31 * TS + ci * 128:31 * TS + (ci + 1) * 128]
            fw.alias["X%d" % ci] = ("R31" + "abcd"[ci],)
            DS.append((va, "dS%da" % ci, vb, "dS%db" % ci, xv, "X%d" % ci))

        def proj_v(wp, col0, dst, dname):
            ps, pn = nb()
            proj_fm(ps, pn, wp, col0, 128)
            A(T["vT"], ps[:], AF.Copy, [pn], ["vT"])
            transpose_k(T["vT"], "vT", 128, [(dst, dname, None, act)])

        def transpose_k(src_bf, srcname, nrow, outs):
            for i in range(4):
                TR(PST[:, i * nrow:(i + 1) * nrow], src_bf[0:nrow, i * 128:(i + 1) * 128], idb[0:nrow, 0:nrow], [srcname, "idb"], ["pst"])
            pv = PST[:, 0:4 * nrow].rearrange("p (i k) -> p i k", k=nrow)
            prev = []
            for (dst, dname, sc, eng) in outs:
                if eng is act:
                    if sc is None:
                        A(dst[:], pv, AF.Copy, ["pst"] + prev, [dname])
                    else:
                        A(dst[:], pv, AF.Copy, ["pst", "pfc"] + prev, [dname], scale=sc)
                else:
                    fw.op(dve, lambda: nc.vector.tensor_scalar(out=dst[:], in0=pv, scalar1=sc, scalar2=None, op0=ALU.mult),
                          ["pst", "pfc"] + prev, [dname])
                prev = [dname]

        def g_gate(l, h, p, wp, d, full, ts=None, cofs=0, res=None, skip=False):
            bwd = (d == "b")
            ts = ts or d
            aa, kk, cs, ep, em, kh = T["aa" + ts], T["kk" + ts], T["cs" + ts], T["ep" + ts], T["em" + ts], T["kh" + ts]
            lf = lfe[ts]
            n = lambda x: x + ts
            if not skip:
                ps, pn = nb()
                proj_fm(ps, pn, wp, (O_FB if bwd else O_FF) + cofs, 128)
                A(aa, ps[:], AF.Tanh, [pn], [n("aa")], scale=-0.5)
                yield
            c1 = c1t[:, l, h:h + 1]
            TSC(dve, kk, aa, c1, c1, ALU.mult, ALU.add, [n("aa"), "c1t"], [n("kk")])
            yield
            A(lf[:, 1:TS + 1], kk, AF.Ln, [n("kk")], [n("lfe")], scale=-1.0, bias=1.0)
            yield
            csv = cs.rearrange("p (c t) -> p c t", t=CH)
            aav = aa.rearrange("p (c t) -> p c t", t=CH)
            Dn, En, Gn = scl["D" + d][p], scl["E" + d][p], scl["G" + d][p]
            t1, t2 = scl["t1" + d][p], scl["t2" + d][p]
            Dnm, Enm, Gnm = "D%s%d" % (d, p), "E%s%d" % (d, p), "G%s%d" % (d, p)
            if res is not None:
                Dn, Dnm = res["D"]
            if not bwd:
                fw.op(dve, lambda: nc.vector.tensor_tensor_scan(out=cs, data0=smask[:], data1=lf[:, 1:TS + 1], initial=0.0,
                                                                op0=ALU.mult, op1=ALU.add), ["smask", n("lfe")], [n("cs")])
                yield
                A(Dn[:], csv[:, :, CH - 1], AF.Exp, [n("cs")], [Dnm])
                if full:
                    A(En[:], csv[:, :, 31], AF.Exp, [n("cs")], [Enm])
                    TT(dve, t1[:], csv[:, :, CH - 1], csv[:, :, 31], ALU.subtract, [n("cs")], [n("ct1")])
                    yield
                    A(Gn[:], t1[:], AF.Exp, [n("ct1")], [Gnm])
                    TT(dve, aav, csv, csv[:, :, 31:32].to_broadcast([128, NCH, CH]), ALU.subtract, [n("cs")], [n("aa")])
            else:
                fw.op(dve, lambda: nc.vector.tensor_tensor_scan(out=cs, data0=lf[:, 0:TS], data1=smask[:], initial=0.0,
                                                                op0=ALU.add, op1=ALU.mult), ["smask", n("lfe")], [n("cs")])
                yield
                lfv = lf[:, 1:TS + 1].rearrange("p (c t) -> p c t", t=CH)
                TT(dve, t1[:], csv[:, :, CH - 1], lfv[:, :, CH - 1], ALU.add, [n("cs"), n("lfe")], [n("ct1")])
                A(Dn[:], t1[:], AF.Exp, [n("ct1")], [Dnm])
                if full:
                    A(Gn[:], csv[:, :, 32], AF.Exp, [n("cs")], [Gnm])
                    TT(dve, t2[:], t1[:], csv[:, :, 32], ALU.subtract, [n("ct1"), n("cs")], [n("ct2")])
                    yield
                    A(En[:], t2[:], AF.Exp, [n("ct2")], [Enm])
                    TT(dve, aav, csv[:, :, 32:33].to_broadcast([128, NCH, CH]), csv, ALU.subtract, [n("cs")], [n("aa")])
            yield
            KH = (KHb if bwd else KHf)[p]
            KHn = "KH%s%d" % (d, p)
            if res is not None:
                KH, KHn = res["KH"]
            if full:
                A(ep, aa, AF.Exp, [n("aa")], [n("ep")])
                yield
                A(em, aa, AF.Exp, [n("aa")], [n("em")], scale=-1.0)
                yield
                Qd, Kd = ((Qb, Kb) if bwd else (Qf, Kf))
                Qn, Kn = "Q%s%d" % (d, p), "K%s%d" % (d, p)
                TT(pool, Qd[p][:], T["qs"], ep, ALU.mult, ["qs", n("ep")], [Qn])
                TT(dve, Kd[p][:], kk, em, ALU.mult, [n("kk"), n("em")], [Kn])
                yield
                TT(dve, kh.rearrange("p (c t) -> p c t", t=CH), Kd[p][:].rearrange("p (c t) -> p c t", t=CH),
                   Gn[:].unsqueeze(2).to_broadcast([128, NCH, CH]), ALU.mult, [Kn, Gnm], [n("kh")])
            else:
                A(ep, cs, AF.Exp, [n("cs")], [n("ep")])
                yield
                TT(dve, kh, kk, ep, ALU.mult, [n("kk"), n("ep")], [n("kh")])
            yield
            transpose_k(kh, n("kh"), 128, [(KH, KHn, None, act)])
            yield

        def g_rope(dst, dname, wp, c_plain, c_swap):
            ps, pn = nb()
            proj_fm(ps, pn, wp, c_plain, 64)
            TT(dve, T["x1"][0:64, :], ps[0:64, :], rC[:], ALU.mult, [pn, "rC"], ["x1"])
            ps, pn = nb()
            proj_fm(ps, pn, wp, c_swap, 64)
            TT(dve, T["x2"][0:64, :], ps[0:64, :], rS[:], ALU.mult, [pn, "rS"], ["x2"])
            TT(pool, dst[:], T["x1"][0:64, :], T["x2"][0:64, :], ALU.add, ["x1", "x2"], [dname])
            yield

        def g_p0(l, h, p, wp, full, cofs=0, res=None, skip=False):
            if full and not skip:
                ps, pn = nb()
                proj_fm(ps, pn, wp, O_AQ, 128)
                A(T["qs"], ps[:], AF.Silu, [pn], ["qs"])
                yield
            vt, vn = res["v"] if res is not None else (vA[p], "vA%d" % p)
            proj_v(wp, O_AI + cofs, vt, vn)
            yield
            if full and not skip:
                ps, pn = nb()
                proj_fm(ps, pn, wp, O_AG, 128)
                A(sgA[p][:], ps[:], AF.Silu, [pn], ["sgA%d" % p])
                yield
            if full:
                yield from g_rope(qr[p], "qr%d" % p, wp, O_RQ, O_RQS)
            yield from g_rope(kr[p], "kr%d" % p, wp, O_RK + cofs, O_RKS + cofs)
            cf = l * 16 + h
            cbk = l * 16 + 8 + h
            if full:
                transpose_k(kr[p], "kr%d" % p, 64, [(KRf[p], "KRf%d" % p, pfc[:, cf:cf + 1], act), (KRb[p], "KRb%d" % p, pfc[:, cbk:cbk + 1], dve)])
                yield
                qv = qr[p][:].rearrange("p (c t) -> p c t", t=CH)
                TT(dve, Qrf[p][:].rearrange("p (c t) -> p c t", t=CH), qv, PFt[:, h, :].unsqueeze(1).to_broadcast([64, NCH, CH]),
                   ALU.mult, ["qr%d" % p, "PFt"], ["Qrf%d" % p])
                yield
                TT(dve, Qrb[p][:].rearrange("p (c t) -> p c t", t=CH), qv, PFt[:, 8 + h, :].unsqueeze(1).to_broadcast([64, NCH, CH]),
                   ALU.mult, ["qr%d" % p, "PFt"], ["Qrb%d" % p])
                yield
            else:
                krt, krn = res["KR"] if res is not None else (KRb[p], "KRb%d" % p)
                transpose_k(kr[p], "kr%d" % p, 64, [(krt, krn, pfc[:, cbk:cbk + 1], act)])
                yield
            vt, vn = res["vR"] if res is not None else (vR[p], "vR%d" % p)
            proj_v(wp, O_RV + cofs, vt, vn)
            yield
            if full and not skip:
                ps, pn = nb()
                proj_fm(ps, pn, wp, O_RG, 128)
                A(sgR[p][:], ps[:], AF.Silu, [pn], ["sgR%d" % p])
                yield

        def g_recur(S, Sname, order, KH, KHname, V, Vname, Dcol, Dname, snap, snapname, Ecol, Ename, nrow, flagcol, bank, pre=None, ci=0, phase=0):
            if phase in (0, 1):
                if pre is not None:
                    pre()
                if flagcol is not None:
                    TSC(dve, S, S, flagcol, None, ALU.mult, None, [Sname, "chn"], [Sname])
            va, van, vb, vbn, X, Xn = DS[ci]
            groups = [(0, va, van, bank), (1, vb, vbn, bank + 1)]
            if order[0] % 2 == 1:
                groups = groups[::-1]
            for (par_, dv, dvn, bk) in (groups if phase in (0, 1) else []):
                pn = "ps%d" % bk
                for j in range(4):
                    c = 2 * j + par_
                    hp = par_ * 64
                    MM(PS[bk][0:nrow, j * 128:(j + 1) * 128], KH[hp:hp + 64, j, :], V[hp:hp + 64, j, :], True, True,
                       [KHname, Vname], [pn])
                A(dv[0:nrow], PS[bk][0:nrow, :].rearrange("p (j k) -> p j k", k=128), AF.Copy, [pn], [dvn])
                yield
            src, srcn, dst, dstn = S, Sname, X[0:nrow, :], Xn
            for n_, c in enumerate(order if phase in (0, 2) else []):
                dv, dvn = (va, van) if c % 2 == 0 else (vb, vbn)
                if snap is not None:
                    if Ecol is not None:
                        A(snap[0:nrow, c, :], src, AF.Copy, [srcn, Ename], [snapname], scale=Ecol(c))
                    else:
                        A(snap[0:nrow, c, :], src, AF.Copy, [srcn], [snapname])
                STT(dst, src, Dcol(c), dv[0:nrow, c // 2, :], ALU.mult, ALU.add, [srcn, Dname, dvn], [dstn])
                src, srcn, dst, dstn = dst, dstn, src, srcn
                yield

        ASC = list(range(NCH))
        DESC = list(range(NCH - 1, -1, -1))

        def g_first2(l, h, p, wp):
            for (col, dst, dn, fn, sc) in ((O_FF, T["aaf"], "aaf", AF.Tanh, -0.5), (O_FB, T["aab"], "aab", AF.Tanh, -0.5),
                                           (O_AQ, T["qs"], "qs", AF.Silu, 1.0), (O_AG, sgA[p][:], "sgA%d" % p, AF.Silu, 1.0),
                                           (O_RG, sgR[p][:], "sgR%d" % p, AF.Silu, 1.0)):
                ps, pn = nb()
                proj_fm(ps, pn, wp, col, 128)
                A(dst, ps[:], fn, [pn], [dn], scale=sc)
            yield

        def g_first1(l, h, wp):
            for (cofs, ts) in ((0, "b"), (S1C, "f")):
                ps, pn = nb()
                proj_fm(ps, pn, wp, O_FB + cofs, 128)
                A(T["aa" + ts], ps[:], AF.Tanh, [pn], ["aa" + ts], scale=-0.5)
            yield

        def early2(l, s, h, p, wp):
            return [[g_first2(l, h, p, wp)],
                    [g_p0(l, h, p, wp, True, skip=True), g_gate(l, h, p, wp, "f", True, skip=True), g_gate(l, h, p, wp, "b", True, skip=True)]]

        v4 = lambda t: t[:].rearrange("p (i k) -> p i k", k=128)
        RES1 = []
        for q_ in range(4):
            j_ = q_ % 2
            if q_ < 2:
                RES1.append(dict(KH=(KHb[j_], "KHb%d" % j_), v=(vA[j_][:], "vA%d" % j_), KR=(KRb[j_], "KRb%d" % j_),
                                 vR=(vR[j_][:], "vR%d" % j_), D=(scl["Db"][j_], "Db%d" % j_)))
            else:
                RES1.append(dict(KH=(KHf[j_], "KHf%d" % j_), v=(v4(sgA[j_]), "sgA%d" % j_), KR=(KRf[j_], "KRf%d" % j_),
                                 vR=(v4(sgR[j_]), "sgR%d" % j_), D=(scl["Df"][j_], "Df%d" % j_)))

        def early1(l, s, h, wp, q0):
            return [[g_first1(l, h, wp)],
                    [g_p0(l, h, 0, wp, False, 0, RES1[q0]), g_gate(l, h, 0, wp, "b", False, "b", 0, RES1[q0], skip=True),
                     g_p0(l, h + 1, 1, wp, False, S1C, RES1[q0 + 1]), g_gate(l, h + 1, 1, wp, "b", False, "f", S1C, RES1[q0 + 1], skip=True)]]

        def late1(l, s, h, q0):
            return [late1u(l, s, h, RES1[q0], 0) + late1u(l, s, h + 1, RES1[q0 + 1], 1)]

        def late1u(l, s, h, res, cj):
            SA, SAn = SbA[:, h, :], "SbA%d" % h
            SR, SRn = SbR[:, h, :], "SbR%d" % h

            def preA():
                TSC(dve, SA, SA, chn[:, s + 1:s + 2], None, ALU.mult, None, [SAn, "chn"], [SAn])
                fw.dma(qs, SBId[s, h, 0], SA, r=[SAn], w=["SBId%d_%d" % (s, h)])

            def preR():
                TSC(dve, SR, SR, chn[0:64, s + 1:s + 2], None, ALU.mult, None, [SRn, "chn"], [SRn])
                fw.dma(qs, SBId[s, h, 1, 0:64], SR, r=[SRn], w=["SBId%d_%d" % (s, h)])
            gcol = g64[0:64, l * 16 + 8 + h:l * 16 + 8 + h + 1]
            Db, Dbn = res["D"]
            return [
                g_recur(SA, SAn, DESC, res["KH"][0], res["KH"][1], res["v"][0], res["v"][1], lambda c: Db[:, c:c + 1], Dbn,
                        None, None, None, None, 128, None, 3, pre=preA, ci=cj),
                g_recur(SR, SRn, DESC, res["KR"][0], res["KR"][1], res["vR"][0], res["vR"][1], lambda c: gcol, "g64",
                        None, None, None, None, 64, None, 5, pre=preR, ci=2 + cj)]

        def g_scA(l, h, p):
            v3 = lambda t: t.rearrange("p (i k) -> p i k", k=128)
            mfb = MF.unsqueeze(1).to_broadcast([128, 4, 128])
            mbb = MB.unsqueeze(1).to_broadcast([128, 4, 128])
            psa, pna = PS[3], "ps3"
            for i in range(4):
                cs_ = slice(i * 128, (i + 1) * 128)
                MM(psa[:, cs_], Kf[p][:, cs_], Qf[p][:, cs_], True, True, ["Kf%d" % p, "Qf%d" % p], [pna])
            TSC(dve, T["m1"], psa[:], BIG, -BIG, ALU.min, ALU.max, [pna], ["m1"])
            TT(pool, v3(T["m1"]), v3(T["m1"]), mfb, ALU.mult, ["m1", "cstt"], ["m1"])
            yield
            psb, pnb = PS[4], "ps4"
            for i in range(4):
                cs_ = slice(i * 128, (i + 1) * 128)
                MM(psb[:, cs_], Kb[p][:, cs_], Qb[p][:, cs_], True, True, ["Kb%d" % p, "Qb%d" % p], [pnb])
            TSC(dve, T["m2"], psb[:], BIG, -BIG, ALU.min, ALU.max, [pnb], ["m2"])
            TT(pool, v3(T["m2"]), v3(T["m2"]), mbb, ALU.mult, ["m2", "cstt"], ["m2"])
            yield
            TT(pool, T["Ac"], T["m1"], T["m2"], ALU.add, ["m1", "m2"], ["Ac"])
            yield

        def g_outA(l, h, p):
            for i in range(4):
                cs_ = slice(i * 128, (i + 1) * 128)
                MM(PS[6][:, cs_], vA[p][:, i, :], T["Ac"][:, cs_], True, False, ["vA%d" % p, "Ac"], ["ps6"])
                for cc in range(2):
                    c = 2 * i + cc
                    ct = slice(c * CH, (c + 1) * CH)
                    MM(PS[6][:, ct], snf[:, c, :], Qf[p][:, ct], False, False, ["snf", "Qf%d" % p], ["ps6"])
                    MM(PS[6][:, ct], snb[:, c, :], Qb[p][:, ct], False, cc == 1, ["snb", "Qb%d" % p], ["ps6"])
            yield
            A(T["osq"], PS[6][:], AF.Square, ["ps6"], ["osq"])
            fw.op(dve, lambda: nc.vector.tensor_copy(out=T["o"], in_=PS[6][:]), ["ps6", "osq"], ["o"])
            yield
            MM(PS[4][:], ones[:], T["osq"], True, True, ["ones", "osq"], ["ps4"])
            A(T["r1"], PS[4][:], AF.Ln, ["ps4"], ["r1"], scale=1.0 / 128.0, bias=EPS_A)
            yield
            A(T["r2"], T["r1"], AF.Exp, ["r1"], ["r2"], scale=-0.5)
            yield
            STT(T["o"], T["o"], anw[:, l:l + 1], T["r2"], ALU.mult, ALU.mult, ["o", "anw", "r2"], ["o"])
            yield
            TT(dve, Y[:, h, :], T["o"], sgA[p][:], ALU.mult, ["o", "sgA%d" % p], ["Y%d" % h])
            yield

        def g_scR(l, h, p):
            v3 = lambda t: t.rearrange("p (i k) -> p i k", k=128)
            psr, pnr = PS[5], "ps5"
            for i in range(4):
                cs_ = slice(i * 128, (i + 1) * 128)
                MM(psr[:, cs_], kr[p][:, cs_], qr[p][:, cs_], True, True, ["kr%d" % p, "qr%d" % p], [pnr])
            TT(dve, v3(T["AcR"]), psr[:].rearrange("p (i k) -> p i k", k=128),
               DMt[:, h, :].unsqueeze(1).to_broadcast([128, 4, 128]), ALU.mult, [pnr, "DMt"], ["AcR"])
            yield

        def g_outR(l, h, p):
            for i in range(4):
                cs_ = slice(i * 128, (i + 1) * 128)
                MM(PS[5][:, cs_], vR[p][:, i, :], T["AcR"][:, cs_], True, False, ["vR%d" % p, "AcR"], ["ps5"])
                for cc in range(2):
                    c = 2 * i + cc
                    ct = slice(c * CH, (c + 1) * CH)
                    MM(PS[5][:, ct], srf[0:64, c, :], Qrf[p][:, ct], False, False, ["srf", "Qrf%d" % p], ["ps5"])
                    MM(PS[5][:, ct], srb[0:64, c, :], Qrb[p][:, ct], False, cc == 1, ["srb", "Qrb%d" % p], ["ps5"])
            yield
            A(T["osqR"], PS[5][:], AF.Square, ["ps5"], ["osqR"])
            fw.op(dve, lambda: nc.vector.tensor_copy(out=T["oR"], in_=PS[5][:]), ["ps5", "osqR"], ["oR"])
            yield
            MM(PS[3][:], ones[:], T["osqR"], True, True, ["ones", "osqR"], ["ps3"])
            psm, pnm = nb()
            MM(psm[:], ones[:], T["oR"], True, True, ["ones", "oR"], [pnm])
            A(T["m1R"], psm[:], AF.Copy, [pnm], ["m1R"], scale=1.0 / 128.0)
            TT(pool, T["r2R"], T["m1R"], T["m1R"], ALU.mult, ["m1R"], ["r2R"])
            yield
            STT(T["r1R"], PS[3][:], 1.0 / 128.0, T["r2R"], ALU.mult, ALU.subtract, ["ps3", "r2R"], ["r1R"])
            A(T["r1R"], T["r1R"], AF.Ln, ["r1R"], ["r1R"], bias=EPS_R)
            yield
            A(T["r2R"], T["r1R"], AF.Exp, ["r1R"], ["r2R"], scale=-0.5)
            TT(dve, T["oR"], T["oR"], T["m1R"], ALU.subtract, ["oR", "m1R"], ["oR"])
            yield
            TT(dve, T["oR"], T["oR"], T["r2R"], ALU.mult, ["oR", "r2R"], ["oR"])
            yield
            TT(dve, Y[:, 8 + h, :], T["oR"], sgR[p][:], ALU.mult, ["oR", "sgR%d" % p], ["Y%d" % (8 + h)])
            yield

        def late2(l, s, h, p):
            cf = l * 16 + h
            cbk = l * 16 + 8 + h
            Df, Ef, Db, Eb = scl["Df"][p], scl["Ef"][p], scl["Db"][p], scl["Eb"][p]
            gf = g64[0:64, cf:cf + 1]
            gb = g64[0:64, cbk:cbk + 1]

            def preAb():
                fw.dma(qs, SbA[:, h, :], SBId[s, h, 0], r=["SBId%d_%d" % (s, h)], w=["SbA%d" % h])

            def preRb():
                fw.dma(qs, SbR[:, h, :], SBId[s, h, 1, 0:64], r=["SBId%d_%d" % (s, h)], w=["SbR%d" % h])
            def recs(phase):
              return [
                g_recur(SfA[:, h, :], "SfA%d" % h, ASC, KHf[p], "KHf%d" % p, vA[p], "vA%d" % p, lambda c: Df[:, c:c + 1], "Df%d" % p,
                        snf, "snf", lambda c: Ef[:, c:c + 1], "Ef%d" % p, 128, chn[:, s:s + 1], 3, ci=0, phase=phase),
                g_recur(SbA[:, h, :], "SbA%d" % h, DESC, KHb[p], "KHb%d" % p, vA[p], "vA%d" % p, lambda c: Db[:, c:c + 1], "Db%d" % p,
                        snb, "snb", lambda c: Eb[:, c:c + 1], "Eb%d" % p, 128, None, 3, pre=preAb, ci=1, phase=phase),
                g_recur(SfR[:, h, :], "SfR%d" % h, ASC, KRf[p], "KRf%d" % p, vR[p], "vR%d" % p, lambda c: gf, "g64",
                        srf, "srf", None, None, 64, chn[0:64, s:s + 1], 5, ci=2, phase=phase),
                g_recur(SbR[:, h, :], "SbR%d" % h, DESC, KRb[p], "KRb%d" % p, vR[p], "vR%d" % p, lambda c: gb, "g64",
                        srb, "srb", None, None, 64, None, 5, pre=preRb, ci=3, phase=phase)]
            return [recs(1), recs(2) + [g_scA(l, h, p), g_scR(l, h, p)], [g_outA(l, h, p), g_outR(l, h, p)]]

        def tail(l, s):
            last = (l == depth - 1)
            yn = ["Y%d" % i for i in range(16)]
            for hf in range(2):
                fw.dma(qs, WO[:, hf], WOd[l, hf], r=["WOd"], w=["WO"])
            fw.dma(qs, g1r, g1pd[l, s].partition_broadcast(128), r=["g1pd"], w=["g1r"])
            fw.dma(qs, lngr, ln_g[l].partition_broadcast(128), r=["ln_g"], w=["lngr"])
            fw.dma(qs, lnbr, ln_b[l].partition_broadcast(128), r=["ln_b"], w=["lnbr"])
            fw.dma(qs, TWb[0], TWd[l, 0], r=["TWd"], w=["TW0"])
            for cb in range(8):
                TW, twn = TWb[cb % 2], "TW%d" % (cb % 2)
                if cb < 7:
                    fw.dma(qs, TWb[(cb + 1) % 2], TWd[l, cb + 1], r=["TWd"], w=["TW%d" % ((cb + 1) % 2)])
                for (j, ps, pn, kofs) in ((0, PS[0], "ps0", 0), (1, PS[1], "ps1", 8)):
                    for kc in range(8):
                        MM(ps[:], TW[:, kc, j, :], Y[:, kofs + kc, :], kc == 0, kc == 7, [twn] + yn, [pn])
                for (j, ps, pn) in ((2, PS[4], "ps4"), (3, PS[5], "ps5")):
                    for kc in range(8):
                        MM(ps[:], TW[:, kc, j, :], uT[:, kc, :], kc == 0, kc == 7, [twn, "uT"], [pn])
                A(T["t_r1"], PS[4][:], AF.Tanh, ["ps4", "hbm"], ["t_r1"], scale=0.5, bias=hbm[:, l, cb:cb + 1])
                A(T["t_r2"], PS[5][:], AF.Tanh, ["ps5", "hbm"], ["t_r2"], scale=0.5, bias=hbm[:, l, 8 + cb:8 + cb + 1])
                STT(T["t_x1"], T["t_r1"], 1.0, PS[0][:], ALU.add, ALU.mult, ["t_r1", "ps0"], ["t_x1"])
                STT(T["t_x2"], T["t_r2"], 1.0, PS[1][:], ALU.add, ALU.mult, ["t_r2", "ps1"], ["t_x2"])
                TT(pool, MT[:, cb, :], T["t_x1"], T["t_x2"], ALU.add, ["t_x1", "t_x2"], ["MT"])
            xsrc = x_tok if l == 0 else x1d
            for i in range(4):
                t0 = s * TS + i * 128
                fw.dma(qs, hh_, xsrc[t0:t0 + 128, :], r=["x1d" if l else "x_tok"], w=["hh"])
                for hf in range(2):
                    ps, pn = (PS[6], "ps6") if hf == 0 else (PS[2], "ps2")
                    for kc in range(8):
                        MM(ps[:], MT[:, kc, i * 128:(i + 1) * 128], WO[:, hf, kc, :], kc == 0, kc == 7, ["MT", "WO"], [pn])
                    hs = slice(hf * 512, (hf + 1) * 512)
                    TT(dve, T["t_o"], ps[:], g1r[:, hs], ALU.mult, [pn, "g1r"], ["t_o"])
                    STT(hh_[:, hs], hh_[:, hs], ALPHA, T["t_o"], ALU.mult, ALU.add, ["hh", "t_o"], ["hh"])
                A(zz, hh_, AF.Identity, ["hh"], ["zz", "st4a"], accum_out=st4[:, 0:1])
                A(zz, hh_, AF.Square, ["hh"], ["zz", "st4b"], accum_out=st4[:, 1:2])
                TSC(dve, st4[:, 2:3], st4[:, 0:1], 1.0 / D, None, ALU.mult, None, ["st4a"], ["st4c"])
                TT(dve, st4[:, 3:4], st4[:, 2:3], st4[:, 2:3], ALU.mult, ["st4c"], ["st4d"])
                STT(st4[:, 4:5], st4[:, 1:2], 1.0 / D, st4[:, 3:4], ALU.mult, ALU.subtract, ["st4b", "st4d"], ["st4e"])
                A(st4[:, 4:5], st4[:, 4:5], AF.Ln, ["st4e"], ["st4e"], bias=1e-5)
                A(st4[:, 5:6], st4[:, 4:5], AF.Exp, ["st4e"], ["st4f"], scale=-0.5)
                STT(st4[:, 6:7], st4[:, 2:3], -1.0, st4[:, 5:6], ALU.mult, ALU.mult, ["st4c", "st4f"], ["st4g"])
                A(zz, hh_, AF.Identity, ["hh", "st4f", "st4g"], ["zz"], scale=st4[:, 5:6], bias=st4[:, 6:7])
                TT(dve, zz, zz, lngr, ALU.mult, ["zz", "lngr"], ["zz"])
                TT(pool, zz, zz, lnbr, ALU.add, ["zz", "lnbr"], ["zz"])
                if last:
                    fw.dma(qs, y_out[t0:t0 + 128, :], zz, r=["zz"], w=["y"])
                else:
                    fw.dma(qs, x1d[t0:t0 + 128, :], zz, r=["zz"], w=["x1d"])
                    for g in range(2):
                        ps, pn = (PS[4], "ps4") if g == 0 else (PS[5], "ps5")
                        for kk_ in range(4):
                            kc = g * 4 + kk_
                            TR(ps[:, kk_ * 128:(kk_ + 1) * 128], zz[:, kc * 128:(kc + 1) * 128], IDF, ["zz", "cstt"], [pn])
                        for kk_ in range(4):
                            kc = g * 4 + kk_
                            A(u1t[:, kc, :], ps[:, kk_ * 128:(kk_ + 1) * 128], AF.Identity, [pn, "adaT"], ["u1t"],
                              scale=adaT[:, l + 1, 8 + kc, s:s + 1], bias=adaT[:, l + 1, kc, s:s + 1])
                    fw.dma(qs, uTd[l + 1][:, t0:t0 + 128].rearrange("(kc p) t -> p kc t", p=128), u1t[:], r=["u1t"], w=["uTd%d" % (l + 1)])

        for l in range(depth):
            layer_consts(l)
            for t_, nm in ((SfA, "SfA"), (SbA, "SbA"), (SfR, "SfR"), (SbR, "SbR")):
                fw.op(pool, lambda: nc.gpsimd.memset(t_[:], 0.0), [], [nm + str(h) for h in range(8)])
            units = [(s, h) for s in range(NS - 1, -1, -1) for h in (0, 2, 4, 6)]
            prev = None

            def ld1(wp_, h_):
                for j in range(2):
                    fw.dma(qs, WH[wp_][:, :, j * S1C:(j + 1) * S1C], WHd[l, h_ + j][:, :, 0:S1C], r=["WHd"], w=["WH%d" % wp_])
            for ui, (s, h) in enumerate(units):
                wp = ui % 2
                if ui == 0:
                    ld1(wp, h)
                if h == 0:
                    load_uT(l, s)
                    load_rope(s)
                if ui + 1 < len(units):
                    ld1(1 - wp, units[ui + 1][1])
                q0 = 2 * (ui % 2)
                sched([early1(l, s, h, wp, q0), prev])
                prev = late1(l, s, h, q0)
            sched([prev])
            stage(5)
            units = [(s, h) for s in range(NS) for h in range(8)]
            prev = None
            for ui, (s, h) in enumerate(units):
                p = wp = ui % 2
                if ui == 0:
                    fw.dma(qs, WH[wp][:], WHd[l, h], r=["WHd"], w=["WH%d" % wp])
                if h == 0:
                    load_uT(l, s)
                    load_rope(s)
                if ui + 1 < len(units):
                    hn = units[ui + 1][1]
                    fw.dma(qs, WH[1 - wp][:], WHd[l, hn], r=["WHd"], w=["WH%d" % (1 - wp)])
                sched([early2(l, s, h, p, wp), prev])
                prev = late2(l, s, h, p)
                if h == 7:
                    sched([prev])
                    prev = None
                    stage(6)
                    tail(l, s)
        fw.finish(["y"])
        build.stats = (fw.n_inst, fw.n_wait)
    return nc


def _consts():
    s = np.arange(128)[:, None]
    t = np.arange(128)[None, :]
    same = (s // 64) == (t // 64)
    MF = (same & (s <= t)).astype(np.float32)
    MB = (same & (s >= t)).astype(np.float32)
    DF = np.where(same & (s <= t), t - s, 0).astype(np.float32)
    DB = np.where(same & (s >= t), s - t, 0).astype(np.float32)
    TP = np.zeros((128, 128), np.float32)
    TP[:, 0:64] = np.arange(64)[None, :] + 1
    TP[:, 64:128] = 64 - np.arange(64)[None, :]
    CF = np.zeros((128, 2), np.float32)
    CF[:, 0] = 63 - (np.arange(128) % 64)
    CF[:, 1] = np.arange(128) % 64
    return np.concatenate([MF, MB, DF, DB, TP, CF, np.eye(128, dtype=np.float32)], axis=1)


def _rope_tables(pos0):
    inv = (10000.0 ** (-np.arange(0, 64, 2, dtype=np.float32) / 64)).astype(np.float32)
    pos = (pos0 + np.arange(TS)).astype(np.float32)
    ang = (pos[None, :] * inv[:, None]).astype(np.float32)
    c = np.cos(ang).astype(np.float32)
    s_ = np.sin(ang).astype(np.float32)
    return np.concatenate([c, c], 0), np.concatenate([-s_, s_], 0)


def make_core_inputs(seqs, NS, weights):
    NT = NS * TS
    x_tok = np.zeros((NT, D), np.float32)
    cT = np.zeros((D, NS), np.float32)
    chain = np.zeros((1, NS + 1), np.float32)
    rC = np.zeros((NS, 64, TS), np.float32)
    rS = np.zeros((NS, 64, TS), np.float32)
    s = 0
    for (x, c) in seqs:
        n = x.shape[0] // TS
        x_tok[s * TS:(s + n) * TS] = x
        for j in range(n):
            cT[:, s + j] = c
            if j > 0:
                chain[0, s + j] = 1.0
            rC[s + j], rS[s + j] = _rope_tables(j * TS)
        s += n
    for j in range(s, NS):
        rC[j], rS[j] = _rope_tables(0)
    m = dict(weights)
    m.update(x_tok=x_tok, xT=np.ascontiguousarray(x_tok.T), cT=cT, chain=chain, ropeC=rC, ropeS=rS, cst=_consts())
    return m


def kernel(x_prompt, x_sample, c_prompt, c_sample, w_ada, b_ada, w_in, hgrn_lb, a_norm_w, ret_decay,
           w_pa, w_pb, w_mg, b_mg, w_out, ln_g, ln_b):
    f = lambda a: np.ascontiguousarray(np.asarray(a, dtype=np.float32))
    weights = dict(w_ada=f(w_ada), b_ada=f(b_ada), w_in=f(w_in), hgrn_lb=f(hgrn_lb), a_norm_w=f(a_norm_w),
                   ret_decay=f(ret_decay).reshape(1, 32), w_pa=f(w_pa), w_pb=f(w_pb), w_mg=f(w_mg), b_mg=f(b_mg),
                   w_out=f(w_out), ln_g=f(ln_g), ln_b=f(ln_b))
    x_prompt, x_sample, c_prompt, c_sample = f(x_prompt), f(x_sample), f(c_prompt), f(c_sample)
    NS = 24
    assign = [[("s", 0), ("p", 0)], [("s", 1), ("p", 1)], [("p", 2), ("p", 3), ("p", 4)], [("p", 5), ("p", 6), ("p", 7)],
              [("p", 8), ("p", 9)], [("p", 10), ("p", 11)], [("p", 12), ("p", 13)], [("p", 14), ("p", 15)]]
    in_maps = []
    for core in assign:
        seqs = [((x_sample[i], c_sample[i]) if k == "s" else (x_prompt[i], c_prompt[i])) for (k, i) in core]
        in_maps.append(make_core_inputs(seqs, NS, weights))
    nc = build(NS)
    res = run_bass_kernel_spmd(nc, in_maps, core_ids=list(range(8)))
    y_prompt = np.zeros_like(x_prompt)
    y_sample = np.zeros_like(x_sample)
    for ci, core in enumerate(assign):
        yc = res.results[ci]["y"]
        s = 0
        for (k, i) in core:
            if k == "s":
                n = x_sample.shape[1]
                y_sample[i] = yc[s:s + n]
            else:
                n = x_prompt.shape[1]
                y_prompt[i] = yc[s:s + n]
            s += n
    return (y_prompt, y_sample)
```

```python
import contextlib
import numpy as np
import concourse.bass as bass
import concourse.mybir as mybir
from concourse.bass_utils import run_bass_kernel_spmd

F32 = mybir.dt.float32
BF16 = mybir.dt.bfloat16
ALU = mybir.AluOpType
AF = mybir.ActivationFunctionType

D = 1024
TS = 512
CH = 64
NCH = TS // CH
WHC = 1152
O_FB, O_AI, O_RK, O_RKS, O_RV, O_AQ, O_FF, O_AG, O_RQ, O_RQS, O_RG = 0, 128, 256, 320, 384, 512, 640, 768, 896, 960, 1024
S1C = 512
IN_OFF = {"aq": 0, "ff": 1024, "fb": 2048, "ai": 3072, "ag": 4096, "rq": 5120, "rk": 5632, "rv": 6144, "rg": 7168}
EPS_A = 1e-5 * 128.0
EPS_R = 1e-5 * 64.0
ALPHA = 4.0 ** 0.25
BIG = 1e30


class Buf:
    __slots__ = ("name", "lw", "rd", "tw", "tr")

    def __init__(self, name):
        self.name = name
        self.lw = None
        self.rd = []
        self.tw = 0.0
        self.tr = 0.0


class Eng:
    def __init__(self, fw, name, handle, is_pe=False):
        self.name = name
        self.h = handle
        self.is_pe = is_pe
        self.sem = fw.new_sem("s_" + name)
        self.count = 0
        self.waited = {}


class DmaQ:
    def __init__(self, fw, name, handle, nslots):
        self.name = name
        self.h = handle
        self.sems = [fw.new_sem("d_%s%d" % (name, i)) for i in range(nslots)]
        self.uses = [0] * nslots
        self.idx = 0
        self.waited = {}
        self.is_pe = False


class FW:
    def __init__(self, nc, stack):
        self.nc = nc
        self.stack = stack
        self.bufs = {}
        self.pe = Eng(self, "pe", nc.tensor, is_pe=True)
        self.act = Eng(self, "act", nc.scalar)
        self.dve = Eng(self, "dve", nc.vector)
        self.pool = Eng(self, "pool", nc.gpsimd)
        self.q_sync = DmaQ(self, "sy", nc.sync, 32)
        self.q_pool = DmaQ(self, "gp", nc.gpsimd, 8)
        self.q_pool.waited = self.pool.waited
        self.n_inst = 0
        self.n_wait = 0
        self.alias = {}
        self.t_eng = {}
        self.last_finish = 0.0

    def _vt(self, eng, reads, writes, dur, occupy=True):
        st = self.t_eng.get(id(eng), 0.0)
        for b in reads:
            if b.tw > st:
                st = b.tw
        for b in writes:
            if b.tw > st:
                st = b.tw
            if b.tr > st:
                st = b.tr
        fin = st + dur
        if occupy:
            self.t_eng[id(eng)] = fin
        else:
            self.t_eng[id(eng)] = st + 0.1
        for b in reads:
            if fin > b.tr:
                b.tr = fin
        for b in writes:
            b.tw = fin + 2.0
        self.last_finish = fin

    def new_sem(self, name):
        return self.stack.enter_context(self.nc.semaphore(name))

    def _bl(self, xs):
        out = []
        for x in xs:
            for nm in self.alias.get(x, (x,)):
                b = self.bufs.get(nm)
                if b is None:
                    b = Buf(nm)
                    self.bufs[nm] = b
                out.append(b)
        return out

    def _wait(self, eng, sem, val):
        key = id(sem)
        if eng.waited.get(key, 0) >= val:
            return
        eng.h.wait_ge(sem, val)
        eng.waited[key] = val
        self.n_wait += 1

    def _deps(self, reads, writes):
        deps = {}
        for b in reads:
            if b.lw is not None:
                deps[(id(b.lw[0]), b.lw[1])] = b.lw
        for b in writes:
            if b.lw is not None:
                deps[(id(b.lw[0]), b.lw[1])] = b.lw
            for r in b.rd:
                deps[(id(r[0]), r[1])] = r
        return list(deps.values())

    def _mark(self, tok, reads, writes):
        for b in reads:
            b.rd.append(tok)
            if len(b.rd) > 12:
                last = {}
                for t in b.rd:
                    k = id(t[0])
                    if k not in last or last[k][1] < t[1]:
                        last[k] = t
                b.rd = list(last.values())
        for b in writes:
            b.lw = tok
            b.rd = []

    def op(self, eng, fn, r=(), w=(), cost=0.6):
        reads = self._bl(r)
        writes = self._bl(w)
        self._vt(eng, reads, writes, cost)
        for (sem, val, src) in self._deps(reads, writes):
            if src is eng and eng.is_pe:
                continue
            self._wait(eng, sem, val)
        inst = fn()
        eng.count += 1
        inst.then_inc(eng.sem, 1)
        self._mark((eng.sem, eng.count, eng), reads, writes)
        self.n_inst += 1
        return inst

    def dma(self, q, out, in_, r=(), w=(), **kw):
        reads = self._bl(r)
        writes = self._bl(w)
        self._vt(q, reads, writes, 4.0, occupy=False)
        for (sem, val, src) in self._deps(reads, writes):
            self._wait(q, sem, val)
        j = q.idx % len(q.sems)
        q.idx += 1
        sem = q.sems[j]
        if q.uses[j] > 0:
            self._wait(q, sem, 16 * q.uses[j])
        q.uses[j] += 1
        inst = q.h.dma_start(out=out, in_=in_, **kw)
        inst.then_inc(sem, 16)
        self._mark((sem, 16 * q.uses[j], q), reads, writes)
        self.n_inst += 1
        return inst

    def barrier(self):
        engs = [self.pe, self.act, self.dve, self.pool]
        qs = [self.q_sync, self.q_pool]
        for e in engs + [self.q_sync]:
            for o in engs:
                if o is not e and o.count > 0:
                    self._wait(e, o.sem, o.count)
            for q in qs:
                for j, sem in enumerate(q.sems):
                    if q.uses[j] > 0:
                        self._wait(e, sem, 16 * q.uses[j])

    def finish(self, names):
        for b in self._bl(names):
            if b.lw is not None:
                self._wait(self.q_sync, b.lw[0], b.lw[1])
            for t in b.rd:
                self._wait(self.q_sync, t[0], t[1])


class _Stop(Exception):
    pass


def build(NS, depth=2, upto=99):
    NT = NS * TS
    nc = bass.Bass("TRN2", target_bir_lowering=False)
    dt_in = lambda name, shape, dt=F32: nc.dram_tensor(name, shape, dt, kind="ExternalInput").ap()
    x_tok = dt_in("x_tok", [NT, D])
    xT = dt_in("xT", [D, NT])
    cT = dt_in("cT", [D, NS])
    chain = dt_in("chain", [1, NS + 1])
    ropeC = dt_in("ropeC", [NS, 64, TS])
    ropeS = dt_in("ropeS", [NS, 64, TS])
    w_ada = dt_in("w_ada", [2, D, 3 * D])
    b_ada = dt_in("b_ada", [2, 3 * D])
    w_in = dt_in("w_in", [2, D, 8192])
    hgrn_lb = dt_in("hgrn_lb", [2, D])
    a_norm_w = dt_in("a_norm_w", [2, 128])
    ret_decay = dt_in("ret_decay", [1, 32])
    w_pa = dt_in("w_pa", [2, D, D])
    w_pb = dt_in("w_pb", [2, D, D])
    w_mg = dt_in("w_mg", [2, D, 2 * D])
    b_mg = dt_in("b_mg", [2, 2 * D])
    w_out = dt_in("w_out", [2, D, D])
    ln_g = dt_in("ln_g", [2, D])
    ln_b = dt_in("ln_b", [2, D])
    cst = dt_in("cst", [128, 4 * 128 + 128 + 2 + 128])
    y_out = nc.dram_tensor("y", [NT, D], F32, kind="ExternalOutput").ap()

    uTd = [nc.dram_tensor("uTd%d" % l, [D, NT], BF16).ap() for l in range(2)]
    x1d = nc.dram_tensor("x1d", [NT, D], F32).ap()
    WHd = nc.dram_tensor("WHd", [2, 8, 128, 8, WHC], BF16).ap()
    TWd = nc.dram_tensor("TWd", [2, 8, 128, 8, 4, 128], BF16).ap()
    WOd = nc.dram_tensor("WOd", [2, 2, 128, 8, 512], BF16).ap()
    g1pd = nc.dram_tensor("g1pd", [2, NS, D], F32).ap()
    SBId = nc.dram_tensor("SBId", [NS, 8, 2, 128, 128], F32).ap()

    with contextlib.ExitStack() as st, contextlib.suppress(_Stop):
        fw = FW(nc, st)
        pe, act, dve, pool, qs, qp = fw.pe, fw.act, fw.dve, fw.pool, fw.q_sync, fw.q_pool
        sb = lambda name, shape, dt=F32: st.enter_context(nc.sbuf_tensor(name, shape, dt))
        psum = lambda name, dt=F32: st.enter_context(nc.psum_tensor(name, [128, 512], dt))
        PS = [psum("ps%d" % i) for i in range(7)]
        PST = psum("pst", BF16)

        def A(out, in_, func, r, w, bias=None, scale=1.0, accum_out=None):
            kw = {}
            if bias is not None:
                kw["bias"] = bias
            if accum_out is not None:
                kw["accum_out"] = accum_out
            return fw.op(act, lambda: nc.scalar.activation(out=out, in_=in_, func=func, scale=scale, **kw), r, w,
                         cost=0.22 + in_.free_size() / 1200.0)

        def TT(eng, out, in0, in1, op, r, w):
            h = nc.vector if eng is dve else nc.gpsimd
            return fw.op(eng, lambda: h.tensor_tensor(out=out, in0=in0, in1=in1, op=op), r, w,
                         cost=(0.1 + in0.free_size() / 960.0) if eng is dve else (0.2 + in0.free_size() / 450.0))

        def TSC(eng, out, in0, s1, s2, op0, op1, r, w):
            h = nc.vector if eng is dve else nc.gpsimd
            if s2 is None:
                return fw.op(eng, lambda: h.tensor_scalar(out=out, in0=in0, scalar1=s1, scalar2=None, op0=op0), r, w,
                             cost=0.1 + in0.free_size() / 960.0)
            return fw.op(eng, lambda: h.tensor_scalar(out=out, in0=in0, scalar1=s1, scalar2=s2, op0=op0, op1=op1), r, w,
                         cost=0.1 + in0.free_size() / 960.0)

        def STT(out, in0, scalar, in1, op0, op1, r, w):
            return fw.op(dve, lambda: nc.vector.scalar_tensor_tensor(out=out, in0=in0, scalar=scalar, in1=in1, op0=op0, op1=op1), r, w,
                         cost=0.1 + in0.free_size() / 960.0)

        def MM(out, lhsT, rhs, start, stop, r, w):
            return fw.op(pe, lambda: nc.tensor.matmul(out, lhsT=lhsT, rhs=rhs, start=start, stop=stop), r, w,
                         cost=(0.07 + rhs.free_size() / 1950.0) * (4.0 if rhs.dtype == F32 else 1.0))

        def TR(out, in_, ident, r, w):
            return fw.op(pe, lambda: nc.tensor.transpose(out, in_, ident), r, w, cost=0.12)

        cstt = sb("cstt", [128, 4 * 128 + 128 + 2 + 128])
        fw.dma(qs, cstt[:], cst, r=["cst"], w=["cstt"])
        MF = cstt[:, 0:128]
        MB = cstt[:, 128:256]
        DF = cstt[:, 256:384]
        DB = cstt[:, 384:512]
        TP1 = cstt[0:64, 512:576]
        TPB = cstt[0:64, 576:640]
        CF = cstt[:, 640:641]
        CB = cstt[:, 641:642]
        IDF = cstt[:, 642:770]
        idb = sb("idb", [128, 128], BF16)
        fw.op(dve, lambda: nc.vector.tensor_copy(out=idb[:], in_=IDF), ["cstt"], ["idb"])
        ones = sb("ones", [128, 128])
        fw.op(pool, lambda: nc.gpsimd.memset(ones[:], 1.0), [], ["ones"])
        smask = sb("smask", [128, TS])
        fw.op(pool, lambda: nc.gpsimd.memset(smask[:], 1.0), [], ["smask"])
        smv = smask[:].rearrange("p (c t) -> p c t", t=CH)
        fw.op(pool, lambda: nc.gpsimd.memset(smv[:, :, 0:1], 0.0), [], ["smask"])
        chn = sb("chn", [128, NS + 1])
        fw.dma(qs, chn[:], chain[0].partition_broadcast(128), r=["chain"], w=["chn"])

        with nc.allow_non_contiguous_dma(reason="weight relayout"):
            for l in range(2):
                WHv = WHd[l].rearrange("h p kc c -> h p kc c")
                segs = [("aq", O_AQ, 128), ("ff", O_FF, 128), ("fb", O_FB, 128), ("ai", O_AI, 128), ("ag", O_AG, 128),
                        ("rq", O_RQ, 64), ("rk", O_RK, 64), ("rv", O_RV, 128), ("rg", O_RG, 128)]
                for (nm, do, wd) in segs:
                    src = w_in[l][:, IN_OFF[nm]:IN_OFF[nm] + 8 * wd].rearrange("(kc p) (h w) -> h p kc w", p=128, w=wd)
                    for hh in range(8):
                        fw.dma(qp, WHd[l, hh][:, :, do:do + wd], src[hh], r=["w_in"], w=["WHd"])
                for (nm, do) in (("rq", O_RQS), ("rk", O_RKS)):
                    src = w_in[l][:, IN_OFF[nm]:IN_OFF[nm] + 512].rearrange("(kc p) (h w) -> h p kc w", p=128, w=64)
                    for hh in range(8):
                        fw.dma(qp, WHd[l, hh][:, :, do:do + 32], src[hh][:, :, 32:64], r=["w_in"], w=["WHd"])
                        fw.dma(qp, WHd[l, hh][:, :, do + 32:do + 64], src[hh][:, :, 0:32], r=["w_in"], w=["WHd"])
                for j, wsrc in enumerate((w_pa[l], w_pb[l], w_mg[l][:, 0:D], w_mg[l][:, D:2 * D])):
                    src = wsrc.rearrange("(kc p) (cb c) -> cb p kc c", p=128, c=128)
                    for cb in range(8):
                        fw.dma(qp, TWd[l, cb][:, :, j, :], src[cb], r=["w_t"], w=["TWd"])
                src = w_out[l].rearrange("(kc p) (hf c) -> hf p kc c", p=128, c=512)
                for hf in range(2):
                    fw.dma(qp, WOd[l, hf], src[hf], r=["w_t"], w=["WOd"])

        prohold = []

        def stage(n):
            if upto <= n:
                fw.barrier()
                for p_ in prohold:
                    p_.close()
                raise _Stop()
        stage(1)
        lbt = sb("lbt", [128, 2, 8])
        c1t = sb("c1t", [128, 2, 8])
        anw = sb("anw", [128, 2])
        hbm = sb("hbm", [128, 2, 16])
        with nc.allow_non_contiguous_dma(reason="tiny param relayout"):
            fw.dma(qs, lbt[:], hgrn_lb.rearrange("l (h k) -> k l h", k=128), r=["hgrn_lb"], w=["lbt"])
            fw.dma(qs, anw[:], a_norm_w.rearrange("l k -> k l"), r=["a_norm_w"], w=["anw"])
            fw.dma(qs, hbm[:], b_mg.rearrange("l (c p) -> p l c", p=128), r=["b_mg"], w=["hbm"])
        tl = sb("tl", [128, 8])
        TT(dve, tl[:], lbt[:, 0, :], lbt[:, 1, :], ALU.subtract, ["lbt"], ["tl"])
        A(tl[:], tl[:], AF.Exp, ["tl"], ["tl"])
        TSC(dve, tl[:], tl[:], 1.0, None, ALU.add, None, ["tl"], ["tl"])
        fw.op(dve, lambda: nc.vector.reciprocal(out=tl[:], in_=tl[:]), ["tl"], ["tl"])
        fw.op(pool, lambda: nc.gpsimd.memset(c1t[:, 0, :], 0.5), [], ["c1t"])
        TSC(dve, c1t[:, 1, :], tl[:], -0.5, 0.5, ALU.mult, ALU.add, ["tl", "c1t"], ["c1t"])
        TSC(dve, hbm[:], hbm[:], 0.5, None, ALU.mult, None, ["hbm"], ["hbm"])
        rdt = sb("rdt", [128, 32])
        fw.dma(qs, rdt[:], ret_decay[0].partition_broadcast(128), r=["ret_decay"], w=["rdt"])
        lg = sb("lg", [128, 32])
        A(lg[:], rdt[:], AF.Exp, ["rdt"], ["lg"], scale=-1.0)
        A(lg[:], lg[:], AF.Ln, ["lg"], ["lg"], bias=1.0)
        TSC(dve, lg[:], lg[:], -1.0, None, ALU.mult, None, ["lg"], ["lg"])
        g64 = sb("g64", [128, 32])
        A(g64[:], lg[:], AF.Exp, ["lg"], ["g64"], scale=64.0)
        DMt = sb("DMt", [128, 8, 128])
        PFt = sb("PFt", [64, 16, 64])
        pfc = sb("pfc", [128, 32])
        tm1 = sb("tm1", [128, 128])
        tm2 = sb("tm2", [128, 128])
        uT = sb("uT", [128, 8, TS], BF16)
        adaT = sb("adaT", [128, 2, 16, NS])
        pro = contextlib.ExitStack()
        prohold.append(pro)
        sbp = lambda name, shape, dt=F32: pro.enter_context(nc.sbuf_tensor(name, shape, dt))
        for l in range(2):
            for h in range(8):
                cf = l * 16 + h
                cbk = l * 16 + 8 + h
                A(pfc[:, cf:cf + 1], CF, AF.Exp, ["cstt", "lg"], ["pfc"], scale=lg[:, cf:cf + 1])
                A(pfc[:, cbk:cbk + 1], CB, AF.Exp, ["cstt", "lg"], ["pfc"], scale=lg[:, cbk:cbk + 1])

        stage(2)
        scT = sbp("scT", [128, 8, NS])
        fw.dma(qs, scT[:], cT.rearrange("(kc p) s -> p kc s", p=128), r=["cT"], w=["scT"])
        A(scT[:], scT[:], AF.Silu, ["scT"], ["scT"])
        badT = sbp("badT", [128, 2, 24])
        with nc.allow_non_contiguous_dma(reason="tiny param relayout"):
            fw.dma(qs, badT[:], b_ada.rearrange("l (c p) -> p l c", p=128), r=["b_ada"], w=["badT"])
        bgr = sbp("bgr", [NS, D])
        wad = sbp("wad", [128, 8, 512])
        grow = sbp("grow", [NS, 512])
        for l in range(2):
            fw.dma(qs, bgr[:], b_ada[l, 2 * D:3 * D].partition_broadcast(NS), r=["b_ada"], w=["bgr"])
            for blk in range(6):
                fw.dma(qs, wad[:], w_ada[l][:, blk * 512:(blk + 1) * 512].rearrange("(kc p) c -> p kc c", p=128),
                       r=["w_ada"], w=["wad"])
                if blk < 4:
                    for cc in range(4):
                        ch = blk * 4 + cc
                        for kc in range(8):
                            MM(PS[0][:, cc * NS:(cc + 1) * NS], wad[:, kc, cc * 128:(cc + 1) * 128], scT[:, kc, :],
                               kc == 0, kc == 7, ["wad", "scT"], ["ps0"])
                        if ch < 8:
                            TSC(dve, adaT[:, l, ch, :], PS[0][:, cc * NS:(cc + 1) * NS], badT[:, l, ch:ch + 1], None,
                                ALU.add, None, ["ps0", "badT"], ["adaT"])
                        else:
                            TSC(dve, adaT[:, l, ch, :], PS[0][:, cc * NS:(cc + 1) * NS], badT[:, l, ch:ch + 1], 1.0,
                                ALU.add, ALU.add, ["ps0", "badT"], ["adaT"])
                else:
                    for kc in range(8):
                        MM(PS[1][0:NS, :], scT[:, kc, :], wad[:, kc, :], kc == 0, kc == 7, ["wad", "scT"], ["ps1"])
                    c0 = (blk - 4) * 512
                    TT(dve, grow[:], PS[1][0:NS, :], bgr[:, c0:c0 + 512], ALU.add, ["ps1", "bgr"], ["grow"])
                    TSC(dve, grow[:], grow[:], 0.5, 0.5, ALU.mult, ALU.add, ["grow"], ["grow"])
                    fw.dma(qs, g1pd[l][:, c0:c0 + 512], grow[:], r=["grow"], w=["g1pd"])

        stage(3)
        xTt = sbp("xTt", [128, 8, TS])
        for s in range(NS):
            fw.dma(qs, xTt[:], xT[:, s * TS:(s + 1) * TS].rearrange("(kc p) t -> p kc t", p=128), r=["xT"], w=["xTt"])
            for kc in range(8):
                A(uT[:, kc, :], xTt[:, kc, :], AF.Identity, ["xTt", "adaT"], ["uT"],
                  scale=adaT[:, 0, 8 + kc, s:s + 1], bias=adaT[:, 0, kc, s:s + 1])
            fw.dma(qs, uTd[0][:, s * TS:(s + 1) * TS].rearrange("(kc p) t -> p kc t", p=128), uT[:], r=["uT"], w=["uTd0"])

        stage(4)
        fw.barrier()
        pro.close()
        prohold.clear()
        NREG = 32
        arena = sb("arena", [128, NREG * TS])
        WH = [sb("WH%d" % i, [128, 8, WHC], BF16) for i in range(2)]
        lfe = {d: sb("lfe" + d, [128, TS + 1]) for d in "fb"}
        for d in "fb":
            fw.op(pool, lambda: nc.gpsimd.memset(lfe[d][:, 0:1], 0.0), [], ["lfe" + d])
        regs = {}

        def view(name, r0, nreg=1, dt=F32, shape=None, half=0):
            ap = arena[:, r0 * TS:(r0 + nreg) * TS]
            if dt is BF16:
                ap = ap.bitcast(BF16)
                if nreg == 1 and shape is None:
                    ap = ap[:, half * TS:(half + 1) * TS]
            if shape is not None:
                ap = ap.rearrange(shape[0], **shape[1])
            regs[name] = ap
            fw.alias[name] = tuple("R%d" % r for r in range(r0, r0 + nreg))
            return ap

        T = {}
        for i, d in enumerate("fb"):
            b0 = i * 6
            T["aa" + d] = view("aa" + d, b0 + 0)
            T["kk" + d] = view("kk" + d, b0 + 1)
            T["cs" + d] = view("cs" + d, b0 + 2)
            T["ep" + d] = view("ep" + d, b0 + 3)
            T["em" + d] = view("em" + d, b0 + 4)
            T["kh" + d] = view("kh" + d, b0 + 5, dt=BF16)
        T["qs"] = view("qs", 12)
        T["x1"] = view("x1", 13)
        T["x2"] = view("x2", 14)
        T["o"] = view("o", 15)
        T["osq"] = view("osq", 16)
        T["r1"] = view("r1", 17)
        T["r2"] = view("r2", 18)
        T["m1"] = view("m1", 19)
        T["m2"] = view("m2", 20)
        T["Ac"] = view("Ac", 21, dt=BF16)
        s3 = ("p (c k) -> p c k", dict(k=128))
        snf = view("snf", 22, dt=BF16, shape=s3)
        snb = view("snb", 23, dt=BF16, shape=s3)
        srf = view("srf", 24, dt=BF16, shape=s3)
        srb = view("srb", 25, dt=BF16, shape=s3)
        WO = view("WO", 0, 8, dt=BF16, shape=("p (h kc c) -> p h kc c", dict(h=2, kc=8)))
        MT = view("MT", 8, 4, dt=BF16, shape=("p (kc t) -> p kc t", dict(kc=8)))
        g1r = view("g1r", 12, 2)
        hh_ = view("hh", 14, 2)
        zz = view("zz", 16, 2)
        T["t_o"] = view("t_o", 18)
        T["t_r1"] = view("t_r1", 19)
        T["t_r2"] = view("t_r2", 20)
        T["t_x1"] = view("t_x1", 21)
        T["t_x2"] = view("t_x2", 22)

        def par(name, shape, dt=BF16):
            return [sb("%s%d" % (name, i), shape, dt) for i in range(2)]
        Qf, Qb, Kf, Kb = par("Qf", [128, TS]), par("Qb", [128, TS]), par("Kf", [128, TS]), par("Kb", [128, TS])
        KHf, KHb = par("KHf", [128, 4, 128]), par("KHb", [128, 4, 128])
        vA, sgA = par("vA", [128, 4, 128]), par("sgA", [128, TS])
        qr, kr, Qrf, Qrb = par("qr", [64, TS]), par("kr", [64, TS]), par("Qrf", [64, TS]), par("Qrb", [64, TS])
        KRf, KRb = par("KRf", [128, 4, 64]), par("KRb", [128, 4, 64])
        vR, sgR = par("vR", [128, 4, 128]), par("sgR", [128, TS])
        scl = {n: par("c_" + n, [128, NCH], F32) for n in ("Df", "Ef", "Gf", "Db", "Eb", "Gb", "t1f", "t2f", "t1b", "t2b")}
        SfA = sb("SfA", [128, 8, 128])
        SbA = sb("SbA", [128, 8, 128])
        SfR = sb("SfR", [64, 8, 128])
        SbR = sb("SbR", [64, 8, 128])
        rC = sb("rC", [64, TS])
        rS = sb("rS", [64, TS])
        Y = sb("Y", [128, 16, TS], BF16)
        TWb = [sb("TW0", [128, 8, 4, 128], BF16)[:], view("TW1", 27, 4, dt=BF16, shape=("p (kc j c) -> p kc j c", dict(kc=8, j=4)))]
        lngr = view("lngr", 23, 2)
        lnbr = view("lnbr", 25, 2)
        st4 = sb("st4", [128, 8])
        u1t = sb("u1t", [128, 8, 128], BF16)

        def run_all(gens):
            gens = [g for g in gens if g is not None]
            while gens:
                for g in list(gens):
                    try:
                        next(g)
                    except StopIteration:
                        gens.remove(g)

        def sched(parts):
            sts = []
            for p_ in parts:
                if p_:
                    sts.append({"stages": [list(x) for x in p_], "cur": {}, "t0": 0.0})
            def refill(st):
                while not st["cur"] and st["stages"]:
                    for g in st["stages"].pop(0):
                        st["cur"][g] = st["t0"]
            for st in sts:
                refill(st)
            while True:
                best = None
                for st in sts:
                    for g, c in st["cur"].items():
                        if best is None or c < best[2]:
                            best = (st, g, c)
                if best is None:
                    break
                st, g, _ = best
                try:
                    next(g)
                    st["cur"][g] = fw.last_finish
                    if fw.last_finish > st["t0"]:
                        st["t0"] = fw.last_finish
                except StopIteration:
                    del st["cur"][g]
                    refill(st)

        def inter(*gens):
            gens = [g for g in gens if g is not None]
            while gens:
                for g in list(gens):
                    try:
                        next(g)
                        yield
                    except StopIteration:
                        gens.remove(g)

        def layer_consts(l):
            for h in range(8):
                cf = l * 16 + h
                cbk = l * 16 + 8 + h
                A(tm1[:], DF, AF.Exp, ["cstt", "lg"], ["tm1"], scale=lg[:, cf:cf + 1])
                TT(dve, tm1[:], tm1[:], MF, ALU.mult, ["tm1", "cstt"], ["tm1"])
                A(tm2[:], DB, AF.Exp, ["cstt", "lg"], ["tm2"], scale=lg[:, cbk:cbk + 1])
                TT(dve, tm2[:], tm2[:], MB, ALU.mult, ["tm2", "cstt"], ["tm2"])
                TT(dve, DMt[:, h, :], tm1[:], tm2[:], ALU.add, ["tm1", "tm2"], ["DMt"])
                A(PFt[:, h, :], TP1, AF.Exp, ["cstt", "lg"], ["PFt"], scale=lg[0:64, cf:cf + 1])
                A(PFt[:, 8 + h, :], TPB, AF.Exp, ["cstt", "lg"], ["PFt"], scale=lg[0:64, cbk:cbk + 1])

        def load_uT(l, s):
            fw.dma(qs, uT[:], uTd[l][:, s * TS:(s + 1) * TS].rearrange("(kc p) t -> p kc t", p=128), r=["uTd%d" % l], w=["uT"])

        def load_rope(s):
            fw.dma(qs, rC[:], ropeC[s], r=["ropeC"], w=["rC"])
            fw.dma(qs, rS[:], ropeS[s], r=["ropeS"], w=["rS"])

        def proj_fm(ps, psname, wp, col0, ncol):
            for kc in range(8):
                MM(ps[0:ncol, :], WH[wp][:, kc, col0:col0 + ncol], uT[:, kc, :], kc == 0, kc == 7, ["WH%d" % wp, "uT"], [psname])

        def proj_tm(ps, psname, wp, col0):
            for i in range(4):
                for kc in range(8):
                    MM(ps[:, i * 128:(i + 1) * 128], uT[:, kc, i * 128:(i + 1) * 128], WH[wp][:, kc, col0:col0 + 128],
                       kc == 0, kc == 7, ["WH%d" % wp, "uT"], [psname])

        pbank = [0]

        def nb():
            b = pbank[0] % 3
            pbank[0] += 1
            return PS[b], "ps%d" % b

        T["vT"] = view("vT", 26, dt=BF16)
        T["AcR"] = view("AcR", 26, dt=BF16, half=1)
        T["oR"] = view("oR", 27)
        T["osqR"] = view("osqR", 28)
        T["r1R"] = view("r1R", 29)
        T["r2R"] = view("r2R", 30)
        T["m1R"] = view("m1R", 31)
        fw.alias["m1R"] = ("R31a", "R31b", "R31c", "R31d")
        s4 = ("p (j k) -> p j k", dict(k=128))
        DS = []
        for ci, (ra, rb) in enumerate(((15, 16), (17, 18), (27, 28), (29, 30))):
            va = view("dS%da" % ci, ra, shape=s4)
            vb = view("dS%db" % ci, rb, shape=s4)
            xv = arena[:, 31 * TS + ci * 128:31 * TS + (ci + 1) * 128]
            fw.alias["X%d" % ci] = ("R31" + "abcd"[ci],)
            DS.append((va, "dS%da" % ci, vb, "dS%db" % ci, xv, "X%d" % ci))

        def proj_v(wp, col0, dst, dname):
            ps, pn = nb()
            proj_fm(ps, pn, wp, col0, 128)
            A(T["vT"], ps[:], AF.Copy, [pn], ["vT"])
            transpose_k(T["vT"], "vT", 128, [(dst, dname, None, act)])

        def transpose_k(src_bf, srcname, nrow, outs):
            for i in range(4):
                TR(PST[:, i * nrow:(i + 1) * nrow], src_bf[0:nrow, i * 128:(i + 1) * 128], idb[0:nrow, 0:nrow], [srcname, "idb"], ["pst"])
            pv = PST[:, 0:4 * nrow].rearrange("p (i k) -> p i k", k=nrow)
            prev = []
            for (dst, dname, sc, eng) in outs:
                if eng is act:
                    if sc is None:
                        A(dst[:], pv, AF.Copy, ["pst"] + prev, [dname])
                    else:
                        A(dst[:], pv, AF.Copy, ["pst", "pfc"] + prev, [dname], scale=sc)
                else:
                    fw.op(dve, lambda: nc.vector.tensor_scalar(out=dst[:], in0=pv, scalar1=sc, scalar2=None, op0=ALU.mult),
                          ["pst", "pfc"] + prev, [dname])
                prev = [dname]

        def g_gate(l, h, p, wp, d, full, ts=None, cofs=0, res=None, skip=False):
            bwd = (d == "b")
            ts = ts or d
            aa, kk, cs, ep, em, kh = T["aa" + ts], T["kk" + ts], T["cs" + ts], T["ep" + ts], T["em" + ts], T["kh" + ts]
            lf = lfe[ts]
            n = lambda x: x + ts
            if not skip:
                ps, pn = nb()
                proj_fm(ps, pn, wp, (O_FB if bwd else O_FF) + cofs, 128)
                A(aa, ps[:], AF.Tanh, [pn], [n("aa")], scale=-0.5)
                yield
            c1 = c1t[:, l, h:h + 1]
            TSC(dve, kk, aa, c1, c1, ALU.mult, ALU.add, [n("aa"), "c1t"], [n("kk")])
            yield
            A(lf[:, 1:TS + 1], kk, AF.Ln, [n("kk")], [n("lfe")], scale=-1.0, bias=1.0)
            yield
            csv = cs.rearrange("p (c t) -> p c t", t=CH)
            aav = aa.rearrange("p (c t) -> p c t", t=CH)
            Dn, En, Gn = scl["D" + d][p], scl["E" + d][p], scl["G" + d][p]
            t1, t2 = scl["t1" + d][p], scl["t2" + d][p]
            Dnm, Enm, Gnm = "D%s%d" % (d, p), "E%s%d" % (d, p), "G%s%d" % (d, p)
            if res is not None:
                Dn, Dnm = res["D"]
            if not bwd:
                fw.op(dve, lambda: nc.vector.tensor_tensor_scan(out=cs, data0=smask[:], data1=lf[:, 1:TS + 1], initial=0.0,
                                                                op0=ALU.mult, op1=ALU.add), ["smask", n("lfe")], [n("cs")])
                yield
                A(Dn[:], csv[:, :, CH - 1], AF.Exp, [n("cs")], [Dnm])
                if full:
                    A(En[:], csv[:, :, 31], AF.Exp, [n("cs")], [Enm])
                    TT(dve, t1[:], csv[:, :, CH - 1], csv[:, :, 31], ALU.subtract, [n("cs")], [n("ct1")])
                    yield
                    A(Gn[:], t1[:], AF.Exp, [n("ct1")], [Gnm])
                    TT(dve, aav, csv, csv[:, :, 31:32].to_broadcast([128, NCH, CH]), ALU.subtract, [n("cs")], [n("aa")])
            else:
                fw.op(dve, lambda: nc.vector.tensor_tensor_scan(out=cs, data0=lf[:, 0:TS], data1=smask[:], initial=0.0,
                                                                op0=ALU.add, op1=ALU.mult), ["smask", n("lfe")], [n("cs")])
                yield
                lfv = lf[:, 1:TS + 1].rearrange("p (c t) -> p c t", t=CH)
                TT(dve, t1[:], csv[:, :, CH - 1], lfv[:, :, CH - 1], ALU.add, [n("cs"), n("lfe")], [n("ct1")])
                A(Dn[:], t1[:], AF.Exp, [n("ct1")], [Dnm])
                if full:
                    A(Gn[:], csv[:, :, 32], AF.Exp, [n("cs")], [Gnm])
                    TT(dve, t2[:], t1[:], csv[:, :, 32], ALU.subtract, [n("ct1"), n("cs")], [n("ct2")])
                    yield
                    A(En[:], t2[:], AF.Exp, [n("ct2")], [Enm])
                    TT(dve, aav, csv[:, :, 32:33].to_broadcast([128, NCH, CH]), csv, ALU.subtract, [n("cs")], [n("aa")])
            yield
            KH = (KHb if bwd else KHf)[p]
            KHn = "KH%s%d" % (d, p)
            if res is not None:
                KH, KHn = res["KH"]
            if full:
                A(ep, aa, AF.Exp, [n("aa")], [n("ep")])
                yield
                A(em, aa, AF.Exp, [n("aa")], [n("em")], scale=-1.0)
                yield
                Qd, Kd = ((Qb, Kb) if bwd else (Qf, Kf))
                Qn, Kn = "Q%s%d" % (d, p), "K%s%d" % (d, p)
                TT(pool, Qd[p][:], T["qs"], ep, ALU.mult, ["qs", n("ep")], [Qn])
                TT(dve, Kd[p][:], kk, em, ALU.mult, [n("kk"), n("em")], [Kn])
                yield
                TT(dve, kh.rearrange("p (c t) -> p c t", t=CH), Kd[p][:].rearrange("p (c t) -> p c t", t=CH),
                   Gn[:].unsqueeze(2).to_broadcast([128, NCH, CH]), ALU.mult, [Kn, Gnm], [n("kh")])
            else:
                A(ep, cs, AF.Exp, [n("cs")], [n("ep")])
                yield
                TT(dve, kh, kk, ep, ALU.mult, [n("kk"), n("ep")], [n("kh")])
            yield
            transpose_k(kh, n("kh"), 128, [(KH, KHn, None, act)])
            yield

        def g_rope(dst, dname, wp, c_plain, c_swap):
            ps, pn = nb()
            proj_fm(ps, pn, wp, c_plain, 64)
            TT(dve, T["x1"][0:64, :], ps[0:64, :], rC[:], ALU.mult, [pn, "rC"], ["x1"])
            ps, pn = nb()
            proj_fm(ps, pn, wp, c_swap, 64)
            TT(dve, T["x2"][0:64, :], ps[0:64, :], rS[:], ALU.mult, [pn, "rS"], ["x2"])
            TT(pool, dst[:], T["x1"][0:64, :], T["x2"][0:64, :], ALU.add, ["x1", "x2"], [dname])
            yield

        def g_p0(l, h, p, wp, full, cofs=0, res=None, skip=False):
            if full and not skip:
                ps, pn = nb()
                proj_fm(ps, pn, wp, O_AQ, 128)
                A(T["qs"], ps[:], AF.Silu, [pn], ["qs"])
                yield
            vt, vn = res["v"] if res is not None else (vA[p], "vA%d" % p)
            proj_v(wp, O_AI + cofs, vt, vn)
            yield
            if full and not skip:
                ps, pn = nb()
                proj_fm(ps, pn, wp, O_AG, 128)
                A(sgA[p][:], ps[:], AF.Silu, [pn], ["sgA%d" % p])
                yield
            if full:
                yield from g_rope(qr[p], "qr%d" % p, wp, O_RQ, O_RQS)
            yield from g_rope(kr[p], "kr%d" % p, wp, O_RK + cofs, O_RKS + cofs)
            cf = l * 16 + h
            cbk = l * 16 + 8 + h
            if full:
                transpose_k(kr[p], "kr%d" % p, 64, [(KRf[p], "KRf%d" % p, pfc[:, cf:cf + 1], act), (KRb[p], "KRb%d" % p, pfc[:, cbk:cbk + 1], dve)])
                yield
                qv = qr[p][:].rearrange("p (c t) -> p c t", t=CH)
                TT(dve, Qrf[p][:].rearrange("p (c t) -> p c t", t=CH), qv, PFt[:, h, :].unsqueeze(1).to_broadcast([64, NCH, CH]),
                   ALU.mult, ["qr%d" % p, "PFt"], ["Qrf%d" % p])
                yield
                TT(dve, Qrb[p][:].rearrange("p (c t) -> p c t", t=CH), qv, PFt[:, 8 + h, :].unsqueeze(1).to_broadcast([64, NCH, CH]),
                   ALU.mult, ["qr%d" % p, "PFt"], ["Qrb%d" % p])
                yield
            else:
                krt, krn = res["KR"] if res is not None else (KRb[p], "KRb%d" % p)
                transpose_k(kr[p], "kr%d" % p, 64, [(krt, krn, pfc[:, cbk:cbk + 1], act)])
                yield
            vt, vn = res["vR"] if res is not None else (vR[p], "vR%d" % p)
            proj_v(wp, O_RV + cofs, vt, vn)
            yield
            if full and not skip:
                ps, pn = nb()
                proj_fm(ps, pn, wp, O_RG, 128)
                A(sgR[p][:], ps[:], AF.Silu, [pn], ["sgR%d" % p])
                yield

        def g_recur(S, Sname, order, KH, KHname, V, Vname, Dcol, Dname, snap, snapname, Ecol, Ename, nrow, flagcol, bank, pre=None, ci=0):
            if pre is not None:
                pre()
            if flagcol is not None:
                TSC(dve, S, S, flagcol, None, ALU.mult, None, [Sname, "chn"], [Sname])
            va, van, vb, vbn, X, Xn = DS[ci]
            groups = [(0, va, van, bank), (1, vb, vbn, bank + 1)]
            if order[0] % 2 == 1:
                groups = groups[::-1]
            for (par_, dv, dvn, bk) in groups:
                pn = "ps%d" % bk
                for j in range(4):
                    c = 2 * j + par_
                    hp = par_ * 64
                    MM(PS[bk][0:nrow, j * 128:(j + 1) * 128], KH[hp:hp + 64, j, :], V[hp:hp + 64, j, :], True, True,
                       [KHname, Vname], [pn])
                A(dv[0:nrow], PS[bk][0:nrow, :].rearrange("p (j k) -> p j k", k=128), AF.Copy, [pn], [dvn])
                yield
            src, srcn, dst, dstn = S, Sname, X[0:nrow, :], Xn
            for n_, c in enumerate(order):
                dv, dvn = (va, van) if c % 2 == 0 else (vb, vbn)
                if snap is not None:
                    if Ecol is not None:
                        A(snap[0:nrow, c, :], src, AF.Copy, [srcn, Ename], [snapname], scale=Ecol(c))
                    else:
                        A(snap[0:nrow, c, :], src, AF.Copy, [srcn], [snapname])
                STT(dst, src, Dcol(c), dv[0:nrow, c // 2, :], ALU.mult, ALU.add, [srcn, Dname, dvn], [dstn])
                src, srcn, dst, dstn = dst, dstn, src, srcn
                yield

        ASC = list(range(NCH))
        DESC = list(range(NCH - 1, -1, -1))

        def g_first2(l, h, p, wp):
            for (col, dst, dn, fn, sc) in ((O_FF, T["aaf"], "aaf", AF.Tanh, -0.5), (O_FB, T["aab"], "aab", AF.Tanh, -0.5),
                                           (O_AQ, T["qs"], "qs", AF.Silu, 1.0), (O_AG, sgA[p][:], "sgA%d" % p, AF.Silu, 1.0),
                                           (O_RG, sgR[p][:], "sgR%d" % p, AF.Silu, 1.0)):
                ps, pn = nb()
                proj_fm(ps, pn, wp, col, 128)
                A(dst, ps[:], fn, [pn], [dn], scale=sc)
            yield

        def g_first1(l, h, wp):
            for (cofs, ts) in ((0, "b"), (S1C, "f")):
                ps, pn = nb()
                proj_fm(ps, pn, wp, O_FB + cofs, 128)
                A(T["aa" + ts], ps[:], AF.Tanh, [pn], ["aa" + ts], scale=-0.5)
            yield

        def early2(l, s, h, p, wp):
            return [[g_first2(l, h, p, wp)],
                    [g_p0(l, h, p, wp, True, skip=True), g_gate(l, h, p, wp, "f", True, skip=True), g_gate(l, h, p, wp, "b", True, skip=True)]]

        v4 = lambda t: t[:].rearrange("p (i k) -> p i k", k=128)
        RES1 = []
        for q_ in range(4):
            j_ = q_ % 2
            if q_ < 2:
                RES1.append(dict(KH=(KHb[j_], "KHb%d" % j_), v=(vA[j_][:], "vA%d" % j_), KR=(KRb[j_], "KRb%d" % j_),
                                 vR=(vR[j_][:], "vR%d" % j_), D=(scl["Db"][j_], "Db%d" % j_)))
            else:
                RES1.append(dict(KH=(KHf[j_], "KHf%d" % j_), v=(v4(sgA[j_]), "sgA%d" % j_), KR=(KRf[j_], "KRf%d" % j_),
                                 vR=(v4(sgR[j_]), "sgR%d" % j_), D=(scl["Df"][j_], "Df%d" % j_)))

        def early1(l, s, h, wp, q0):
            return [[g_first1(l, h, wp)],
                    [g_p0(l, h, 0, wp, False, 0, RES1[q0]), g_gate(l, h, 0, wp, "b", False, "b", 0, RES1[q0], skip=True),
                     g_p0(l, h + 1, 1, wp, False, S1C, RES1[q0 + 1]), g_gate(l, h + 1, 1, wp, "b", False, "f", S1C, RES1[q0 + 1], skip=True)]]

        def late1(l, s, h, q0):
            return [late1u(l, s, h, RES1[q0], 0) + late1u(l, s, h + 1, RES1[q0 + 1], 1)]

        def late1u(l, s, h, res, cj):
            SA, SAn = SbA[:, h, :], "SbA%d" % h
            SR, SRn = SbR[:, h, :], "SbR%d" % h

            def preA():
                TSC(dve, SA, SA, chn[:, s + 1:s + 2], None, ALU.mult, None, [SAn, "chn"], [SAn])
                fw.dma(qs, SBId[s, h, 0], SA, r=[SAn], w=["SBId%d_%d" % (s, h)])

            def preR():
                TSC(dve, SR, SR, chn[0:64, s + 1:s + 2], None, ALU.mult, None, [SRn, "chn"], [SRn])
                fw.dma(qs, SBId[s, h, 1, 0:64], SR, r=[SRn], w=["SBId%d_%d" % (s, h)])
            gcol = g64[0:64, l * 16 + 8 + h:l * 16 + 8 + h + 1]
            Db, Dbn = res["D"]
            return [
                g_recur(SA, SAn, DESC, res["KH"][0], res["KH"][1], res["v"][0], res["v"][1], lambda c: Db[:, c:c + 1], Dbn,
                        None, None, None, None, 128, None, 3, pre=preA, ci=cj),
                g_recur(SR, SRn, DESC, res["KR"][0], res["KR"][1], res["vR"][0], res["vR"][1], lambda c: gcol, "g64",
                        None, None, None, None, 64, None, 5, pre=preR, ci=2 + cj)]

        def g_scA(l, h, p):
            v3 = lambda t: t.rearrange("p (i k) -> p i k", k=128)
            mfb = MF.unsqueeze(1).to_broadcast([128, 4, 128])
            mbb = MB.unsqueeze(1).to_broadcast([128, 4, 128])
            psa, pna = nb()
            for i in range(4):
                cs_ = slice(i * 128, (i + 1) * 128)
                MM(psa[:, cs_], Kf[p][:, cs_], Qf[p][:, cs_], True, True, ["Kf%d" % p, "Qf%d" % p], [pna])
            TSC(dve, T["m1"], psa[:], BIG, -BIG, ALU.min, ALU.max, [pna], ["m1"])
            TT(pool, v3(T["m1"]), v3(T["m1"]), mfb, ALU.mult, ["m1", "cstt"], ["m1"])
            yield
            psb, pnb = nb()
            for i in range(4):
                cs_ = slice(i * 128, (i + 1) * 128)
                MM(psb[:, cs_], Kb[p][:, cs_], Qb[p][:, cs_], True, True, ["Kb%d" % p, "Qb%d" % p], [pnb])
            TSC(dve, T["m2"], psb[:], BIG, -BIG, ALU.min, ALU.max, [pnb], ["m2"])
            TT(pool, v3(T["m2"]), v3(T["m2"]), mbb, ALU.mult, ["m2", "cstt"], ["m2"])
            yield
            TT(pool, T["Ac"], T["m1"], T["m2"], ALU.add, ["m1", "m2"], ["Ac"])
            yield

        def g_outA(l, h, p):
            for i in range(4):
                cs_ = slice(i * 128, (i + 1) * 128)
                MM(PS[6][:, cs_], vA[p][:, i, :], T["Ac"][:, cs_], True, False, ["vA%d" % p, "Ac"], ["ps6"])
                for cc in range(2):
                    c = 2 * i + cc
                    ct = slice(c * CH, (c + 1) * CH)
                    MM(PS[6][:, ct], snf[:, c, :], Qf[p][:, ct], False, False, ["snf", "Qf%d" % p], ["ps6"])
                    MM(PS[6][:, ct], snb[:, c, :], Qb[p][:, ct], False, cc == 1, ["snb", "Qb%d" % p], ["ps6"])
            yield
            A(T["osq"], PS[6][:], AF.Square, ["ps6"], ["osq"])
            fw.op(dve, lambda: nc.vector.tensor_copy(out=T["o"], in_=PS[6][:]), ["ps6", "osq"], ["o"])
            yield
            MM(PS[4][:], ones[:], T["osq"], True, True, ["ones", "osq"], ["ps4"])
            A(T["r1"], PS[4][:], AF.Ln, ["ps4"], ["r1"], scale=1.0 / 128.0, bias=EPS_A)
            yield
            A(T["r2"], T["r1"], AF.Exp, ["r1"], ["r2"], scale=-0.5)
            yield
            STT(T["o"], T["o"], anw[:, l:l + 1], T["r2"], ALU.mult, ALU.mult, ["o", "anw", "r2"], ["o"])
            yield
            TT(dve, Y[:, h, :], T["o"], sgA[p][:], ALU.mult, ["o", "sgA%d" % p], ["Y%d" % h])
            yield

        def g_scR(l, h, p):
            v3 = lambda t: t.rearrange("p (i k) -> p i k", k=128)
            psr, pnr = nb()
            for i in range(4):
                cs_ = slice(i * 128, (i + 1) * 128)
                MM(psr[:, cs_], kr[p][:, cs_], qr[p][:, cs_], True, True, ["kr%d" % p, "qr%d" % p], [pnr])
            TT(dve, v3(T["AcR"]), psr[:].rearrange("p (i k) -> p i k", k=128),
               DMt[:, h, :].unsqueeze(1).to_broadcast([128, 4, 128]), ALU.mult, [pnr, "DMt"], ["AcR"])
            yield

        def g_outR(l, h, p):
            for i in range(4):
                cs_ = slice(i * 128, (i + 1) * 128)
                MM(PS[5][:, cs_], vR[p][:, i, :], T["AcR"][:, cs_], True, False, ["vR%d" % p, "AcR"], ["ps5"])
                for cc in range(2):
                    c = 2 * i + cc
                    ct = slice(c * CH, (c + 1) * CH)
                    MM(PS[5][:, ct], srf[0:64, c, :], Qrf[p][:, ct], False, False, ["srf", "Qrf%d" % p], ["ps5"])
                    MM(PS[5][:, ct], srb[0:64, c, :], Qrb[p][:, ct], False, cc == 1, ["srb", "Qrb%d" % p], ["ps5"])
            yield
            A(T["osqR"], PS[5][:], AF.Square, ["ps5"], ["osqR"])
            fw.op(dve, lambda: nc.vector.tensor_copy(out=T["oR"], in_=PS[5][:]), ["ps5", "osqR"], ["oR"])
            yield
            MM(PS[3][:], ones[:], T["osqR"], True, True, ["ones", "osqR"], ["ps3"])
            psm, pnm = nb()
            MM(psm[:], ones[:], T["oR"], True, True, ["ones", "oR"], [pnm])
            A(T["m1R"], psm[:], AF.Copy, [pnm], ["m1R"], scale=1.0 / 128.0)
            TT(pool, T["r2R"], T["m1R"], T["m1R"], ALU.mult, ["m1R"], ["r2R"])
            yield
            STT(T["r1R"], PS[3][:], 1.0 / 128.0, T["r2R"], ALU.mult, ALU.subtract, ["ps3", "r2R"], ["r1R"])
            A(T["r1R"], T["r1R"], AF.Ln, ["r1R"], ["r1R"], bias=EPS_R)
            yield
            A(T["r2R"], T["r1R"], AF.Exp, ["r1R"], ["r2R"], scale=-0.5)
            TT(dve, T["oR"], T["oR"], T["m1R"], ALU.subtract, ["oR", "m1R"], ["oR"])
            yield
            TT(dve, T["oR"], T["oR"], T["r2R"], ALU.mult, ["oR", "r2R"], ["oR"])
            yield
            TT(dve, Y[:, 8 + h, :], T["oR"], sgR[p][:], ALU.mult, ["oR", "sgR%d" % p], ["Y%d" % (8 + h)])
            yield

        def late2(l, s, h, p):
            cf = l * 16 + h
            cbk = l * 16 + 8 + h
            Df, Ef, Db, Eb = scl["Df"][p], scl["Ef"][p], scl["Db"][p], scl["Eb"][p]
            gf = g64[0:64, cf:cf + 1]
            gb = g64[0:64, cbk:cbk + 1]

            def preAb():
                fw.dma(qs, SbA[:, h, :], SBId[s, h, 0], r=["SBId%d_%d" % (s, h)], w=["SbA%d" % h])

            def preRb():
                fw.dma(qs, SbR[:, h, :], SBId[s, h, 1, 0:64], r=["SBId%d_%d" % (s, h)], w=["SbR%d" % h])
            rec = [
                g_recur(SfA[:, h, :], "SfA%d" % h, ASC, KHf[p], "KHf%d" % p, vA[p], "vA%d" % p, lambda c: Df[:, c:c + 1], "Df%d" % p,
                        snf, "snf", lambda c: Ef[:, c:c + 1], "Ef%d" % p, 128, chn[:, s:s + 1], 3, ci=0),
                g_recur(SbA[:, h, :], "SbA%d" % h, DESC, KHb[p], "KHb%d" % p, vA[p], "vA%d" % p, lambda c: Db[:, c:c + 1], "Db%d" % p,
                        snb, "snb", lambda c: Eb[:, c:c + 1], "Eb%d" % p, 128, None, 3, pre=preAb, ci=1),
                g_recur(SfR[:, h, :], "SfR%d" % h, ASC, KRf[p], "KRf%d" % p, vR[p], "vR%d" % p, lambda c: gf, "g64",
                        srf, "srf", None, None, 64, chn[0:64, s:s + 1], 5, ci=2),
                g_recur(SbR[:, h, :], "SbR%d" % h, DESC, KRb[p], "KRb%d" % p, vR[p], "vR%d" % p, lambda c: gb, "g64",
                        srb, "srb", None, None, 64, None, 5, pre=preRb, ci=3)]
            return [rec + [g_scA(l, h, p), g_scR(l, h, p)], [g_outA(l, h, p), g_outR(l, h, p)]]

        def tail(l, s):
            last = (l == depth - 1)
            yn = ["Y%d" % i for i in range(16)]
            for hf in range(2):
                fw.dma(qs, WO[:, hf], WOd[l, hf], r=["WOd"], w=["WO"])
            fw.dma(qs, g1r, g1pd[l, s].partition_broadcast(128), r=["g1pd"], w=["g1r"])
            fw.dma(qs, lngr, ln_g[l].partition_broadcast(128), r=["ln_g"], w=["lngr"])
            fw.dma(qs, lnbr, ln_b[l].partition_broadcast(128), r=["ln_b"], w=["lnbr"])
            fw.dma(qs, TWb[0], TWd[l, 0], r=["TWd"], w=["TW0"])
            for cb in range(8):
                TW, twn = TWb[cb % 2], "TW%d" % (cb % 2)
                if cb < 7:
                    fw.dma(qs, TWb[(cb + 1) % 2], TWd[l, cb + 1], r=["TWd"], w=["TW%d" % ((cb + 1) % 2)])
                for (j, ps, pn, kofs) in ((0, PS[0], "ps0", 0), (1, PS[1], "ps1", 8)):
                    for kc in range(8):
                        MM(ps[:], TW[:, kc, j, :], Y[:, kofs + kc, :], kc == 0, kc == 7, [twn] + yn, [pn])
                for (j, ps, pn) in ((2, PS[4], "ps4"), (3, PS[5], "ps5")):
                    for kc in range(8):
                        MM(ps[:], TW[:, kc, j, :], uT[:, kc, :], kc == 0, kc == 7, [twn, "uT"], [pn])
                A(T["t_r1"], PS[4][:], AF.Tanh, ["ps4", "hbm"], ["t_r1"], scale=0.5, bias=hbm[:, l, cb:cb + 1])
                A(T["t_r2"], PS[5][:], AF.Tanh, ["ps5", "hbm"], ["t_r2"], scale=0.5, bias=hbm[:, l, 8 + cb:8 + cb + 1])
                STT(T["t_x1"], T["t_r1"], 1.0, PS[0][:], ALU.add, ALU.mult, ["t_r1", "ps0"], ["t_x1"])
                STT(T["t_x2"], T["t_r2"], 1.0, PS[1][:], ALU.add, ALU.mult, ["t_r2", "ps1"], ["t_x2"])
                TT(pool, MT[:, cb, :], T["t_x1"], T["t_x2"], ALU.add, ["t_x1", "t_x2"], ["MT"])
            xsrc = x_tok if l == 0 else x1d
            for i in range(4):
                t0 = s * TS + i * 128
                fw.dma(qs, hh_, xsrc[t0:t0 + 128, :], r=["x1d" if l else "x_tok"], w=["hh"])
                for hf in range(2):
                    ps, pn = (PS[6], "ps6") if hf == 0 else (PS[2], "ps2")
                    for kc in range(8):
                        MM(ps[:], MT[:, kc, i * 128:(i + 1) * 128], WO[:, hf, kc, :], kc == 0, kc == 7, ["MT", "WO"], [pn])
                    hs = slice(hf * 512, (hf + 1) * 512)
                    TT(dve, T["t_o"], ps[:], g1r[:, hs], ALU.mult, [pn, "g1r"], ["t_o"])
                    STT(hh_[:, hs], hh_[:, hs], ALPHA, T["t_o"], ALU.mult, ALU.add, ["hh", "t_o"], ["hh"])
                A(zz, hh_, AF.Identity, ["hh"], ["zz", "st4a"], accum_out=st4[:, 0:1])
                A(zz, hh_, AF.Square, ["hh"], ["zz", "st4b"], accum_out=st4[:, 1:2])
                TSC(dve, st4[:, 2:3], st4[:, 0:1], 1.0 / D, None, ALU.mult, None, ["st4a"], ["st4c"])
                TT(dve, st4[:, 3:4], st4[:, 2:3], st4[:, 2:3], ALU.mult, ["st4c"], ["st4d"])
                STT(st4[:, 4:5], st4[:, 1:2], 1.0 / D, st4[:, 3:4], ALU.mult, ALU.subtract, ["st4b", "st4d"], ["st4e"])
                A(st4[:, 4:5], st4[:, 4:5], AF.Ln, ["st4e"], ["st4e"], bias=1e-5)
                A(st4[:, 5:6], st4[:, 4:5], AF.Exp, ["st4e"], ["st4f"], scale=-0.5)
                STT(st4[:, 6:7], st4[:, 2:3], -1.0, st4[:, 5:6], ALU.mult, ALU.mult, ["st4c", "st4f"], ["st4g"])
                A(zz, hh_, AF.Identity, ["hh", "st4f", "st4g"], ["zz"], scale=st4[:, 5:6], bias=st4[:, 6:7])
                TT(dve, zz, zz, lngr, ALU.mult, ["zz", "lngr"], ["zz"])
                TT(pool, zz, zz, lnbr, ALU.add, ["zz", "lnbr"], ["zz"])
                if last:
                    fw.dma(qs, y_out[t0:t0 + 128, :], zz, r=["zz"], w=["y"])
                else:
                    fw.dma(qs, x1d[t0:t0 + 128, :], zz, r=["zz"], w=["x1d"])
                    for g in range(2):
                        ps, pn = (PS[4], "ps4") if g == 0 else (PS[5], "ps5")
                        for kk_ in range(4):
                            kc = g * 4 + kk_
                            TR(ps[:, kk_ * 128:(kk_ + 1) * 128], zz[:, kc * 128:(kc + 1) * 128], IDF, ["zz", "cstt"], [pn])
                        for kk_ in range(4):
                            kc = g * 4 + kk_
                            A(u1t[:, kc, :], ps[:, kk_ * 128:(kk_ + 1) * 128], AF.Identity, [pn, "adaT"], ["u1t"],
                              scale=adaT[:, l + 1, 8 + kc, s:s + 1], bias=adaT[:, l + 1, kc, s:s + 1])
                    fw.dma(qs, uTd[l + 1][:, t0:t0 + 128].rearrange("(kc p) t -> p kc t", p=128), u1t[:], r=["u1t"], w=["uTd%d" % (l + 1)])

        for l in range(depth):
            layer_consts(l)
            for t_, nm in ((SfA, "SfA"), (SbA, "SbA"), (SfR, "SfR"), (SbR, "SbR")):
                fw.op(pool, lambda: nc.gpsimd.memset(t_[:], 0.0), [], [nm + str(h) for h in range(8)])
            units = [(s, h) for s in range(NS - 1, -1, -1) for h in (0, 2, 4, 6)]
            prev = None

            def ld1(wp_, h_):
                for j in range(2):
                    fw.dma(qs, WH[wp_][:, :, j * S1C:(j + 1) * S1C], WHd[l, h_ + j][:, :, 0:S1C], r=["WHd"], w=["WH%d" % wp_])
            for ui, (s, h) in enumerate(units):
                wp = ui % 2
                if ui == 0:
                    ld1(wp, h)
                if h == 0:
                    load_uT(l, s)
                    load_rope(s)
                if ui + 1 < len(units):
                    ld1(1 - wp, units[ui + 1][1])
                q0 = 2 * (ui % 2)
                sched([early1(l, s, h, wp, q0), prev])
                prev = late1(l, s, h, q0)
            sched([prev])
            stage(5)
            units = [(s, h) for s in range(NS) for h in range(8)]
            prev = None
            for ui, (s, h) in enumerate(units):
                p = wp = ui % 2
                if ui == 0:
                    fw.dma(qs, WH[wp][:], WHd[l, h], r=["WHd"], w=["WH%d" % wp])
                if h == 0:
                    load_uT(l, s)
                    load_rope(s)
                if ui + 1 < len(units):
                    hn = units[ui + 1][1]
                    fw.dma(qs, WH[1 - wp][:], WHd[l, hn], r=["WHd"], w=["WH%d" % (1 - wp)])
                sched([early2(l, s, h, p, wp), prev])
                prev = late2(l, s, h, p)
                if h == 7:
                    sched([prev])
                    prev = None
                    stage(6)
                    tail(l, s)
        fw.finish(["y"])
        build.stats = (fw.n_inst, fw.n_wait)
    return nc


def _consts():
    s = np.arange(128)[:, None]
    t = np.arange(128)[None, :]
    same = (s // 64) == (t // 64)
    MF = (same & (s <= t)).astype(np.float32)
    MB = (same & (s >= t)).astype(np.float32)
    DF = np.where(same & (s <= t), t - s, 0).astype(np.float32)
    DB = np.where(same & (s >= t), s - t, 0).astype(np.float32)
    TP = np.zeros((128, 128), np.float32)
    TP[:, 0:64] = np.arange(64)[None, :] + 1
    TP[:, 64:128] = 64 - np.arange(64)[None, :]
    CF = np.zeros((128, 2), np.float32)
    CF[:, 0] = 63 - (np.arange(128) % 64)
    CF[:, 1] = np.arange(128) % 64
    return np.concatenate([MF, MB, DF, DB, TP, CF, np.eye(128, dtype=np.float32)], axis=1)


def _rope_tables(pos0):
    inv = (10000.0 ** (-np.arange(0, 64, 2, dtype=np.float32) / 64)).astype(np.float32)
    pos = (pos0 + np.arange(TS)).astype(np.float32)
    ang = (pos[None, :] * inv[:, None]).astype(np.float32)
    c = np.cos(ang).astype(np.float32)
    s_ = np.sin(ang).astype(np.float32)
    return np.concatenate([c, c], 0), np.concatenate([-s_, s_], 0)


def make_core_inputs(seqs, NS, weights):
    NT = NS * TS
    x_tok = np.zeros((NT, D), np.float32)
    cT = np.zeros((D, NS), np.float32)
    chain = np.zeros((1, NS + 1), np.float32)
    rC = np.zeros((NS, 64, TS), np.float32)
    rS = np.zeros((NS, 64, TS), np.float32)
    s = 0
    for (x, c) in seqs:
        n = x.shape[0] // TS
        x_tok[s * TS:(s + n) * TS] = x
        for j in range(n):
            cT[:, s + j] = c
            if j > 0:
                chain[0, s + j] = 1.0
            rC[s + j], rS[s + j] = _rope_tables(j * TS)
        s += n
    for j in range(s, NS):
        rC[j], rS[j] = _rope_tables(0)
    m = dict(weights)
    m.update(x_tok=x_tok, xT=np.ascontiguousarray(x_tok.T), cT=cT, chain=chain, ropeC=rC, ropeS=rS, cst=_consts())
    return m


def kernel(x_prompt, x_sample, c_prompt, c_sample, w_ada, b_ada, w_in, hgrn_lb, a_norm_w, ret_decay,
           w_pa, w_pb, w_mg, b_mg, w_out, ln_g, ln_b):
    f = lambda a: np.ascontiguousarray(np.asarray(a, dtype=np.float32))
    weights = dict(w_ada=f(w_ada), b_ada=f(b_ada), w_in=f(w_in), hgrn_lb=f(hgrn_lb), a_norm_w=f(a_norm_w),
                   ret_decay=f(ret_decay).reshape(1, 32), w_pa=f(w_pa), w_pb=f(w_pb), w_mg=f(w_mg), b_mg=f(b_mg),
                   w_out=f(w_out), ln_g=f(ln_g), ln_b=f(ln_b))
    x_prompt, x_sample, c_prompt, c_sample = f(x_prompt), f(x_sample), f(c_prompt), f(c_sample)
    NS = 24
    assign = [[("s", 0), ("p", 0)], [("s", 1), ("p", 1)], [("p", 2), ("p", 3), ("p", 4)], [("p", 5), ("p", 6), ("p", 7)],
              [("p", 8), ("p", 9)], [("p", 10), ("p", 11)], [("p", 12), ("p", 13)], [("p", 14), ("p", 15)]]
    in_maps = []
    for core in assign:
        seqs = [((x_sample[i], c_sample[i]) if k == "s" else (x_prompt[i], c_prompt[i])) for (k, i) in core]
        in_maps.append(make_core_inputs(seqs, NS, weights))
    nc = build(NS)
    res = run_bass_kernel_spmd(nc, in_maps, core_ids=list(range(8)))
    y_prompt = np.zeros_like(x_prompt)
    y_sample = np.zeros_like(x_sample)
    for ci, core in enumerate(assign):
        yc = res.results[ci]["y"]
        s = 0
        for (k, i) in core:
            if k == "s":
                n = x_sample.shape[1]
                y_sample[i] = yc[s:s + n]
            else:
                n = x_prompt.shape[1]
                y_prompt[i] = yc[s:s + n]
            s += n
    return (y_prompt, y_sample)
```
